# Optimizing a Trainium2 kernel written in Bass

```python
import math
import jax, jax.numpy as jnp
from jax import lax
import numpy as np

D_MODEL = 1024
BATCH = 4
SEQ = 8192
DEPTH = 2

N_A_LAYERS = DEPTH // 2
N_B_LAYERS = DEPTH - N_A_LAYERS
D_FF = 2816
FFN_HALF = 0.5
SSM_EXPAND = 2
SSM_D_INNER = SSM_EXPAND * D_MODEL
SSM_HEAD_DIM = 64
SSM_HEADS = SSM_D_INNER // SSM_HEAD_DIM
SSM_GROUPS = 4
SSM_STATE = 128
SSM_CONV = 4
SSM_CHUNK = 256
SSM_CONV_DIM = SSM_D_INNER + 2 * SSM_GROUPS * SSM_STATE
SSM_IN_DIM = SSM_D_INNER + SSM_CONV_DIM + SSM_HEADS
ATT_HEAD_DIM = 64
ATT_HEADS = D_MODEL // ATT_HEAD_DIM
ATT_KV_HEADS = ATT_HEADS // 8
ATT_GROUP = ATT_HEADS // ATT_KV_HEADS
ATT_WINDOW = 128
ATT_BLOCK = ATT_WINDOW
REL_BUCKETS = 32
REL_MAX_DIST = ATT_WINDOW
EPS = 1e-6

kernel_name = 'yoco_mamba2_swa_sink_macaron'


def rmsnorm(x, g):
    xf = x.astype(jnp.float32)
    y = xf * lax.rsqrt(jnp.mean(xf * xf, axis=-1, keepdims=True) + EPS)
    return (y * g.astype(jnp.float32)).astype(x.dtype)


def swiglu_half_step(h, g, w1, w3, w2):
    u = rmsnorm(h, g)
    return h + FFN_HALF * ((jax.nn.silu(u @ w1) * (u @ w3)) @ w2)


def causal_depthwise_conv(x, w, b):
    c = x.shape[-1]
    y = lax.conv_general_dilated(
        x, w.astype(x.dtype)[:, None, :], window_strides=(1,),
        padding=[(SSM_CONV - 1, 0)], dimension_numbers=('NWC', 'WIO', 'NWC'),
        feature_group_count=c)
    return y + b.astype(x.dtype)


def ssd_chunked_scan(x, dt, a, b_in, c_in):
    bsz, t, h, p = x.shape
    g, n = b_in.shape[2], b_in.shape[3]
    r = h // g
    L = SSM_CHUNK
    nc = -(-t // L)
    pad = nc * L - t
    f32 = jnp.float32

    def chunks(z):
        z = z.astype(f32)
        z = jnp.pad(z, [(0, 0), (0, pad)] + [(0, 0)] * (z.ndim - 2))
        z = z.reshape((bsz, nc, L) + z.shape[2:])
        return jnp.moveaxis(z, 1, 0)

    xc = chunks(x.reshape(bsz, t, g, r, p))
    dtc = chunks(dt.reshape(bsz, t, g, r))
    bc = chunks(b_in)
    cc = chunks(c_in)
    a_gr = a.astype(f32).reshape(g, r)
    causal = jnp.tril(jnp.ones((L, L), bool))[:, :, None, None]

    def step(state, inp):
        xk, dtk, bk, ck = inp
        acum = jnp.cumsum(dtk * a_gr, axis=1)
        seg = acum[:, :, None] - acum[:, None, :]
        decay = jnp.exp(jnp.where(causal, seg, -jnp.inf))
        cb = jnp.einsum('blgn,bsgn->blsg', ck, bk)
        scores = cb[..., None] * decay
        y_diag = jnp.einsum('blsgr,bsgr,bsgrp->blgrp', scores, dtk, xk)
        y_off = jnp.einsum('blgn,bgrpn,blgr->blgrp', ck, state, jnp.exp(acum))
        w_end = jnp.exp(acum[:, -1:] - acum) * dtk
        state = (state * jnp.exp(acum[:, -1])[..., None, None]
                 + jnp.einsum('bsgn,bsgr,bsgrp->bgrpn', bk, w_end, xk))
        return state, y_diag + y_off

    state0 = jnp.zeros((bsz, g, r, p, n), f32)
    _, y = lax.scan(step, state0, (xc, dtc, bc, cc))
    y = jnp.moveaxis(y, 0, 1).reshape(bsz, nc * L, h, p)[:, :t]
    return y.astype(x.dtype)


def mamba2_mixer(u, w_in, conv_w, conv_b, dt_bias, a_log, d_skip, gate_norm, w_out):
    bsz, t, _ = u.shape
    zxbcdt = u @ w_in
    z, xbc, dt = jnp.split(zxbcdt, [SSM_D_INNER, SSM_D_INNER + SSM_CONV_DIM], axis=-1)
    xbc = jax.nn.silu(causal_depthwise_conv(xbc, conv_w, conv_b))
    xs, b_in, c_in = jnp.split(xbc, [SSM_D_INNER, SSM_D_INNER + SSM_GROUPS * SSM_STATE], axis=-1)
    xs = xs.reshape(bsz, t, SSM_HEADS, SSM_HEAD_DIM)
    b_in = b_in.reshape(bsz, t, SSM_GROUPS, SSM_STATE)
    c_in = c_in.reshape(bsz, t, SSM_GROUPS, SSM_STATE)
    dt = jax.nn.softplus((dt + dt_bias).astype(jnp.float32))
    a = -jnp.exp(a_log.astype(jnp.float32))
    y = ssd_chunked_scan(xs, dt, a, b_in, c_in) + d_skip[:, None].astype(xs.dtype) * xs
    y = y.reshape(bsz, t, SSM_D_INNER) * jax.nn.silu(z)
    y = rmsnorm(y.reshape(bsz, t, SSM_GROUPS, SSM_D_INNER // SSM_GROUPS),
                gate_norm.reshape(SSM_GROUPS, SSM_D_INNER // SSM_GROUPS))
    return y.reshape(bsz, t, SSM_D_INNER) @ w_out


def shared_kv(h, kv_norm, w_kv, k_norm):
    bsz, t, _ = h.shape
    kv = rmsnorm(h, kv_norm) @ w_kv
    k, v = jnp.split(kv, 2, axis=-1)
    k = rmsnorm(k.reshape(bsz, t, ATT_KV_HEADS, ATT_HEAD_DIM), k_norm)
    v = v.reshape(bsz, t, ATT_KV_HEADS, ATT_HEAD_DIM)
    return k, v


def t5_bucket(dist):
    n = jnp.maximum(dist, 0)
    max_exact = REL_BUCKETS // 2
    nf = jnp.maximum(n, 1).astype(jnp.float32)
    large = max_exact + (jnp.log(nf / max_exact) / math.log(REL_MAX_DIST / max_exact)
                         * (REL_BUCKETS - max_exact)).astype(jnp.int32)
    large = jnp.minimum(large, REL_BUCKETS - 1)
    return jnp.where(n < max_exact, n, large)


def sliding_window_attention(u, k, v, w_q, q_norm, sinks, rel_bias, w_o):
    bsz, t, _ = u.shape
    blk = ATT_BLOCK
    nb = t // blk
    q = rmsnorm((u @ w_q).reshape(bsz, t, ATT_KV_HEADS, ATT_GROUP, ATT_HEAD_DIM), q_norm)
    qb = jnp.moveaxis(q.reshape(bsz, nb, blk, ATT_KV_HEADS, ATT_GROUP, ATT_HEAD_DIM), 1, 0)

    def band(z):
        prev = jnp.pad(z, [(0, 0), (blk, 0), (0, 0), (0, 0)])[:, :t]
        zz = jnp.concatenate([prev.reshape(bsz, nb, blk, ATT_KV_HEADS, ATT_HEAD_DIM),
                              z.reshape(bsz, nb, blk, ATT_KV_HEADS, ATT_HEAD_DIM)], axis=2)
        return jnp.moveaxis(zz, 1, 0)

    kb, vb = band(k), band(v)
    qi = jnp.arange(blk)[:, None] + blk
    kj = jnp.arange(2 * blk)[None, :]
    dist = qi - kj
    in_window = (dist >= 0) & (dist < ATT_WINDOW)
    bias = rel_bias[t5_bucket(dist)]
    bias = jnp.transpose(bias.reshape(blk, 2 * blk, ATT_KV_HEADS, ATT_GROUP),
                         (2, 3, 0, 1)).astype(jnp.float32)
    sink = sinks.reshape(ATT_KV_HEADS, ATT_GROUP)[None, :, :, None, None].astype(jnp.float32)
    scale = ATT_HEAD_DIM ** -0.5

    def block(args):
        qk, kk, vk, bi = args
        s = jnp.einsum('bqkrd,bskd->bkrqs', qk, kk).astype(jnp.float32) * scale + bias
        valid = in_window & ((bi > 0) | (kj >= blk))
        s = jnp.where(valid, s, -jnp.inf)
        m = jnp.maximum(jnp.max(s, axis=-1, keepdims=True), sink)
        p = jnp.exp(s - m)
        denom = jnp.sum(p, axis=-1, keepdims=True) + jnp.exp(sink - m)
        return jnp.einsum('bkrqs,bskd->bqkrd', (p / denom).astype(vk.dtype), vk)

    o = lax.map(block, (qb, kb, vb, jnp.arange(nb)))
    o = jnp.moveaxis(o, 0, 1).reshape(bsz, t, ATT_HEADS * ATT_HEAD_DIM)
    return o @ w_o


def setup_inputs(seed: int = 0) -> dict:
    key = jax.random.key(seed)
    ks = jax.random.split(key, 24)
    nrm = jax.random.normal
    f32 = jnp.float32
    x = nrm(ks[0], (BATCH, SEQ, D_MODEL), f32)
    ffn_norm = 1.0 + 0.05 * nrm(ks[1], (DEPTH, 2, D_MODEL), f32)
    ffn_w1 = nrm(ks[2], (DEPTH, 2, D_MODEL, D_FF), f32) * D_MODEL ** -0.5
    ffn_w3 = nrm(ks[3], (DEPTH, 2, D_MODEL, D_FF), f32) * D_MODEL ** -0.5
    ffn_w2 = nrm(ks[4], (DEPTH, 2, D_FF, D_MODEL), f32) * D_FF ** -0.5
    ssm_norm = 1.0 + 0.05 * nrm(ks[5], (N_A_LAYERS, D_MODEL), f32)
    ssm_w_in = nrm(ks[6], (N_A_LAYERS, D_MODEL, SSM_IN_DIM), f32) * D_MODEL ** -0.5
    ssm_conv_w = nrm(ks[7], (N_A_LAYERS, SSM_CONV, SSM_CONV_DIM), f32) * SSM_CONV ** -0.5
    ssm_conv_b = 0.01 * nrm(ks[8], (N_A_LAYERS, SSM_CONV_DIM), f32)
    dt0 = jnp.exp(jax.random.uniform(ks[9], (N_A_LAYERS, SSM_HEADS), f32,
                                     math.log(1e-3), math.log(1e-1)))
    ssm_dt_bias = dt0 + jnp.log(-jnp.expm1(-dt0))
    ssm_a_log = jnp.log(jax.random.uniform(ks[10], (N_A_LAYERS, SSM_HEADS), f32, 1.0, 16.0))
    ssm_d = 1.0 + 0.1 * nrm(ks[11], (N_A_LAYERS, SSM_HEADS), f32)
    ssm_gate_norm = 1.0 + 0.05 * nrm(ks[12], (N_A_LAYERS, SSM_D_INNER), f32)
    ssm_w_out = nrm(ks[13], (N_A_LAYERS, SSM_D_INNER, D_MODEL), f32) * SSM_D_INNER ** -0.5
    kv_norm = 1.0 + 0.05 * nrm(ks[14], (D_MODEL,), f32)
    w_kv = nrm(ks[15], (D_MODEL, 2 * ATT_KV_HEADS * ATT_HEAD_DIM), f32) * D_MODEL ** -0.5
    k_norm = 1.0 + 0.05 * nrm(ks[16], (ATT_HEAD_DIM,), f32)
    attn_norm = 1.0 + 0.05 * nrm(ks[17], (N_B_LAYERS, D_MODEL), f32)
    w_q = nrm(ks[18], (N_B_LAYERS, D_MODEL, ATT_HEADS * ATT_HEAD_DIM), f32) * D_MODEL ** -0.5
    q_norm = 1.0 + 0.05 * nrm(ks[19], (N_B_LAYERS, ATT_HEAD_DIM), f32)
    sinks = 0.5 * nrm(ks[20], (N_B_LAYERS, ATT_HEADS), f32)
    w_o = nrm(ks[21], (N_B_LAYERS, ATT_HEADS * ATT_HEAD_DIM, D_MODEL), f32) * (ATT_HEADS * ATT_HEAD_DIM) ** -0.5
    rel_bias = 0.5 * nrm(ks[22], (REL_BUCKETS, ATT_HEADS), f32)
    return {'x': x, 'ffn_norm': ffn_norm, 'ffn_w1': ffn_w1, 'ffn_w3': ffn_w3, 'ffn_w2': ffn_w2,
            'ssm_norm': ssm_norm, 'ssm_w_in': ssm_w_in, 'ssm_conv_w': ssm_conv_w,
            'ssm_conv_b': ssm_conv_b, 'ssm_dt_bias': ssm_dt_bias, 'ssm_a_log': ssm_a_log,
            'ssm_d': ssm_d, 'ssm_gate_norm': ssm_gate_norm, 'ssm_w_out': ssm_w_out,
            'kv_norm': kv_norm, 'w_kv': w_kv, 'k_norm': k_norm,
            'attn_norm': attn_norm, 'w_q': w_q, 'q_norm': q_norm, 'sinks': sinks, 'w_o': w_o,
            'rel_bias': rel_bias}


def reference(x, ffn_norm, ffn_w1, ffn_w3, ffn_w2,
              ssm_norm, ssm_w_in, ssm_conv_w, ssm_conv_b, ssm_dt_bias, ssm_a_log,
              ssm_d, ssm_gate_norm, ssm_w_out,
              kv_norm, w_kv, k_norm,
              attn_norm, w_q, q_norm, sinks, w_o,
              rel_bias):
    h = x
    k_shared, v_shared = None, None
    for layer in range(DEPTH):
        h = swiglu_half_step(h, ffn_norm[layer, 0], ffn_w1[layer, 0], ffn_w3[layer, 0], ffn_w2[layer, 0])
        if layer < N_A_LAYERS:
            i = layer
            h = h + mamba2_mixer(rmsnorm(h, ssm_norm[i]), ssm_w_in[i], ssm_conv_w[i], ssm_conv_b[i],
                                 ssm_dt_bias[i], ssm_a_log[i], ssm_d[i], ssm_gate_norm[i], ssm_w_out[i])
        else:
            j = layer - N_A_LAYERS
            h = h + sliding_window_attention(rmsnorm(h, attn_norm[j]), k_shared, v_shared,
                                             w_q[j], q_norm[j], sinks[j], rel_bias, w_o[j])
        h = swiglu_half_step(h, ffn_norm[layer, 1], ffn_w1[layer, 1], ffn_w3[layer, 1], ffn_w2[layer, 1])
        if layer == N_A_LAYERS - 1:
            k_shared, v_shared = shared_kv(h, kv_norm, w_kv, k_norm)
    return h
```

```python
import contextlib
import math
import ml_dtypes
from concourse.bass_utils import run_bass_kernel_spmd
import numpy as np
import concourse.bass as bass
import concourse.mybir as mybir

F32 = mybir.dt.float32
BF16 = mybir.dt.bfloat16
AF = mybir.ActivationFunctionType
ALU = mybir.AluOpType


_UNIQ = [0]


def uniq(n):
    _UNIQ[0] += 1
    return f"{n}_{_UNIQ[0]}"


class Buf:
    __slots__ = ("name", "w", "r")

    def __init__(self, name):
        self.name = name
        self.w = {}
        self.r = {}


class DSem:
    def __init__(self, sem, key):
        self.sem = sem
        self.key = key
        self.cnt = 0


class KB:
    SAME_ENG_SYNC = True
    NOSYNC_SAME = ('pe',)

    def __init__(self, nc, stack):
        self.nc = nc
        self.stack = stack
        self.root = stack
        self.eng = {"pe": nc.tensor, "act": nc.scalar, "dve": nc.vector,
                    "pool": nc.gpsimd, "sp": nc.sync}
        self.esem = {}
        self.ecnt = {}
        self.seen = {}
        for e in self.eng:
            self.esem[e] = stack.enter_context(nc.semaphore("es_" + e))
            self.ecnt[e] = 0
            self.seen[e] = {}
        self.dsems = []
        self.nwait = 0

    def sb(self, name, shape, dt):
        return self.stack.enter_context(self.nc.sbuf_tensor(name, list(shape), dt))

    def ps(self, name, shape, dt):
        return self.stack.enter_context(self.nc.psum_tensor(name, list(shape), dt))

    def dsem(self, name):
        s = self.root.enter_context(self.nc.semaphore(name))
        d = DSem(s, name)
        self.dsems.append(d)
        return d

    def _waits(self, e, toks):
        need = {}
        for (key, sem, val, src) in toks:
            if src == e and (e in self.NOSYNC_SAME or not self.SAME_ENG_SYNC):
                continue
            if key not in need or need[key][1] < val:
                need[key] = (sem, val, src)
        for key, (sem, val, src) in need.items():
            if self.seen[e].get(key, 0) >= val:
                continue
            if src is not None and val > self.ecnt[src]:
                raise RuntimeError(f"unresolved lazy token {key}={val} (cnt {self.ecnt[src]}) waited by {e}")
            self.eng[e].wait_ge(sem, val)
            self.nwait += 1
            self.seen[e][key] = val

    @staticmethod
    def _gather(reads, writes):
        toks = []
        for b in reads:
            toks += [(k,) + v for k, v in b.w.items()]
        for b in writes:
            toks += [(k,) + v for k, v in b.w.items()]
            toks += [(k,) + v for k, v in b.r.items()]
        return toks

    @staticmethod
    def _update(tok, key, reads, writes, part=False):
        for b in writes:
            if part:
                b.w[key] = tok
            else:
                b.w = {key: tok}
                b.r = {}
        for b in reads:
            if b in writes:
                continue
            old = b.r.get(key)
            if old is None or old[1] < tok[1]:
                b.r[key] = tok

    def op(self, e, fn, reads=(), writes=(), inc=True):
        self._waits(e, self._gather(reads, writes))
        ins = fn(self.eng[e])
        key = "es_" + e
        if inc:
            self.ecnt[e] += 1
            ins.then_inc(self.esem[e], 1)
            tok = (self.esem[e], self.ecnt[e], e)
        else:
            tok = (self.esem[e], self.ecnt[e] + 1, e)
        self._update(tok, key, reads, writes)
        return ins

    def dma(self, q, out, in_, ds, reads=(), writes=(), part=False, **kw):
        self._waits(q, self._gather(reads, writes))
        ins = self.eng[q].dma_start(out=out, in_=in_, **kw)
        ds.cnt += 16
        ins.then_inc(ds.sem, 16)
        tok = (ds.sem, ds.cnt, None)
        self._update(tok, ds.key, reads, writes, part=part)
        return ins

    def barrier(self):
        for e in self.eng:
            toks = []
            for x in self.eng:
                if x != e and self.ecnt[x] > 0:
                    toks.append(("es_" + x, self.esem[x], self.ecnt[x], x))
            for d in self.dsems:
                if d.cnt > 0:
                    toks.append((d.key, d.sem, d.cnt, None))
            self._waits(e, toks)


class Ring:
    def __init__(self, kb, name, depth, items):
        self.kb = kb
        self.depth = depth
        self.items = items
        self.bufs = [Buf(f"{name}{i}") for i in range(depth)]
        self.sems = [kb.dsem(f"ds_{name}{i}") for i in range(depth)]
        self.issued = 0
        self.taken = 0

    def _issue_upto(self, n):
        while self.issued < min(n, len(self.items)):
            i = self.issued
            s = i % self.depth
            self.items[i](self.kb, s, self.bufs[s], self.sems[s])
            self.issued += 1

    def get(self):
        i = self.taken
        self._issue_upto(i + 1)
        self.taken += 1
        s = i % self.depth
        return s, self.bufs[s]

    def prefetch(self):
        self._issue_upto(self.taken + self.depth - 1)


D = 1024
KD = 8
DFF = 2816
KF = 22
EPS = 1e-6


def ffn_rings(kb, name, tiles_w):
    W13 = [kb.sb(f"{name}w13_{i}", [128, 2, KD, 256], BF16) for i in range(3)]
    W2 = [kb.sb(f"{name}w2_{i}", [128, KF, 256], BF16) for i in range(2)]
    it13, it2 = [], []
    for (w1, w3, w2) in tiles_w:
        w1v = w1.rearrange("(k p) n -> p k n", p=128)
        w3v = w3.rearrange("(k p) n -> p k n", p=128)
        w2v = w2.rearrange("(j p) n -> p j n", p=128)
        for jp in range(KF // 2):
            def f(kb, s, buf, ds, jp=jp, w1v=w1v, w3v=w3v):
                kb.dma("pool", W13[s][:, 0], w1v[:, :, jp * 256:(jp + 1) * 256], ds, writes=[buf])
                kb.dma("pool", W13[s][:, 1], w3v[:, :, jp * 256:(jp + 1) * 256], ds, writes=[buf], part=True)
            it13.append(f)
        for mb in range(4):
            def f2(kb, s, buf, ds, mb=mb, w2v=w2v):
                kb.dma("pool", W2[s][:], w2v[:, :, mb * 256:(mb + 1) * 256], ds, writes=[buf])
            it2.append(f2)
    r13 = Ring(kb, name + "r13", 3, it13)
    r2 = Ring(kb, name + "r2", 2, it2)
    r13.tiles = W13
    r2.tiles = W2
    return r13, r2


class FFNRes:
    def __init__(self, kb, T, name="f"):
        self.T = T
        self.sq = kb.sb(name + "sq", [128, KD, T], BF16)
        self.sqb = [Buf("sq0"), Buf("sq1")]
        self.uT = kb.sb(name + "uT", [128, KD, T], BF16)
        self.ub = [Buf(f"uT{k}") for k in range(KD)]
        self.rstd = kb.sb(name + "rstd", [128, T], F32)
        self.rb = Buf("rstd")
        self.ones = kb.sb(name + "ones", [128, 128], BF16)
        self.onesb = Buf("ones")
        kb.op("pool", lambda e: e.memset(self.ones[:], 1.0), writes=[self.onesb])
        self.epsb = kb.sb(name + "eps", [128, 1], F32)
        self.epsbuf = Buf("eps")
        kb.op("pool", lambda e: e.memset(self.epsb[:], EPS), writes=[self.epsbuf])


def rmsnorm_T(kb, R, hT, hb, gvec, gb, PN, PNb, T):
    for hf in range(2):
        kb.op("act", lambda e, hf=hf: e.activation(out=R.sq[:, hf * 4:hf * 4 + 4, :T], in_=hT[:, hf * 4:hf * 4 + 4, :T], func=AF.Square),
              reads=[hb], writes=[R.sqb[hf]])
    for k in range(KD):
        kb.op("pe", lambda e, k=k: e.matmul(PN[:, :T], R.ones[:], R.sq[:, k, :T], start=(k == 0), stop=(k == KD - 1)),
              reads=[R.sqb[k // 4], R.onesb], writes=[PNb], inc=(k == KD - 1))
    kb.op("act", lambda e: e.activation(out=R.rstd[:, :T], in_=PN[:, :T], func=AF.Sqrt, bias=R.epsb[:], scale=1.0 / D),
          reads=[PNb, R.epsbuf], writes=[R.rb])
    kb.op("dve", lambda e: e.reciprocal(PN[:, :T], R.rstd[:, :T]), reads=[R.rb, PNb], writes=[PNb])
    for k in range(KD):
        kb.op("dve", lambda e, k=k: e.scalar_tensor_tensor(out=R.uT[:, k, :T], in0=hT[:, k, :T], scalar=gvec[:, k:k + 1],
                                                            in1=PN[:, :T], op0=ALU.mult, op1=ALU.mult),
              reads=[hb, gb, PNb], writes=[R.ub[k]])


def ffn(kb, R, hT, hb, gvec, gb, r13, r2, T):
    with contextlib.ExitStack() as st:
        _ffn(kb, st, R, hT, hb, gvec, gb, r13, r2, T)
        kb.barrier()


def _ffn(kb, st, R, hT, hb, gvec, gb, r13, r2, T):
    sb = lambda n, s, d: st.enter_context(kb.nc.sbuf_tensor(uniq(n), list(s), d))
    ps = lambda n, s, d: st.enter_context(kb.nc.psum_tensor(uniq(n), list(s), d))
    PS = {"n": ps("pn", [128, 512], F32), "nb": Buf("pn"),
          "h1": [ps(f"ph1{i}", [128, 512], F32) for i in range(2)], "h1b": [Buf(f"ph1{i}") for i in range(2)],
          "h3": [ps(f"ph3{i}", [128, 512], F32) for i in range(2)], "h3b": [Buf(f"ph3{i}") for i in range(2)],
          "o": [ps(f"po{i}", [128, 512], F32) for i in range(2)], "ob": [Buf(f"po{i}") for i in range(2)]}
    gT = sb("f_gT", [128, KF, T], BF16)
    gbufs = [Buf(f"g{j}") for j in range(KF)]
    s1 = [sb(f"f_s1_{i}", [128, T], F32) for i in range(2)]
    s1b = [Buf(f"s1_{i}") for i in range(2)]
    rmsnorm_T(kb, R, hT, hb, gvec, gb, PS["n"], PS["nb"], T)
    for jp in range(KF // 2):
        s, wb = r13.get()
        W = r13.tiles[s]
        for jj in range(2):
            j = jp * 2 + jj
            pi = j % 2
            P1, P1b = PS["h1"][pi], PS["h1b"][pi]
            P3, P3b = PS["h3"][pi], PS["h3b"][pi]
            for k in range(KD):
                kb.op("pe", lambda e, k=k: e.matmul(P1[:, :T], W[:, 0, k, jj * 128:(jj + 1) * 128], R.uT[:, k, :T],
                                                     start=(k == 0), stop=(k == KD - 1)),
                      reads=[wb, R.ub[k]], writes=[P1b], inc=(k == KD - 1))
            for k in range(KD):
                kb.op("pe", lambda e, k=k: e.matmul(P3[:, :T], W[:, 1, k, jj * 128:(jj + 1) * 128], R.uT[:, k, :T],
                                                     start=(k == 0), stop=(k == KD - 1)),
                      reads=[wb, R.ub[k]], writes=[P3b], inc=(k == KD - 1))
            kb.op("act", lambda e: e.activation(out=s1[pi][:, :T], in_=P1[:, :T], func=AF.Silu),
                  reads=[P1b], writes=[s1b[pi]])
            kb.op("dve", lambda e: e.tensor_tensor(out=gT[:, j, :T], in0=P3[:, :T], in1=s1[pi][:, :T], op=ALU.mult),
                  reads=[P3b, s1b[pi]], writes=[gbufs[j]])
        r13.prefetch()
    for m in range(KD):
        s, wb = r2.get()
        W = r2.tiles[s]
        pi = m % 2
        PO, POb = PS["o"][pi], PS["ob"][pi]
        for j in range(KF):
            kb.op("pe", lambda e, j=j: e.matmul(PO[:, :T], W[:, j, :], gT[:, j, :T],
                                                 start=(j == 0), stop=(j == KF - 1)),
                  reads=[wb, gbufs[j]], writes=[POb], inc=(j == KF - 1))
        kb.op("dve", lambda e, m=m: e.scalar_tensor_tensor(out=hT[:, m, :T], in0=PO[:, :T], scalar=0.5, in1=hT[:, m, :T],
                                                            op0=ALU.mult, op1=ALU.add),
              reads=[POb, hb], writes=[hb])
        r2.prefetch()


class FFNSet:
    def __init__(self, kb, nc, name, w1, w3, w2):
        self.s13 = nc.dram_tensor(name + "_s13", [KF // 2, 128, 2, KD, 256], BF16, kind="Internal").ap()
        self.s2 = nc.dram_tensor(name + "_s2", [8, 128, KF, 128], BF16, kind="Internal").ap()
        self.bufs = [Buf(f"{name}_scr{i}") for i in range(3)]
        self.dss = [kb.dsem(f"ds_cv_{name}_{i}") for i in range(3)]
        self.w = (w1, w3, w2)

    def grp13(self, jp):
        return 0 if jp < 3 else 1

    def steps(self, kb):
        w1, w3, w2 = self.w
        w1v = w1.rearrange("(k p) n -> p k n", p=128)
        w3v = w3.rearrange("(k p) n -> p k n", p=128)
        w2v = w2.rearrange("(j p) n -> p j n", p=128)
        out = []
        for jp in range(KF // 2):
            gi = self.grp13(jp)
            out.append(lambda jp=jp, gi=gi: kb.dma("pool", self.s13[jp, :, 0, :, :], w1v[:, :, jp * 256:(jp + 1) * 256], self.dss[gi], writes=[self.bufs[gi]], part=True))
            out.append(lambda jp=jp, gi=gi: kb.dma("pool", self.s13[jp, :, 1, :, :], w3v[:, :, jp * 256:(jp + 1) * 256], self.dss[gi], writes=[self.bufs[gi]], part=True))
        for mb in range(8):
            out.append(lambda mb=mb: kb.dma("pool", self.s2[mb, :, :, :], w2v[:, :, mb * 128:(mb + 1) * 128], self.dss[2], writes=[self.bufs[2]], part=True))
        return out

    def convert(self, kb):
        for s in self.steps(kb):
            s()


def ffn_rings_bf16(kb, name, passes, q13="sp", q2="sp"):
    W13 = [kb.sb(f"{name}w13_{i}", [128, 2, KD, 256], BF16) for i in range(3)]
    W2 = [kb.sb(f"{name}w2_{i}", [128, KF, 128], BF16) for i in range(2)]
    it13, it2 = [], []
    for fs in passes:
        for jp in range(KF // 2):
            def f(kb, s, buf, ds, jp=jp, fs=fs):
                b_ = fs.bufs[fs.grp13(jp)]
                assert b_.w, "FFN weight block used before its conversion was emitted"
                kb.dma(q13, W13[s][:], fs.s13[jp], ds, reads=[b_], writes=[buf])
            it13.append(f)
        for mb in range(8):
            def f2(kb, s, buf, ds, mb=mb, fs=fs):
                assert fs.bufs[2].w, "FFN weight block used before its conversion was emitted"
                kb.dma(q2, W2[s][:], fs.s2[mb], ds, reads=[fs.bufs[2]], writes=[buf])
            it2.append(f2)
    r13 = Ring(kb, name + "r13", 3, it13)
    r2 = Ring(kb, name + "r2", 2, it2)
    r13.tiles = W13
    r2.tiles = W2
    return r13, r2


NH = 32
HP = 64
NG = 4
DI = 2048
C_Z, C_X, C_B, C_C, C_DT = 0, 2048, 4096, 4608, 5120


class WScr:
    def __init__(self, kb, nc):
        self.kb, self.nc = kb, nc
        self.blocks = {}
        self.pending = []

    def item(self, W8, key, parts, nelem, k, defer=False):
        kb = self.kb
        if key not in self.blocks:
            ap = self.nc.dram_tensor("wscr_" + key, [128, 4096], BF16, kind="Internal").ap()
            buf = Buf("wscr_" + key)
            ds = kb.dsem("ds_ws_" + key)

            def conv():
                first = True
                for (off, n, src) in parts:
                    dst = ap[:, 0:nelem].rearrange("p (k n) -> p k n", k=k)[:, :, off:off + n]
                    kb.dma("pool", dst, src, ds, writes=[buf], part=not first)
                    first = False
            self.blocks[key] = (ap, buf)
            if defer:
                self.pending.append(conv)
            else:
                conv()
        ap, sbuf = self.blocks[key]

        def f(kb, s, buf, ds):
            assert sbuf.w, "weight block used before its conversion was emitted: " + key
            kb.dma("pool", W8[s][:, 0:nelem], ap[:, 0:nelem], ds, reads=[sbuf], writes=[buf])
        return f

    def flush(self):
        while self.pending:
            self.pending.pop(0)()


def mamba_items_tile(W8, w_in, w_out, mode, ws):
    wv = w_in.rearrange("(k p) n -> p k n", p=128)
    items = []
    full = mode == "full"
    def xi(g):
        return ws.item(W8, f"x{g}", [(0, 512, wv[:, :, C_X + g * 512:C_X + (g + 1) * 512])], 8 * 512, 8)

    def bci(g):
        return ws.item(W8, f"bc{g}", [(0, 128, wv[:, :, C_B + g * 128:C_B + (g + 1) * 128]),
                                      (128, 128, wv[:, :, C_C + g * 128:C_C + (g + 1) * 128])], 8 * 256, 8)

    def zi(g):
        return ws.item(W8, f"z{g}", [(0, 512, wv[:, :, C_Z + g * 512:C_Z + (g + 1) * 512])], 8 * 512, 8)
    items += [xi(0), bci(0), ws.item(W8, "dt", [(0, 32, wv[:, :, C_DT:C_DT + 32])], 8 * 32, 8)]
    for g in range(NG):
        if g + 1 < NG:
            items += [xi(g + 1), bci(g + 1)]
        if full:
            items.append(zi(g))
    if full:
        wo = w_out.rearrange("(c p) n -> p c n", p=128)
        for mb in range(4):
            items.append(ws.item(W8, f"o{mb}", [(0, 256, wo[:, :, mb * 256:(mb + 1) * 256])], 16 * 256, 16))
    return items


class MambaP:
    def __init__(self, kb, dr, full):
        self.full = full
        sb = kb.sb
        self.cb_ = Buf("mconst")
        cbuf = self.cb_
        dsc = kb.dsem("ds_mc")
        self.g = sb("m_g", [128, KD], F32)
        self.cw = sb("m_cw", [128, 24, 4], F32)
        self.cb = sb("m_cb", [128, 24], F32)
        self.dtb = sb("m_dtb", [32, 1], F32)
        self.a = sb("m_a", [32, 1], F32)
        self.isec = sb("m_isec", [128, 1], F32)
        kb.dma("sp", self.g[:], dr["ssm_g"][:, :], dsc, writes=[cbuf])
        kb.dma("sp", self.cw[:], dr["cw"][:, :, :], dsc, writes=[cbuf], part=True)
        kb.dma("sp", self.cb[:], dr["cb"][:, :], dsc, writes=[cbuf], part=True)
        kb.dma("sp", self.dtb[:], dr["dtb"][:, :], dsc, writes=[cbuf], part=True)
        kb.dma("sp", self.a[:], dr["alog"][:, :], dsc, writes=[cbuf], part=True)
        kb.dma("sp", self.isec[:], dr["isec"][:, :], dsc, writes=[cbuf], part=True)
        if full:
            self.dch = sb("m_dch", [128, 16], F32)
            self.gn = sb("m_gn", [128, 16], F32)
            kb.dma("sp", self.dch[:], dr["dch"][:, :], dsc, writes=[cbuf], part=True)
            kb.dma("sp", self.gn[:], dr["gn"][:, :], dsc, writes=[cbuf], part=True)
        kb.op("act", lambda e: e.activation(out=self.a[:], in_=self.a[:], func=AF.Exp), reads=[cbuf], writes=[cbuf])
        kb.op("dve", lambda e: e.tensor_scalar(out=self.a[:], in0=self.a[:], scalar1=-1.0, scalar2=None, op0=ALU.mult),
              reads=[cbuf], writes=[cbuf])
        self.identf = sb("m_idf", [128, 128], F32)
        self.ident = sb("m_idb", [128, 128], BF16)
        self.tri = sb("m_tri", [128, 128], F32)
        self.sel = sb("m_sel", [128, 128], F32)
        self.kb_ = Buf("mk")
        k_ = self.kb_
        kb.op("pool", lambda e: e.memset(self.identf[:], 1.0), writes=[k_])
        kb.op("pool", lambda e: e.affine_select(out=self.identf[:], in_=self.identf[:], pattern=[[-1, 128]],
                                                 compare_op=ALU.is_equal, fill=0.0, base=0, channel_multiplier=1),
              reads=[k_], writes=[k_])
        kb.op("pool", lambda e: e.tensor_copy(out=self.ident[:], in_=self.identf[:]), reads=[k_], writes=[k_])
        kb.op("pool", lambda e: e.memset(self.tri[:], 1.0), reads=[k_], writes=[k_])
        kb.op("pool", lambda e: e.affine_select(out=self.tri[:], in_=self.tri[:], pattern=[[1, 128]],
                                                 compare_op=ALU.is_ge, fill=0.0, base=0, channel_multiplier=-1),
              reads=[k_], writes=[k_])
        kb.op("pool", lambda e: e.memset(self.sel[:], 1.0), reads=[k_], writes=[k_])
        kb.op("pool", lambda e: e.affine_select(out=self.sel[:], in_=self.sel[:], pattern=[[0, 128]],
                                                 compare_op=ALU.is_equal, fill=0.0, base=-127, channel_multiplier=1),
              reads=[k_], writes=[k_])
        self.DG = sb("m_DG", [128, 6 * 4, 128], BF16)
        self.DGb = Buf("DG")
        self.S = sb("m_S", [128, DI], F32)
        self.Sb = sb("m_Sb", [128, DI], BF16)
        self.Sbuf = [Buf(f"S{g}") for g in range(NG)]
        self.Sbb = [Buf(f"Sb{g}") for g in range(NG)]
        self.HAL = sb("m_hal", [128, 24, 4], BF16)
        self.halb = Buf("hal")
        self.acd = dr.get("acum_scr")
        self.acdb = Buf("acd")
        self.acds = kb.dsem("ds_acd")
        self.abcs = [kb.dsem(f"ds_abc{i}") for i in range(2)]


def mamba_tile(kb, P, R, hT, hb, T, mode, ring, W8, first=False, after_norm=None):
    full = mode == "full"
    NB = max(T // 128, 1)
    with contextlib.ExitStack() as st:
        sb = lambda n, s, d: st.enter_context(kb.nc.sbuf_tensor(uniq(n), list(s), d))
        ps = lambda n, s, d: st.enter_context(kb.nc.psum_tensor(uniq(n), list(s), d))
        PA = [ps(f"mpa{i}", [128, 512], F32) for i in range(2)]
        PAb = [Buf(f"mpa{i}") for i in range(2)]
        pai = [0]

        def nextpa():
            i = pai[0] % 2
            pai[0] += 1
            return PA[i], PAb[i]

        PT = ps("mpt", [128, 512], F32)
        PTb = Buf("mpt")
        rmsnorm_T(kb, R, hT, hb, P.g, P.cb_, PT, PTb, T)
        uT, ub = R.uT, R.ub
        if after_norm is not None:
            after_norm()

        XB = sb("m_XB", [128, 6, 4 + 512], BF16)
        XBb = Buf("XB")
        XC = sb("m_XC", [128, 6, 512], BF16)
        XCb = Buf("XC")

        def stageA(g):
            chunks = [g * 4 + i for i in range(4)] + [16 + g, 20 + g]
            if first:
                kb.op("dve", lambda e: e.memset(XB[:, :, 0:4], 0.0), writes=[XBb])
            else:
                kb.op("dve", lambda e: e.tensor_copy(out=XB[:, 0:4, 0:4], in_=P.HAL[:, g * 4:g * 4 + 4, :]), reads=[P.halb], writes=[XBb])
                kb.op("dve", lambda e: e.tensor_copy(out=XB[:, 4, 0:4], in_=P.HAL[:, 16 + g, :]), reads=[P.halb], writes=[XBb])
                kb.op("dve", lambda e: e.tensor_copy(out=XB[:, 5, 0:4], in_=P.HAL[:, 20 + g, :]), reads=[P.halb], writes=[XBb])
            s, wb = ring.get()
            Wx = W8[s][:, 0:8 * 512].rearrange("p (k n) -> p k n", k=8)
            for i in range(4):
                Pq, Pqb = nextpa()
                for k in range(KD):
                    kb.op("pe", lambda e, k=k, i=i: e.matmul(Pq[:, :T], Wx[:, k, i * 128:(i + 1) * 128], uT[:, k, :T],
                                                              start=(k == 0), stop=(k == KD - 1)),
                          reads=[wb, ub[k]], writes=[Pqb], inc=(k == KD - 1))
                kb.op("act", lambda e, i=i: e.activation(out=XB[:, i, 4:4 + T], in_=Pq[:, :T], func=AF.Copy),
                      reads=[Pqb], writes=[XBb])
            ring.prefetch()
            s, wb = ring.get()
            Wbc = W8[s][:, 0:8 * 256].rearrange("p (k n) -> p k n", k=8)
            for i in range(2):
                Pq, Pqb = nextpa()
                for k in range(KD):
                    kb.op("pe", lambda e, k=k, i=i: e.matmul(Pq[:, :T], Wbc[:, k, i * 128:(i + 1) * 128], uT[:, k, :T],
                                                              start=(k == 0), stop=(k == KD - 1)),
                          reads=[wb, ub[k]], writes=[Pqb], inc=(k == KD - 1))
                kb.op("act", lambda e, i=i: e.activation(out=XB[:, 4 + i, 4:4 + T], in_=Pq[:, :T], func=AF.Copy),
                      reads=[Pqb], writes=[XBb])
            ring.prefetch()
            for (d0, nd, c0) in ((0, 16, g * 4), (16, 4, 16 + g), (20, 4, 20 + g)):
                nch = nd // 4
                kb.op("pool", lambda e, d0=d0, nd=nd, c0=c0, nch=nch: e.tensor_tensor(
                    out=P.DG[:, d0:d0 + nd, :],
                    in0=P.identf[:].unsqueeze(1).to_broadcast([128, nd, 128]),
                    in1=P.cw[:, c0:c0 + nch, :].rearrange("p c k -> p (c k)").unsqueeze(2).to_broadcast([128, nd, 128]), op=ALU.mult),
                      reads=[P.kb_, P.cb_], writes=[P.DGb])

        assert mode != "halo"
        stageA(0)
        if mode != "halo":
            dtT = sb("m_dtT", [32, 512], F32)
            acT = sb("m_acT", [32, 512], F32)
            dtb_ = Buf("dtT")
            acb_ = Buf("acT")
            TK = sb("m_TK", [128, 4, 5, 32], F32)
            TKb = [Buf(f"TK{i}") for i in range(4)]
            s, wb = ring.get()
            Wd = W8[s][:, 0:8 * 32].rearrange("p (k n) -> p k n", k=8)
            Pd, Pdb = nextpa()
            for k in range(KD):
                kb.op("pe", lambda e, k=k: e.matmul(Pd[0:32, :T], Wd[:, k, 0:32], uT[:, k, :T], start=(k == 0), stop=(k == KD - 1)),
                      reads=[wb, ub[k]], writes=[Pdb], inc=(k == KD - 1))
            ring.prefetch()
            kb.op("act", lambda e: e.activation(out=dtT[:, :T], in_=Pd[0:32, :T], func=AF.Exp, bias=P.dtb[:], scale=1.0),
                  reads=[Pdb, P.cb_], writes=[dtb_])
            kb.op("act", lambda e: e.activation(out=dtT[:, :T], in_=dtT[:, :T], func=AF.Ln, bias=1.0, scale=1.0),
                  reads=[dtb_], writes=[dtb_])
            kb.op("dve", lambda e: e.tensor_scalar(out=acT[:, :T], in0=dtT[:, :T], scalar1=P.a[:, 0:1], scalar2=None, op0=ALU.mult),
                  reads=[dtb_, P.cb_], writes=[acb_])
            for tb in range(NB):
                sl = slice(tb * 128, (tb + 1) * 128)
                kb.op("dve", lambda e, sl=sl: e.tensor_tensor_scan(out=acT[:, sl], data0=R.ones[0:32, 0:128], data1=acT[:, sl],
                                                                    initial=0.0, op0=ALU.mult, op1=ALU.add),
                      reads=[acb_, R.onesb], writes=[acb_])
            if full:
                kb.dma("sp", P.acd[:, :T], acT[:, :T], P.acds, reads=[acb_], writes=[P.acdb])
            for tb in range(NB):
                sl = slice(tb * 128, (tb + 1) * 128)
                kb.op("pe", lambda e, sl=sl: e.transpose(PT[:, 0:32], dtT[:, sl], P.identf[0:32, 0:32]),
                      reads=[dtb_, P.kb_], writes=[PTb])
                kb.op("dve", lambda e, tb=tb: e.tensor_copy(out=TK[:, tb, 0, :], in_=PT[:, 0:32]), reads=[PTb], writes=[TKb[tb]])
                kb.op("pe", lambda e, sl=sl: e.transpose(PT[:, 32:64], acT[:, sl], P.identf[0:32, 0:32]),
                      reads=[acb_, P.kb_], writes=[PTb])
                kb.op("dve", lambda e, tb=tb: e.tensor_copy(out=TK[:, tb, 4, :], in_=PT[:, 32:64]), reads=[PTb], writes=[TKb[tb]])
                kb.op("dve", lambda e, tb=tb: e.tensor_scalar(out=TK[:, tb, 1, :], in0=TK[:, tb, 4, :], scalar1=-1.0, scalar2=None, op0=ALU.mult),
                      reads=[TKb[tb]], writes=[TKb[tb]])
                kb.op("act", lambda e, tb=tb: e.activation(out=TK[:, tb, 2, :], in_=TK[:, tb, 4, :], func=AF.Exp),
                      reads=[TKb[tb]], writes=[TKb[tb]])
                kb.op("pe", lambda e, tb=tb: e.matmul(PT[:, 64:96], P.sel[:], TK[:, tb, 4, :], start=True, stop=True),
                      reads=[TKb[tb], P.kb_], writes=[PTb])
                kb.op("dve", lambda e, tb=tb: e.tensor_tensor(out=TK[:, tb, 3, :], in0=PT[:, 64:96], in1=TK[:, tb, 4, :], op=ALU.subtract),
                      reads=[PTb, TKb[tb]], writes=[TKb[tb]])
                kb.op("act", lambda e, tb=tb: e.activation(out=TK[:, tb, 3, :], in_=TK[:, tb, 3, :], func=AF.Exp),
                      reads=[TKb[tb]], writes=[TKb[tb]])
                kb.op("dve", lambda e, tb=tb: e.tensor_tensor(out=TK[:, tb, 3, :], in0=TK[:, tb, 3, :], in1=TK[:, tb, 0, :], op=ALU.mult),
                      reads=[TKb[tb]], writes=[TKb[tb]])
                kb.op("act", lambda e, tb=tb: e.activation(out=TK[:, tb, 4, :], in_=PT[:, 64:96], func=AF.Exp),
                      reads=[PTb, TKb[tb]], writes=[TKb[tb]])
            PX = ps("mpx", [128, 1024], BF16)
            PXb = Buf("mpx")
            xw = sb("m_xw", [128, 4, 512], BF16)
            xwb = Buf("xw")
            Btok = sb("m_Bt", [128, 4, 128], BF16)
            Btb = Buf("Bt")
            t2 = sb("m_t2", [128, 512], F32)
            t2b = Buf("t2")
            if full:
                PYd = [ps(f"mpyd{i}", [128, 512], F32) for i in range(2)]
                PYdb = [Buf(f"mpyd{i}") for i in range(2)]
                PYo = [ps(f"mpyo{i}", [128, 512], F32) for i in range(2)]
                PYob = [Buf(f"mpyo{i}") for i in range(2)]
                SZ = sb("m_SZ", [128, 4, 512], BF16)
                SZb = Buf("SZ")
                xD = sb("m_xD", [128, 4, 512], BF16)
                xDb = Buf("xD")
                xdt = sb("m_xdt", [128, 4, 512], BF16)
                xdtb = Buf("xdt")
                ABC = [sb(f"m_abc{i}", [128, 8, 128], F32) for i in range(2)]
                ABCb = [Buf(f"abc{i}") for i in range(2)]
                dar = [sb(f"m_dar{i}", [128, 4, 128], F32) for i in range(4)]
                darb = [Buf(f"dar{i}") for i in range(4)]
                Ee = [sb(f"m_Ee{i}", [128, 4, 128], BF16) for i in range(4)]
                Eeb = [Buf(f"Ee{i}") for i in range(4)]
                CBm = [sb(f"m_CBm{i}", [128, 128], F32) for i in range(2)]
                CBmb = [Buf(f"CBm{i}") for i in range(2)]
                scT = [sb(f"m_sc{i}", [128, 4, 128], BF16) for i in range(4)]
                scb = [Buf(f"sc{i}") for i in range(4)]
                yg = sb("m_yg", [128, 4, 512], F32)
                ygb = [Buf(f"yg{i}") for i in range(4)]
                t1s = [sb(f"m_t1_{i}", [128, 512], F32) for i in range(2)]
                t1bs = [Buf(f"t1_{i}") for i in range(2)]
                pend = []
                pend2 = []
                ynb = sb("m_ynb", [128, 4, 512], BF16)
                ynbb = Buf("ynb")
                ynT = sb("m_ynT", [128, 16, 512], BF16)
                ynTb = Buf("ynT")
                ss = sb("m_ss", [128, 4], F32)
                ssb = Buf("ss")
                abci = [0]

        norm_pend = []
        for g in range(NG):
            chunks = [g * 4 + i for i in range(4)] + [16 + g, 20 + g]
            for i, c in enumerate(chunks):
                Pq, Pqb = nextpa()
                for k in range(4):
                    kb.op("pe", lambda e, k=k, i=i, c=c: e.matmul(Pq[:, :T], P.DG[:, i * 4 + k, :], XB[:, i, 1 + k:1 + k + T],
                                                                   start=(k == 0), stop=(k == 3)),
                          reads=[XBb, P.DGb], writes=[Pqb], inc=(k == 3))
                kb.op("act", lambda e, i=i, c=c: e.activation(out=XC[:, i, :T], in_=Pq[:, :T], func=AF.Silu, bias=P.cb[:, c:c + 1], scale=1.0),
                      reads=[Pqb, P.cb_], writes=[XCb])
            kb.op("dve", lambda e: e.tensor_copy(out=P.HAL[:, g * 4:g * 4 + 4, :], in_=XB[:, 0:4, T:T + 4]), reads=[XBb], writes=[P.halb])
            kb.op("dve", lambda e: e.tensor_copy(out=P.HAL[:, 16 + g, :], in_=XB[:, 4, T:T + 4]), reads=[XBb], writes=[P.halb])
            kb.op("dve", lambda e: e.tensor_copy(out=P.HAL[:, 20 + g, :], in_=XB[:, 5, T:T + 4]), reads=[XBb], writes=[P.halb])
            if g + 1 < NG:
                stageA(g + 1)
            if full:
                s, wb = ring.get()
                Wz = W8[s][:, 0:8 * 512].rearrange("p (k n) -> p k n", k=8)
                for tb in range(NB):
                    Pq, Pqb = nextpa()
                    for k in range(KD):
                        kb.op("pe", lambda e, k=k, tb=tb: e.matmul(Pq[:, :], uT[:, k, tb * 128:(tb + 1) * 128], Wz[:, k, :],
                                                                    start=(k == 0), stop=(k == KD - 1)),
                              reads=[wb, ub[k]], writes=[Pqb], inc=(k == KD - 1))
                    kb.op("act", lambda e, tb=tb: e.activation(out=SZ[:, tb, :], in_=Pq[:, :], func=AF.Silu),
                          reads=[Pqb], writes=[SZb])
                ring.prefetch()
                for i in range(4):
                    kb.op("act", lambda e, i=i: e.activation(out=xD[:, i, :T], in_=XC[:, i, :T], func=AF.Copy,
                                                              scale=P.dch[:, g * 4 + i:g * 4 + i + 1]),
                          reads=[XCb, P.cb_], writes=[xDb])
            while norm_pend:
                norm_pend.pop(0)()
            for tb in range(NB):
                sl = slice(tb * 128, (tb + 1) * 128)
                kb.op("pe", lambda e, sl=sl, tb=tb: e.transpose(PX[:, 512 + tb * 128:640 + tb * 128], XC[:, 4, sl], P.ident[:]),
                      reads=[XCb, P.kb_], writes=[PXb], inc=(tb == NB - 1))
            kb.op("act", lambda e: e.activation(out=Btok[:, 0:NB, :], in_=PX[:, 512:512 + NB * 128].rearrange("p (t n) -> p t n", t=NB), func=AF.Copy),
                  reads=[PXb], writes=[Btb])
            for tb in range(NB):
                sl = slice(tb * 128, (tb + 1) * 128)
                for i in range(4):
                    kb.op("pe", lambda e, i=i, sl=sl: e.transpose(PX[:, i * 128:(i + 1) * 128], XC[:, i, sl], P.ident[:]),
                          reads=[XCb, P.kb_], writes=[PXb], inc=(i == 3))
                hs = slice(g * 8, (g + 1) * 8)
                kb.op("dve", lambda e, tb=tb: e.tensor_tensor(out=xw[:, tb, :].rearrange("p (h d) -> p h d", h=8),
                                                               in0=PX[:, 0:512].rearrange("p (h d) -> p h d", h=8),
                                                               in1=TK[:, tb, 3, hs].unsqueeze(2).to_broadcast([128, 8, 64]), op=ALU.mult),
                      reads=[PXb, TKb[tb]], writes=[xwb])
                if full:
                    kb.op("dve", lambda e, tb=tb: e.tensor_tensor(out=xdt[:, tb, :].rearrange("p (h d) -> p h d", h=8),
                                                                   in0=PX[:, 0:512].rearrange("p (h d) -> p h d", h=8),
                                                                   in1=TK[:, tb, 0, hs].unsqueeze(2).to_broadcast([128, 8, 64]), op=ALU.mult),
                          reads=[PXb, TKb[tb]], writes=[xdtb])
            gs = slice(g * 512, (g + 1) * 512)
            hs = slice(g * 8, (g + 1) * 8)

            def state_update(tb):
                PS_, PSb = nextpa()
                kb.op("pe", lambda e: e.matmul(PS_[:, :], Btok[:, tb, :], xw[:, tb, :], start=True, stop=True),
                      reads=[Btb, xwb], writes=[PSb])
                kb.op("pool", lambda e: e.tensor_tensor(out=t2[:].rearrange("p (h d) -> p h d", h=8),
                                                        in0=P.S[:, gs].rearrange("p (h d) -> p h d", h=8),
                                                        in1=TK[:, tb, 4, hs].unsqueeze(2).to_broadcast([128, 8, 64]), op=ALU.mult),
                      reads=[P.Sbuf[g], TKb[tb]], writes=[t2b])
                if full:
                    kb.op("dve", lambda e: e.tensor_tensor(out=P.Sb[:, gs], in0=PS_[:, :], in1=t2[:], op=ALU.add),
                          reads=[PSb, t2b], writes=[P.Sbb[g]])
                kb.op("dve", lambda e: e.tensor_tensor(out=P.S[:, gs], in0=PS_[:, :], in1=t2[:], op=ALU.add),
                      reads=[PSb, t2b], writes=[P.Sbuf[g]])

            if not full:
                for tb in range(NB):
                    state_update(tb)
            else:
                def head_a(tb):
                    sl = slice(tb * 128, (tb + 1) * 128)
                    p = tb % 2
                    ai = abci[0] % 2
                    abci[0] += 1
                    src = P.acd[g * 8:(g + 1) * 8, sl]
                    kb.dma("sp", ABC[ai][:], src.partition_broadcast(128), P.abcs[ai], reads=[P.acdb], writes=[ABCb[ai]])
                    kb.op("pe", lambda e: e.matmul(PT[:, 0:128], XC[:, 4, sl], XC[:, 5, sl], start=True, stop=True),
                          reads=[XCb], writes=[PTb])
                    kb.op("dve", lambda e: e.tensor_tensor(out=CBm[p][:], in0=PT[:, 0:128], in1=P.tri[:], op=ALU.mult),
                          reads=[PTb, P.kb_], writes=[CBmb[p]])
                    for hb4 in range(2):
                        bi = p * 2 + hb4
                        h0 = g * 8 + hb4 * 4
                        kb.op("pool", lambda e, hb4=hb4, h0=h0, bi=bi: e.tensor_tensor(
                            out=dar[bi][:], in0=ABC[ai][:, hb4 * 4:hb4 * 4 + 4, :],
                            in1=TK[:, tb, 1, h0:h0 + 4].unsqueeze(2).to_broadcast([128, 4, 128]), op=ALU.add),
                              reads=[ABCb[ai], TKb[tb]], writes=[darb[bi]])
                        kb.op("pool", lambda e, bi=bi: e.tensor_tensor(out=dar[bi][:], in0=dar[bi][:],
                                                                        in1=P.tri[:].unsqueeze(1).to_broadcast([128, 4, 128]), op=ALU.mult),
                              reads=[darb[bi], P.kb_], writes=[darb[bi]])
                        kb.op("act", lambda e, bi=bi: e.activation(out=Ee[bi][:], in_=dar[bi][:], func=AF.Exp),
                              reads=[darb[bi]], writes=[Eeb[bi]])

                def head_b(tb):
                    p = tb % 2
                    for hb4 in range(2):
                        bi = p * 2 + hb4
                        kb.op("dve", lambda e, bi=bi: e.tensor_tensor(
                            out=scT[bi][:], in0=Ee[bi][:], in1=CBm[p][:].unsqueeze(1).to_broadcast([128, 4, 128]), op=ALU.mult),
                              reads=[Eeb[bi], CBmb[p]], writes=[scb[bi]])

                def body_pre(tb):
                    sl = slice(tb * 128, (tb + 1) * 128)
                    p = tb % 2
                    kb.op("pe", lambda e: e.matmul(PYo[p][:, :], XC[:, 5, sl], P.Sb[:, gs], start=True, stop=True),
                          reads=[XCb, P.Sbb[g]], writes=[PYob[p]])
                    t1, t1b = t1s[p], t1bs[p]
                    kb.op("dve", lambda e: e.tensor_tensor(out=t1[:].rearrange("p (h d) -> p h d", h=8),
                                                           in0=PYo[p][:, :].rearrange("p (h d) -> p h d", h=8),
                                                           in1=TK[:, tb, 2, hs].unsqueeze(2).to_broadcast([128, 8, 64]), op=ALU.mult),
                          reads=[PYob[p], TKb[tb]], writes=[t1b])
                    state_update(tb)
                    while pend:
                        pend.pop(0)()

                def body_mm(tb):
                    sl = slice(tb * 128, (tb + 1) * 128)
                    p = tb % 2
                    t1, t1b = t1s[p], t1bs[p]
                    for hb4 in range(2):
                        bi = p * 2 + hb4
                        for pp in range(2):
                            pr = hb4 * 2 + pp
                            kb.op("pe", lambda e, pr=pr: e.matmul(PYd[p][:, pr * 128:(pr + 1) * 128], xD[:, pr, sl], P.ident[:],
                                                                   start=True, stop=False),
                                  reads=[xDb, P.kb_], writes=[PYdb[p]], inc=False)
                            for hh in range(2):
                                hl = pr * 2 + hh
                                j4 = pp * 2 + hh
                                kb.op("pe", lambda e, hl=hl, j4=j4, bi=bi, hh=hh: e.matmul(
                                    PYd[p][:, hl * 64:(hl + 1) * 64], scT[bi][:, j4, :], xdt[:, tb, hl * 64:(hl + 1) * 64], start=False, stop=(hh == 1)),
                                      reads=[scb[bi], xdtb], writes=[PYdb[p]], inc=(hh == 1))
                    while pend2:
                        pend2.pop(0)()

                    def tail1():
                        kb.op("dve", lambda e: e.tensor_tensor(out=t1[:], in0=PYd[p][:, :], in1=t1[:], op=ALU.add),
                              reads=[PYdb[p], t1b], writes=[t1b])

                    def tail2():
                        kb.op("pool", lambda e: e.tensor_tensor(out=yg[:, tb, :], in0=t1[:], in1=SZ[:, tb, :], op=ALU.mult),
                              reads=[t1b, SZb], writes=[ygb[tb]])
                    pend.append(tail1)
                    pend2.append(tail2)

                head_a(0)
                head_b(0)
                for tb in range(NB):
                    if tb + 1 < NB:
                        head_a(tb + 1)
                    body_pre(tb)
                    if tb + 1 < NB:
                        head_b(tb + 1)
                    body_mm(tb)
                while pend:
                    pend.pop(0)()
                while pend2:
                    pend2.pop(0)()
            if full:
                for tb in range(NB):
                    kb.op("act", lambda e, tb=tb: e.activation(out=ynb[:, tb, :], in_=yg[:, tb, :], func=AF.Square, accum_out=ss[:, tb:tb + 1]),
                          reads=[ygb[tb]], writes=[ynbb, ssb])
                kb.op("act", lambda e: e.activation(out=ss[:, 0:NB], in_=ss[:, 0:NB], func=AF.Sqrt, bias=R.epsb[:], scale=1.0 / 512),
                      reads=[ssb, R.epsbuf], writes=[ssb])
                kb.op("dve", lambda e: e.reciprocal(ss[:, 0:NB], ss[:, 0:NB]), reads=[ssb], writes=[ssb])
                for tb in range(NB):
                    kb.op("dve", lambda e, tb=tb: e.tensor_scalar(out=ynb[:, tb, :], in0=yg[:, tb, :], scalar1=ss[:, tb:tb + 1], scalar2=None, op0=ALU.mult),
                          reads=[ygb[tb], ssb], writes=[ynbb])
                def norm_tr(g=g):
                    for i in range(4):
                        for tb in range(NB):
                            kb.op("pe", lambda e, i=i, tb=tb: e.transpose(PX[:, tb * 128:(tb + 1) * 128], ynb[:, tb, i * 128:(i + 1) * 128], P.ident[:]),
                                  reads=[ynbb, P.kb_], writes=[PXb], inc=(tb == NB - 1))
                        c = g * 4 + i
                        kb.op("dve", lambda e, c=c: e.tensor_scalar(out=ynT[:, c, :T], in0=PX[:, 0:T], scalar1=P.gn[:, c:c + 1], scalar2=None, op0=ALU.mult),
                              reads=[PXb, P.cb_], writes=[ynTb])
                norm_pend.append(norm_tr)
        while norm_pend:
            norm_pend.pop(0)()
        if full:
            for mb in range(4):
                s, wb = ring.get()
                Wo = W8[s][:, 0:16 * 256].rearrange("p (c n) -> p c n", c=16)
                for mm in range(2):
                    m = mb * 2 + mm
                    Pq, Pqb = nextpa()
                    for c in range(16):
                        kb.op("pe", lambda e, c=c, mm=mm: e.matmul(Pq[:, :T], Wo[:, c, mm * 128:(mm + 1) * 128], ynT[:, c, :T],
                                                                    start=(c == 0), stop=(c == 15)),
                              reads=[wb, ynTb], writes=[Pqb], inc=(c == 15))
                    kb.op("dve", lambda e, m=m: e.tensor_tensor(out=hT[:, m, :T], in0=Pq[:, :T], in1=hT[:, m, :T], op=ALU.add),
                          reads=[Pqb, hb], writes=[hb])
                ring.prefetch()
        kb.barrier()


T = 512
NTOK = 4096
NT = NTOK // T


def dram_in(nc, name, shape, dt=F32):
    return nc.dram_tensor(name, list(shape), dt, kind="ExternalInput").ap()


def dram_out(nc, name, shape, dt=F32):
    return nc.dram_tensor(name, list(shape), dt, kind="ExternalOutput").ap()


def load_vec(kb, name, ap, shape, buf, ds, dt=F32):
    t = kb.sb(name, shape, dt)
    kb.dma("sp", t[:], ap, ds, writes=[buf], part=True)
    return t


def mamba_inputs(nc, full):
    dr = {"ssm_g": dram_in(nc, "ssm_g", [128, KD]), "cw": dram_in(nc, "cw", [128, 24, 4]), "cb": dram_in(nc, "cb", [128, 24]),
          "dtb": dram_in(nc, "dtb", [32, 1]), "alog": dram_in(nc, "alog", [32, 1]), "isec": dram_in(nc, "isec", [128, 1]),
          "w_in": dram_in(nc, "w_in", [D, 5152])}
    if full:
        dr["dch"] = dram_in(nc, "dch", [128, 16])
        dr["gn"] = dram_in(nc, "gn", [128, 16])
        dr["w_out"] = dram_in(nc, "w_out", [2048, D])
        dr["acum_scr"] = nc.dram_tensor("acum_scr", [32, 512], F32, kind="Internal").ap()
    return dr


def kv_items(W8, w_kv, ws):
    wv = w_kv.rearrange("(k p) n -> p k n", p=128)
    parts = []
    for kvh in range(2):
        for dup in range(2):
            parts.append((kvh * 128 + dup * 64, 64, wv[:, :, kvh * 64:(kvh + 1) * 64]))
    parts.append((256, 128, wv[:, :, 128:256]))
    return [ws.item(W8, "kv", parts, 8 * 384, 8)]


def kv_tile(kb, R, KVc, hT, hb, ring, W8, KT_out, V_out, t, osem, dbuf=None, after_norm=None):
    with contextlib.ExitStack() as st:
        sb = lambda n, s, d: st.enter_context(kb.nc.sbuf_tensor(uniq(n), list(s), d))
        ps = lambda n, s, d: st.enter_context(kb.nc.psum_tensor(uniq(n), list(s), d))
        PA = [ps(f"kpa{i}", [128, 512], F32) for i in range(2)]
        PAb = [Buf(f"kpa{i}") for i in range(2)]
        PN = ps("kpn", [128, 512], F32); PNb = Buf("kpn")
        rmsnorm_T(kb, R, hT, hb, KVc["g"], KVc["buf"], PN, PNb, T)
        if after_norm is not None:
            after_norm()
        s, wb = ring.get()
        W = W8[s][:, 0:8 * 384].rearrange("p (k n) -> p k n", k=8)
        KT = sb("k_KT", [128, 2, T], BF16); KTb = Buf("KT")
        sq = sb("k_sq", [128, T], BF16); sqb = Buf("ksq")
        rs = sb("k_rs", [128, T], F32); rsb = Buf("krs")
        Vt = sb("k_V", [128, 4, 128], BF16); Vb = Buf("kV")
        for kvh in range(2):
            Pq, Pqb = PA[kvh], PAb[kvh]
            for k in range(KD):
                kb.op("pe", lambda e, k=k: e.matmul(Pq[:, :T], W[:, k, kvh * 128:(kvh + 1) * 128], R.uT[:, k, :T],
                                                     start=(k == 0), stop=(k == KD - 1)),
                      reads=[wb, R.ub[k]], writes=[Pqb], inc=(k == KD - 1))
            kb.op("act", lambda e: e.activation(out=sq[:, :T], in_=Pq[:, :T], func=AF.Square), reads=[Pqb], writes=[sqb])
            kb.op("pe", lambda e: e.matmul(PN[:, :T], KVc["BD"][:], sq[:, :T], start=True, stop=True),
                  reads=[sqb, KVc["buf"]], writes=[PNb])
            kb.op("act", lambda e: e.activation(out=rs[:, :T], in_=PN[:, :T], func=AF.Sqrt, bias=R.epsb[:], scale=1.0 / 64),
                  reads=[PNb, R.epsbuf], writes=[rsb])
            kb.op("dve", lambda e: e.reciprocal(rs[:, :T], rs[:, :T]), reads=[rsb], writes=[rsb])
            kb.op("dve", lambda e: e.scalar_tensor_tensor(out=KT[:, kvh, :T], in0=Pq[:, :T], scalar=KVc["kn2"][:, 0:1], in1=rs[:, :T],
                                                           op0=ALU.mult, op1=ALU.mult),
                  reads=[Pqb, rsb, KVc["buf"]], writes=[KTb])
        for tb in range(4):
            Pq, Pqb = PA[tb % 2], PAb[tb % 2]
            for k in range(KD):
                kb.op("pe", lambda e, k=k: e.matmul(Pq[:, 0:128], R.uT[:, k, tb * 128:(tb + 1) * 128], W[:, k, 256:384],
                                                     start=(k == 0), stop=(k == KD - 1)),
                      reads=[wb, R.ub[k]], writes=[Pqb], inc=(k == KD - 1))
            kb.op("act", lambda e: e.activation(out=Vt[:, tb, :], in_=Pq[:, 0:128], func=AF.Copy), reads=[Pqb], writes=[Vb])
        ring.prefetch()
        wr = [dbuf] if dbuf is not None else []
        if dbuf is None:
            for kvh in range(2):
                kb.dma("sp", KT_out[kvh, :, t * T:(t + 1) * T], KT[:, kvh, :], osem, reads=[KTb])
            kb.dma("sp", V_out[t * T:(t + 1) * T, :].rearrange("(b p) c -> p b c", p=128), Vt[:], osem, reads=[Vb])
        elif t < 0:
            for kvh in range(2):
                kb.dma("sp", KT_out[kvh, :, 0:128], KT[:, kvh, T - 128:T], osem, reads=[KTb], writes=wr, part=True)
            kb.dma("sp", V_out[0:128, :], Vt[:, 3, :], osem, reads=[Vb], writes=wr, part=True)
        else:
            o = 128 + t * T
            for kvh in range(2):
                kb.dma("sp", KT_out[kvh, :, o:o + T], KT[:, kvh, :], osem, reads=[KTb], writes=wr, part=True)
            kb.dma("sp", V_out[o:o + T, :].rearrange("(b p) c -> p b c", p=128), Vt[:], osem, reads=[Vb], writes=wr, part=True)
        kb.barrier()


def make_BD(kb, name, buf):
    t = kb.sb(name, [128, 128], BF16)
    kb.op("pool", lambda e: e.memset(t[:], 0.0), writes=[buf])
    kb.op("pool", lambda e: e.memset(t[0:64, 0:64], 1.0), reads=[buf], writes=[buf])
    kb.op("pool", lambda e: e.memset(t[64:128, 64:128], 1.0), reads=[buf], writes=[buf])
    return t


CSH = 4.0


def attn_items(W8, w_q, w_o, ws):
    wq = w_q.rearrange("(k p) n -> p k n", p=128)
    wo = w_o.rearrange("(k p) n -> p k n", p=128)
    a = [ws.item(W8, f"q{h}", [(0, 512, wq[:, :, h * 512:(h + 1) * 512])], 8 * 512, 8, defer=True) for h in range(2)]
    b = [ws.item(W8, f"wo{h}", [(0, 512, wo[:, :, h * 512:(h + 1) * 512])], 8 * 512, 8, defer=True) for h in range(2)]
    return a, b


def attn_tile(kb, R, AC, hT, hb, ring, W8, t):
    with contextlib.ExitStack() as st:
        sb = lambda n, s, d: st.enter_context(kb.nc.sbuf_tensor(uniq(n), list(s), d))
        ps = lambda n, s, d: st.enter_context(kb.nc.psum_tensor(uniq(n), list(s), d))
        PSs = [[ps(f"aps{q}{i}", [128, 512], F32) for i in range(2)] for q in range(2)]
        PSsb = [[Buf(f"aps{q}{i}") for i in range(2)] for q in range(2)]
        PO = [ps(f"apo{i}", [128, 512], F32) for i in range(2)]; POb = [Buf(f"apo{i}") for i in range(2)]
        PD = ps("apd", [128, 512], F32); PDb = Buf("apd")
        PX = ps("apx", [128, 1024], BF16); PXb = Buf("apx")
        cb = AC["buf"]
        rmsnorm_T(kb, R, hT, hb, AC["g"], cb, PD, PDb, T)
        QT = sb("a_QT", [128, 8, T], BF16); QTb = Buf("QT")
        sq = sb("a_sq", [128, T], BF16); sqb = Buf("asq")
        rs = sb("a_rs", [128, T], F32); rsb = Buf("ars")
        for half in range(2):
            s, wb = ring.get()
            W = W8[s][:, 0:8 * 512].rearrange("p (k n) -> p k n", k=8)
            for i in range(4):
                qc = half * 4 + i
                Pq, Pqb = PSs[0][i % 2], PSsb[0][i % 2]
                for k in range(KD):
                    kb.op("pe", lambda e, k=k, i=i: e.matmul(Pq[:, :T], W[:, k, i * 128:(i + 1) * 128], R.uT[:, k, :T],
                                                              start=(k == 0), stop=(k == KD - 1)),
                          reads=[wb, R.ub[k]], writes=[Pqb], inc=(k == KD - 1))
                kb.op("act", lambda e: e.activation(out=sq[:, :T], in_=Pq[:, :T], func=AF.Square), reads=[Pqb], writes=[sqb])
                kb.op("pe", lambda e: e.matmul(PD[:, :T], AC["BD"][:], sq[:, :T], start=True, stop=True),
                      reads=[sqb, cb], writes=[PDb])
                kb.op("act", lambda e: e.activation(out=rs[:, :T], in_=PD[:, :T], func=AF.Sqrt, bias=R.epsb[:], scale=1.0 / 64),
                      reads=[PDb, R.epsbuf], writes=[rsb])
                kb.op("dve", lambda e: e.reciprocal(rs[:, :T], rs[:, :T]), reads=[rsb], writes=[rsb])
                kb.op("dve", lambda e, qc=qc: e.scalar_tensor_tensor(out=QT[:, qc, :T], in0=Pq[:, :T], scalar=AC["qn2"][:, 0:1], in1=rs[:, :T],
                                                                      op0=ALU.mult, op1=ALU.mult),
                      reads=[Pqb, rsb, cb], writes=[QTb])
            ring.prefetch()
        OT = sb("a_OT", [128, 8, T], BF16); OTb = Buf("OT")
        PTs = [[sb(f"a_PT{q}{i}", [128, 512], BF16) for i in range(2)] for q in range(2)]
        PTsb = [[Buf(f"aPT{q}{i}") for i in range(2)] for q in range(2)]
        Ee = [[sb(f"a_Ee{q}{i}", [128, 512], BF16) for i in range(2)] for q in range(2)]
        Eeb = [[Buf(f"aEe{q}{i}") for i in range(2)] for q in range(2)]
        On = sb("a_On", [128, 1024], BF16); Onb = Buf("On")
        den = sb("a_den", [128, 16], F32); denb = Buf("den")
        KTs, Vs = AC["KT"], AC["V"]
        quads = [(qb, kvh, par) for qb in range(4) for kvh in range(2) for par in range(2)]

        def front(qi):
            qb, kvh, par = quads[qi]
            pq = qi % 2
            gb = t * 4 + qb
            EBx = AC["EB0"] if gb == 0 else AC["EB"]
            qs = slice(qb * 128, (qb + 1) * 128)
            rows = slice(par * 64, par * 64 + 64)
            for blk in range(2):
                kc = gb * 128 + blk * 128
                kb.op("pe", lambda e, blk=blk, kc=kc: e.matmul(PSs[pq][blk][:, :], KTs[rows, kvh, kc:kc + 128],
                                                                QT[rows, kvh * 4:(kvh + 1) * 4, qs], start=True, stop=True),
                      reads=[AC["kvbuf"], QTb], writes=[PSsb[pq][blk]])
                kb.op("act", lambda e, blk=blk: e.activation(out=Ee[pq][blk][:], in_=PSs[pq][blk][:, :], func=AF.Exp, scale=0.125),
                      reads=[PSsb[pq][blk]], writes=[Eeb[pq][blk]])
                idx = (blk * 2 + kvh) * 2 + par
                kb.op("dve", lambda e, blk=blk, idx=idx: e.tensor_tensor(
                    out=PTs[pq][blk][:], in0=Ee[pq][blk][:],
                    in1=EBx[:, idx * 4:(idx + 1) * 4, :].rearrange("p j q -> p (j q)"), op=ALU.mult),
                      reads=[Eeb[pq][blk], cb], writes=[PTsb[pq][blk]])

        def back(qi):
            qb, kvh, par = quads[qi]
            pq = qi % 2
            gb = t * 4 + qb
            for jj in range(4):
                h = kvh * 8 + 2 * jj + par
                bank, hc = h // 8, (h % 8) * 64
                for blk in range(2):
                    kb.op("pe", lambda e, blk=blk: e.matmul(PO[bank][:, hc:hc + 64], PTs[pq][blk][:, jj * 128:(jj + 1) * 128],
                                                             Vs[:, gb + blk, kvh * 64:(kvh + 1) * 64], start=(blk == 0), stop=(blk == 1)),
                          reads=[PTsb[pq][blk], AC["kvbuf"]], writes=[POb[bank]], inc=(blk == 1))
                for blk in range(2):
                    kb.op("pe", lambda e, blk=blk: e.matmul(PD[:, h:h + 1], PTs[pq][blk][:, jj * 128:(jj + 1) * 128],
                                                             AC["onec"][:, 0:1], start=(blk == 0), stop=(blk == 1)),
                          reads=[PTsb[pq][blk], cb], writes=[PDb], inc=(blk == 1))

        front(0)
        for qi in range(len(quads)):
            qb = quads[qi][0]
            qs = slice(qb * 128, (qb + 1) * 128)
            if qi + 1 < len(quads):
                front(qi + 1)
            back(qi)
            if qi % 4 != 3:
                continue
            kb.op("dve", lambda e: e.tensor_tensor(out=den[:], in0=PD[:, 0:16], in1=AC["esink"][:], op=ALU.add),
                  reads=[PDb, cb], writes=[denb])
            kb.op("dve", lambda e: e.reciprocal(den[:], den[:]), reads=[denb], writes=[denb])
            for bank in range(2):
                kb.op("dve", lambda e, bank=bank: e.tensor_tensor(
                    out=On[:, bank * 512:(bank + 1) * 512].rearrange("p (h d) -> p h d", h=8),
                    in0=PO[bank][:, :].rearrange("p (h d) -> p h d", h=8),
                    in1=den[:, bank * 8:(bank + 1) * 8].unsqueeze(2).to_broadcast([128, 8, 64]), op=ALU.mult),
                      reads=[POb[bank], denb], writes=[Onb])
            for c in range(8):
                kb.op("pe", lambda e, c=c: e.transpose(PX[:, c * 128:(c + 1) * 128], On[:, c * 128:(c + 1) * 128], AC["ident"][:]),
                      reads=[Onb, cb], writes=[PXb], inc=(c == 7))
            kb.op("act", lambda e: e.activation(out=OT[:, :, qs], in_=PX[:, :].rearrange("p (c q) -> p c q", c=8), func=AF.Copy),
                  reads=[PXb], writes=[OTb])
        for half in range(2):
            s, wb = ring.get()
            W = W8[s][:, 0:8 * 512].rearrange("p (k n) -> p k n", k=8)
            for i in range(4):
                m = half * 4 + i
                Pq, Pqb = PSs[0][i % 2], PSsb[0][i % 2]
                for c in range(8):
                    kb.op("pe", lambda e, c=c, i=i: e.matmul(Pq[:, :T], W[:, c, i * 128:(i + 1) * 128], OT[:, c, :T],
                                                              start=(c == 0), stop=(c == 7)),
                          reads=[wb, OTb], writes=[Pqb], inc=(c == 7))
                kb.op("dve", lambda e, m=m: e.tensor_tensor(out=hT[:, m, :T], in0=Pq[:, :T], in1=hT[:, m, :T], op=ALU.add),
                      reads=[Pqb, hb], writes=[hb])
            ring.prefetch()
        kb.barrier()


def build_fused(ntiles=NT):
    nc = bass.Bass("TRN2", target_bir_lowering=False)
    ntok = ntiles * T
    xTp = dram_in(nc, "xTp", [D, ntok])
    xT = dram_in(nc, "xT", [D, ntok])
    gF = [dram_in(nc, f"gF{i}", [128, KD]) for i in range(4)]
    wF = [[dram_in(nc, f"wF{i}_{j}", s) for j, s in enumerate([[D, DFF], [D, DFF], [DFF, D]])] for i in range(4)]
    dr = mamba_inputs(nc, True)
    kvg = dram_in(nc, "kvg", [128, KD]); kn2 = dram_in(nc, "kn2", [128, 1]); w_kv = dram_in(nc, "w_kv", [D, 256])
    gat = dram_in(nc, "gat", [128, KD])
    w_q = dram_in(nc, "w_q", [D, D]); w_o = dram_in(nc, "w_o", [D, D])
    qn2 = dram_in(nc, "qn2", [128, 1]); sinkrow = dram_in(nc, "sinkrow", [128, 16])
    biasT = dram_in(nc, "biasT", [128, 32, 128]); biasT0 = dram_in(nc, "biasT0", [128, 32, 128])
    outT = dram_out(nc, "outT", [D, ntok])
    h3d = nc.dram_tensor("h3_scr", [D, ntok], F32, kind="Internal").ap()
    KTd = nc.dram_tensor("KT_scr", [2, 128, 128 + ntok], BF16, kind="Internal").ap()
    Vd = nc.dram_tensor("V_scr", [128 + ntok, 128], BF16, kind="Internal").ap()
    h3b = Buf("h3d"); kvdb = Buf("kvd")
    nblk = ntok // 128 + 1
    with contextlib.ExitStack() as st:
        kb = KB(nc, st)
        R = FFNRes(kb, T)
        cbuf = Buf("c"); dsc = kb.dsem("ds_c")
        gv = [load_vec(kb, f"gv{i}", gF[i][:, :], [128, KD], cbuf, dsc) for i in range(4)]
        KVc = {"buf": cbuf}
        KVc["g"] = load_vec(kb, "kvg_sb", kvg[:, :], [128, KD], cbuf, dsc)
        KVc["kn2"] = load_vec(kb, "kn2_sb", kn2[:, :], [128, 1], cbuf, dsc)
        AC = {"buf": cbuf}
        AC["g"] = load_vec(kb, "gvat", gat[:, :], [128, KD], cbuf, dsc)
        AC["qn2"] = load_vec(kb, "qn2_sb", qn2[:, :], [128, 1], cbuf, dsc)
        AC["esink"] = load_vec(kb, "esink", sinkrow[:, :], [128, 16], cbuf, dsc)
        BD = make_BD(kb, "BD", cbuf)
        KVc["BD"] = BD; AC["BD"] = BD
        H = kb.sb("h", [128, KD, T], F32); Hb = Buf("h"); Hs = kb.dsem("ds_h")
        osem = kb.dsem("ds_o")
        FS = [FFNSet(kb, nc, f"f{i}", *wF[i]) for i in range(4)]
        ws = WScr(kb, nc)
        W8 = [kb.sb(f"w8_{i}", [128, 4096], BF16) for i in range(2)]
        FS[0].convert(kb)
        items = []
        for t in range(ntiles - 1):
            items += mamba_items_tile(W8, dr["w_in"], None, "state", ws)
        items += mamba_items_tile(W8, dr["w_in"], dr["w_out"], "full", ws) + kv_items(W8, w_kv, ws)
        for t in range(ntiles):
            items += mamba_items_tile(W8, dr["w_in"], dr["w_out"], "full", ws) + kv_items(W8, w_kv, ws)
        for t in range(ntiles):
            a, b = attn_items(W8, w_q, w_o, ws)
            items += a + b
        tw = [FS[0]] * ntiles + [FS[1]]
        for t in range(ntiles):
            tw += [FS[0], FS[1]]
        for t in range(ntiles):
            tw += [FS[2], FS[3]]
        r13, r2 = ffn_rings_bf16(kb, "a", tw)
        ring = Ring(kb, "w8", 2, items)
        xpv = xTp.rearrange("(k p) t -> p k t", p=128)
        xv = xT.rearrange("(k p) t -> p k t", p=128)
        h3v = h3d.rearrange("(k p) t -> p k t", p=128)
        yv = outT.rearrange("(k p) t -> p k t", p=128)
        with contextlib.ExitStack() as stM:
            kb.stack = stM
            P = MambaP(kb, dr, True)
            for g in range(NG):
                kb.op("pool", lambda e, g=g: e.memset(P.S[:, g * 512:(g + 1) * 512], 0.0), writes=[P.Sbuf[g]])
                kb.op("pool", lambda e, g=g: e.memset(P.Sb[:, g * 512:(g + 1) * 512], 0.0), writes=[P.Sbb[g]])
            def load_prev(t):
                kb.dma("sp", H[:], xpv[:, :, t * T:(t + 1) * T], Hs, writes=[Hb])

            def load_own(t):
                kb.dma("sp", H[:], xv[:, :, t * T:(t + 1) * T], Hs, writes=[Hb])
            bg = FS[1].steps(kb) + list(ws.pending) + FS[2].steps(kb) + FS[3].steps(kb)
            ws.pending = []
            nbg = max(6, -(-len(bg) // max(ntiles - 2, 1)))

            def trickle(n=None):
                for _ in range(nbg if n is None else n):
                    if bg:
                        bg.pop(0)()
            load_prev(0)
            for t in range(ntiles):
                if t >= 1 or ntiles == 1:
                    trickle()
                ffn(kb, R, H, Hb, gv[0], cbuf, r13, r2, T)
                if t < ntiles - 1:
                    mamba_tile(kb, P, R, H, Hb, T, "state", ring, W8, first=(t == 0), after_norm=lambda t=t: load_prev(t + 1))
                else:
                    for g in range(NG):
                        kb.op("act", lambda e, g=g: e.activation(out=P.Sb[:, g * 512:(g + 1) * 512], in_=P.S[:, g * 512:(g + 1) * 512], func=AF.Copy),
                              reads=[P.Sbuf[g]], writes=[P.Sbb[g]])
                    mamba_tile(kb, P, R, H, Hb, T, "full", ring, W8, first=(t == 0))
                    assert len(bg) <= len(FS[2].steps(kb)) + len(FS[3].steps(kb)) + 8, "FS[1] must be converted before its first use"
                    ffn(kb, R, H, Hb, gv[1], cbuf, r13, r2, T)
                    kv_tile(kb, R, KVc, H, Hb, ring, W8, KTd, Vd, -1, osem, kvdb, after_norm=lambda: load_own(0))
            for g in range(NG):
                gs = slice(g * 512, (g + 1) * 512)
                kb.op("dve", lambda e, gs=gs: e.tensor_scalar(out=P.S[:, gs], in0=P.S[:, gs], scalar1=P.isec[:, 0:1], scalar2=None, op0=ALU.mult),
                      reads=[P.Sbuf[g], P.cb_], writes=[P.Sbuf[g]])
                kb.op("dve", lambda e, gs=gs: e.tensor_scalar(out=P.Sb[:, gs], in0=P.S[:, gs], scalar1=1.0, scalar2=None, op0=ALU.mult),
                      reads=[P.Sbuf[g]], writes=[P.Sbb[g]])
            kb.op("dve", lambda e: e.tensor_scalar(out=P.HAL[:], in0=P.HAL[:], scalar1=P.isec[:, 0:1], scalar2=None, op0=ALU.mult),
                  reads=[P.halb, P.cb_], writes=[P.halb])
            for t in range(ntiles):
                trickle(len(bg) if t == ntiles - 1 else None)
                ffn(kb, R, H, Hb, gv[0], cbuf, r13, r2, T)
                mamba_tile(kb, P, R, H, Hb, T, "full", ring, W8)
                ffn(kb, R, H, Hb, gv[1], cbuf, r13, r2, T)
                kb.dma("sp", h3v[:, :, t * T:(t + 1) * T], H[:], Hs, reads=[Hb], writes=[h3b], part=(t > 0))
                kv_tile(kb, R, KVc, H, Hb, ring, W8, KTd, Vd, t, osem, kvdb,
                        after_norm=(lambda t=t: load_own(t + 1)) if t + 1 < ntiles else None)
            kb.barrier()
            kb.stack = st
        AC["kvbuf"] = Buf("kv")
        AC["KT"] = kb.sb("KTs", [128, 2, 128 + ntok], BF16)
        AC["V"] = kb.sb("Vs", [128, nblk, 128], BF16)
        dkv = kb.dsem("ds_kv")
        for kvh in range(2):
            kb.dma("sp", AC["KT"][:, kvh, :], KTd[kvh, :, :], dkv, reads=[kvdb], writes=[AC["kvbuf"]], part=(kvh > 0))
        kb.dma("sp", AC["V"][:], Vd.rearrange("(b p) c -> p b c", p=128), dkv, reads=[kvdb], writes=[AC["kvbuf"]], part=True)
        AC["EB"] = kb.sb("EB", [128, 32, 128], BF16)
        AC["EB0"] = kb.sb("EB0", [128, 32, 128], BF16)
        negc = kb.sb("negc", [128, 1], F32)
        kb.op("pool", lambda e: e.memset(negc[:], -CSH), reads=[cbuf], writes=[cbuf])
        with contextlib.ExitStack() as st2:
            stg = st2.enter_context(nc.sbuf_tensor("bias_stage", [128, 32, 128], F32))
            stb = Buf("stage"); dst_ = kb.dsem("ds_stage")
            for src, dstt in ((biasT, AC["EB"]), (biasT0, AC["EB0"])):
                kb.dma("sp", stg[:], src[:, :, :], dst_, reads=[], writes=[stb])
                kb.op("act", lambda e, dstt=dstt: e.activation(out=dstt[:], in_=stg[:], func=AF.Exp, bias=negc[:], scale=1.0),
                      reads=[stb, cbuf], writes=[cbuf])
            kb.barrier()
        kb.op("act", lambda e: e.activation(out=AC["esink"][:], in_=AC["esink"][:], func=AF.Exp, bias=negc[:], scale=1.0),
              reads=[cbuf], writes=[cbuf])
        AC["onec"] = kb.sb("onec", [128, 1], BF16)
        kb.op("pool", lambda e: e.memset(AC["onec"][:], 1.0), reads=[cbuf], writes=[cbuf])
        identf = kb.sb("identf", [128, 128], F32)
        AC["ident"] = kb.sb("identb", [128, 128], BF16)
        kb.op("pool", lambda e: e.memset(identf[:], 1.0), reads=[cbuf], writes=[cbuf])
        kb.op("pool", lambda e: e.affine_select(out=identf[:], in_=identf[:], pattern=[[-1, 128]],
                                                 compare_op=ALU.is_equal, fill=0.0, base=0, channel_multiplier=1),
              reads=[cbuf], writes=[cbuf])
        kb.op("pool", lambda e: e.tensor_copy(out=AC["ident"][:], in_=identf[:]), reads=[cbuf], writes=[cbuf])
        H2 = kb.sb("h2", [128, KD, T], F32); H2b = Buf("h2"); Hs2 = kb.dsem("ds_h2")
        HH = [(H, Hb, Hs), (H2, H2b, Hs2)]

        def load_l1(t):
            h_, hb_, hs_ = HH[t % 2]
            kb.dma("sp", h_[:], h3v[:, :, t * T:(t + 1) * T], hs_, reads=[h3b], writes=[hb_])
        load_l1(0)
        for t in range(ntiles):
            h_, hb_, hs_ = HH[t % 2]
            if t + 1 < ntiles:
                load_l1(t + 1)
            ffn(kb, R, h_, hb_, gv[2], cbuf, r13, r2, T)
            attn_tile(kb, R, AC, h_, hb_, ring, W8, t)
            ffn(kb, R, h_, hb_, gv[3], cbuf, r13, r2, T)
            kb.dma("sp", yv[:, :, t * T:(t + 1) * T], h_[:], hs_, reads=[hb_])
        kb.barrier()
        print("F: waits", kb.nwait, kb.ecnt)
    return nc

import numpy as np
def chunkvec(v, nch):
    return np.ascontiguousarray(v.reshape(nch, 128).T).astype(np.float32)
def prep_common(I):
    c = {}
    c["ssm_g"] = chunkvec(I["ssm_norm"][0], 8)
    c["cw"] = np.ascontiguousarray(I["ssm_conv_w"][0].reshape(4, 24, 128).transpose(2, 1, 0)).astype(np.float32)
    c["cb"] = chunkvec(I["ssm_conv_b"][0], 24)
    c["dtb"] = I["ssm_dt_bias"][0].reshape(32, 1).astype(np.float32)
    c["alog"] = I["ssm_a_log"][0].reshape(32, 1).astype(np.float32)
    c["w_in"] = np.ascontiguousarray(I["ssm_w_in"][0])
    c["dch"] = chunkvec(np.repeat(I["ssm_d"][0], 64), 16)
    c["gn"] = chunkvec(I["ssm_gate_norm"][0], 16)
    c["w_out"] = np.ascontiguousarray(I["ssm_w_out"][0])
    return c
import math
def t5_bucket_np(dist):
    n = np.maximum(dist, 0); me = 16
    nf = np.maximum(n, 1).astype(np.float32)
    large = me + (np.log(nf / me) / math.log(128 / me) * (32 - me)).astype(np.int32)
    large = np.minimum(large, 31)
    return np.where(n < me, n, large)
def prep_bias(rel_bias, first_valid):
    qi = np.arange(128)[:, None] + 128; kj = np.arange(256)[None, :]
    dist = qi - kj
    valid = (dist >= 0) & (dist < 128)
    bias = rel_bias[t5_bucket_np(dist)]
    out = np.empty((128, 32, 128), np.float32)
    for blk in range(2):
        for kvh in range(2):
            for par in range(2):
                for jj in range(4):
                    h = kvh * 8 + 2 * jj + par
                    idx = ((blk * 2 + kvh) * 2 + par) * 4 + jj
                    b = bias[:, blk * 128:(blk + 1) * 128, h]
                    v = valid[:, blk * 128:(blk + 1) * 128]
                    if blk == 0 and not first_valid:
                        v = np.zeros_like(v)
                    out[:, idx, :] = np.where(v, b, np.float32(-30000.0)).T
    return out


_NC_CACHE = {}


def kernel(**I):
    I = {k: np.asarray(v) for k, v in I.items()}
    c = prep_common(I)
    x = I["x"].astype(np.float32)
    n = 8
    cores = list(range(n))
    f32c = lambda a: np.ascontiguousarray(a, dtype=np.float32)
    if "F" not in _NC_CACHE:
        _NC_CACHE["F"] = build_fused()
    rb = I["rel_bias"].astype(np.float32)
    bias1 = prep_bias(rb, True)
    bias0 = prep_bias(rb, False)
    shared = {"kvg": chunkvec(I["kv_norm"], 8), "kn2": np.tile(I["k_norm"], 2).reshape(128, 1).astype(np.float32),
              "w_kv": f32c(I["w_kv"]), "gat": chunkvec(I["attn_norm"][0], 8), "w_q": f32c(I["w_q"][0]), "w_o": f32c(I["w_o"][0]),
              "qn2": np.tile(I["q_norm"][0], 2).reshape(128, 1).astype(np.float32),
              "sinkrow": np.ascontiguousarray(np.broadcast_to(I["sinks"][0][None, :], (128, 16))).astype(np.float32),
              "biasT": bias1}
    for i, (l, w) in enumerate([(0, 0), (0, 1), (1, 0), (1, 1)]):
        shared[f"gF{i}"] = chunkvec(I["ffn_norm"][l, w], 8)
        shared[f"wF{i}_0"] = f32c(I["ffn_w1"][l, w])
        shared[f"wF{i}_1"] = f32c(I["ffn_w3"][l, w])
        shared[f"wF{i}_2"] = f32c(I["ffn_w2"][l, w])
    for k in ["ssm_g", "cw", "cb", "dtb", "alog", "w_in", "dch", "gn", "w_out"]:
        shared[k] = c[k]
    maps = []
    for core in cores:
        b, hf = core // 2, core % 2
        m = dict(shared)
        m["xTp"] = f32c(x[b, 0:NTOK].T) if hf else np.zeros((D, NTOK), np.float32)
        m["xT"] = f32c(x[b, hf * NTOK:(hf + 1) * NTOK].T)
        m["biasT0"] = bias1 if hf else bias0
        m["isec"] = np.full((128, 1), float(hf), np.float32)
        maps.append(m)
    res = run_bass_kernel_spmd(_NC_CACHE["F"], maps, core_ids=cores).results
    out = np.empty((4, 2 * NTOK, D), np.float32)
    for core in cores:
        b, hf = core // 2, core % 2
        out[b, hf * NTOK:(hf + 1) * NTOK] = np.asarray(res[core]["outT"]).T
    return out
```

```python
import contextlib
import math
import ml_dtypes
from concourse.bass_utils import run_bass_kernel_spmd
import numpy as np
import concourse.bass as bass
import concourse.mybir as mybir

F32 = mybir.dt.float32
BF16 = mybir.dt.bfloat16
AF = mybir.ActivationFunctionType
ALU = mybir.AluOpType


_UNIQ = [0]


def uniq(n):
    _UNIQ[0] += 1
    return f"{n}_{_UNIQ[0]}"


class Buf:
    __slots__ = ("name", "w", "r")

    def __init__(self, name):
        self.name = name
        self.w = {}
        self.r = {}


class DSem:
    def __init__(self, sem, key):
        self.sem = sem
        self.key = key
        self.cnt = 0


class KB:
    SAME_ENG_SYNC = True
    NOSYNC_SAME = ('pe',)

    def __init__(self, nc, stack):
        self.nc = nc
        self.stack = stack
        self.root = stack
        self.eng = {"pe": nc.tensor, "act": nc.scalar, "dve": nc.vector,
                    "pool": nc.gpsimd, "sp": nc.sync}
        self.esem = {}
        self.ecnt = {}
        self.seen = {}
        for e in self.eng:
            self.esem[e] = stack.enter_context(nc.semaphore("es_" + e))
            self.ecnt[e] = 0
            self.seen[e] = {}
        self.dsems = []
        self.nwait = 0

    def sb(self, name, shape, dt):
        return self.stack.enter_context(self.nc.sbuf_tensor(name, list(shape), dt))

    def ps(self, name, shape, dt):
        return self.stack.enter_context(self.nc.psum_tensor(name, list(shape), dt))

    def dsem(self, name):
        s = self.root.enter_context(self.nc.semaphore(name))
        d = DSem(s, name)
        self.dsems.append(d)
        return d

    def _waits(self, e, toks):
        need = {}
        for (key, sem, val, src) in toks:
            if src == e and (e in self.NOSYNC_SAME or not self.SAME_ENG_SYNC):
                continue
            if key not in need or need[key][1] < val:
                need[key] = (sem, val, src)
        for key, (sem, val, src) in need.items():
            if self.seen[e].get(key, 0) >= val:
                continue
            if src is not None and val > self.ecnt[src]:
                raise RuntimeError(f"unresolved lazy token {key}={val} (cnt {self.ecnt[src]}) waited by {e}")
            self.eng[e].wait_ge(sem, val)
            self.nwait += 1
            self.seen[e][key] = val

    @staticmethod
    def _gather(reads, writes):
        toks = []
        for b in reads:
            toks += [(k,) + v for k, v in b.w.items()]
        for b in writes:
            toks += [(k,) + v for k, v in b.w.items()]
            toks += [(k,) + v for k, v in b.r.items()]
        return toks

    @staticmethod
    def _update(tok, key, reads, writes, part=False):
        for b in writes:
            if part:
                b.w[key] = tok
            else:
                b.w = {key: tok}
                b.r = {}
        for b in reads:
            if b in writes:
                continue
            old = b.r.get(key)
            if old is None or old[1] < tok[1]:
                b.r[key] = tok

    def op(self, e, fn, reads=(), writes=(), inc=True):
        self._waits(e, self._gather(reads, writes))
        ins = fn(self.eng[e])
        key = "es_" + e
        if inc:
            self.ecnt[e] += 1
            ins.then_inc(self.esem[e], 1)
            tok = (self.esem[e], self.ecnt[e], e)
        else:
            tok = (self.esem[e], self.ecnt[e] + 1, e)
        self._update(tok, key, reads, writes)
        return ins

    def dma(self, q, out, in_, ds, reads=(), writes=(), part=False, **kw):
        self._waits(q, self._gather(reads, writes))
        ins = self.eng[q].dma_start(out=out, in_=in_, **kw)
        ds.cnt += 16
        ins.then_inc(ds.sem, 16)
        tok = (ds.sem, ds.cnt, None)
        self._update(tok, ds.key, reads, writes, part=part)
        return ins

    def barrier(self):
        for e in self.eng:
            toks = []
            for x in self.eng:
                if x != e and self.ecnt[x] > 0:
                    toks.append(("es_" + x, self.esem[x], self.ecnt[x], x))
            for d in self.dsems:
                if d.cnt > 0:
                    toks.append((d.key, d.sem, d.cnt, None))
            self._waits(e, toks)


class Ring:
    def __init__(self, kb, name, depth, items):
        self.kb = kb
        self.depth = depth
        self.items = items
        self.bufs = [Buf(f"{name}{i}") for i in range(depth)]
        self.sems = [kb.dsem(f"ds_{name}{i}") for i in range(depth)]
        self.issued = 0
        self.taken = 0

    def _issue_upto(self, n):
        while self.issued < min(n, len(self.items)):
            i = self.issued
            s = i % self.depth
            self.items[i](self.kb, s, self.bufs[s], self.sems[s])
            self.issued += 1

    def get(self):
        i = self.taken
        self._issue_upto(i + 1)
        self.taken += 1
        s = i % self.depth
        return s, self.bufs[s]

    def prefetch(self):
        self._issue_upto(self.taken + self.depth - 1)


D = 1024
KD = 8
DFF = 2816
KF = 22
EPS = 1e-6


def ffn_rings(kb, name, tiles_w):
    W13 = [kb.sb(f"{name}w13_{i}", [128, 2, KD, 256], BF16) for i in range(3)]
    W2 = [kb.sb(f"{name}w2_{i}", [128, KF, 256], BF16) for i in range(2)]
    it13, it2 = [], []
    for (w1, w3, w2) in tiles_w:
        w1v = w1.rearrange("(k p) n -> p k n", p=128)
        w3v = w3.rearrange("(k p) n -> p k n", p=128)
        w2v = w2.rearrange("(j p) n -> p j n", p=128)
        for jp in range(KF // 2):
            def f(kb, s, buf, ds, jp=jp, w1v=w1v, w3v=w3v):
                kb.dma("pool", W13[s][:, 0], w1v[:, :, jp * 256:(jp + 1) * 256], ds, writes=[buf])
                kb.dma("pool", W13[s][:, 1], w3v[:, :, jp * 256:(jp + 1) * 256], ds, writes=[buf], part=True)
            it13.append(f)
        for mb in range(4):
            def f2(kb, s, buf, ds, mb=mb, w2v=w2v):
                kb.dma("pool", W2[s][:], w2v[:, :, mb * 256:(mb + 1) * 256], ds, writes=[buf])
            it2.append(f2)
    r13 = Ring(kb, name + "r13", 3, it13)
    r2 = Ring(kb, name + "r2", 2, it2)
    r13.tiles = W13
    r2.tiles = W2
    return r13, r2


class FFNRes:
    def __init__(self, kb, T, name="f"):
        self.T = T
        self.sq = kb.sb(name + "sq", [128, KD, T], BF16)
        self.sqb = [Buf("sq0"), Buf("sq1")]
        self.uT = kb.sb(name + "uT", [128, KD, T], BF16)
        self.ub = [Buf(f"uT{k}") for k in range(KD)]
        self.rstd = kb.sb(name + "rstd", [128, T], F32)
        self.rb = Buf("rstd")
        self.ones = kb.sb(name + "ones", [128, 128], BF16)
        self.onesb = Buf("ones")
        kb.op("pool", lambda e: e.memset(self.ones[:], 1.0), writes=[self.onesb])
        self.epsb = kb.sb(name + "eps", [128, 1], F32)
        self.epsbuf = Buf("eps")
        kb.op("pool", lambda e: e.memset(self.epsb[:], EPS), writes=[self.epsbuf])


def rmsnorm_T(kb, R, hT, hb, gvec, gb, PN, PNb, T):
    for hf in range(2):
        kb.op("act", lambda e, hf=hf: e.activation(out=R.sq[:, hf * 4:hf * 4 + 4, :T], in_=hT[:, hf * 4:hf * 4 + 4, :T], func=AF.Square),
              reads=[hb], writes=[R.sqb[hf]])
    for k in range(KD):
        kb.op("pe", lambda e, k=k: e.matmul(PN[:, :T], R.ones[:], R.sq[:, k, :T], start=(k == 0), stop=(k == KD - 1)),
              reads=[R.sqb[k // 4], R.onesb], writes=[PNb], inc=(k == KD - 1))
    kb.op("act", lambda e: e.activation(out=R.rstd[:, :T], in_=PN[:, :T], func=AF.Sqrt, bias=R.epsb[:], scale=1.0 / D),
          reads=[PNb, R.epsbuf], writes=[R.rb])
    kb.op("dve", lambda e: e.reciprocal(PN[:, :T], R.rstd[:, :T]), reads=[R.rb, PNb], writes=[PNb])
    for k in range(KD):
        kb.op("dve", lambda e, k=k: e.scalar_tensor_tensor(out=R.uT[:, k, :T], in0=hT[:, k, :T], scalar=gvec[:, k:k + 1],
                                                            in1=PN[:, :T], op0=ALU.mult, op1=ALU.mult),
              reads=[hb, gb, PNb], writes=[R.ub[k]])


def ffn(kb, R, hT, hb, gvec, gb, r13, r2, T):
    with contextlib.ExitStack() as st:
        _ffn(kb, st, R, hT, hb, gvec, gb, r13, r2, T)
        kb.barrier()


def _ffn(kb, st, R, hT, hb, gvec, gb, r13, r2, T):
    sb = lambda n, s, d: st.enter_context(kb.nc.sbuf_tensor(uniq(n), list(s), d))
    ps = lambda n, s, d: st.enter_context(kb.nc.psum_tensor(uniq(n), list(s), d))
    PS = {"n": ps("pn", [128, 512], F32), "nb": Buf("pn"),
          "h1": [ps(f"ph1{i}", [128, 512], F32) for i in range(2)], "h1b": [Buf(f"ph1{i}") for i in range(2)],
          "h3": [ps(f"ph3{i}", [128, 512], F32) for i in range(2)], "h3b": [Buf(f"ph3{i}") for i in range(2)],
          "o": [ps(f"po{i}", [128, 512], F32) for i in range(2)], "ob": [Buf(f"po{i}") for i in range(2)]}
    gT = sb("f_gT", [128, KF, T], BF16)
    gbufs = [Buf(f"g{j}") for j in range(KF)]
    s1 = [sb(f"f_s1_{i}", [128, T], F32) for i in range(2)]
    s1b = [Buf(f"s1_{i}") for i in range(2)]
    rmsnorm_T(kb, R, hT, hb, gvec, gb, PS["n"], PS["nb"], T)
    for jp in range(KF // 2):
        s, wb = r13.get()
        W = r13.tiles[s]
        for jj in range(2):
            j = jp * 2 + jj
            pi = j % 2
            P1, P1b = PS["h1"][pi], PS["h1b"][pi]
            P3, P3b = PS["h3"][pi], PS["h3b"][pi]
            for k in range(KD):
                kb.op("pe", lambda e, k=k: e.matmul(P1[:, :T], W[:, 0, k, jj * 128:(jj + 1) * 128], R.uT[:, k, :T],
                                                     start=(k == 0), stop=(k == KD - 1)),
                      reads=[wb, R.ub[k]], writes=[P1b], inc=(k == KD - 1))
            for k in range(KD):
                kb.op("pe", lambda e, k=k: e.matmul(P3[:, :T], W[:, 1, k, jj * 128:(jj + 1) * 128], R.uT[:, k, :T],
                                                     start=(k == 0), stop=(k == KD - 1)),
                      reads=[wb, R.ub[k]], writes=[P3b], inc=(k == KD - 1))
            kb.op("act", lambda e: e.activation(out=s1[pi][:, :T], in_=P1[:, :T], func=AF.Silu),
                  reads=[P1b], writes=[s1b[pi]])
            kb.op("dve", lambda e: e.tensor_tensor(out=gT[:, j, :T], in0=P3[:, :T], in1=s1[pi][:, :T], op=ALU.mult),
                  reads=[P3b, s1b[pi]], writes=[gbufs[j]])
        r13.prefetch()
    for m in range(KD):
        s, wb = r2.get()
        W = r2.tiles[s]
        pi = m % 2
        PO, POb = PS["o"][pi], PS["ob"][pi]
        for j in range(KF):
            kb.op("pe", lambda e, j=j: e.matmul(PO[:, :T], W[:, j, :], gT[:, j, :T],
                                                 start=(j == 0), stop=(j == KF - 1)),
                  reads=[wb, gbufs[j]], writes=[POb], inc=(j == KF - 1))
        kb.op("dve", lambda e, m=m: e.scalar_tensor_tensor(out=hT[:, m, :T], in0=PO[:, :T], scalar=0.5, in1=hT[:, m, :T],
                                                            op0=ALU.mult, op1=ALU.add),
              reads=[POb, hb], writes=[hb])
        r2.prefetch()


class FFNSet:
    def __init__(self, kb, nc, name, w1, w3, w2):
        self.s13 = nc.dram_tensor(name + "_s13", [KF // 2, 128, 2, KD, 256], BF16, kind="Internal").ap()
        self.s2 = nc.dram_tensor(name + "_s2", [8, 128, KF, 128], BF16, kind="Internal").ap()
        self.bufs = [Buf(f"{name}_scr{i}") for i in range(3)]
        self.dss = [kb.dsem(f"ds_cv_{name}_{i}") for i in range(3)]
        self.w = (w1, w3, w2)

    def grp13(self, jp):
        return 0 if jp < 3 else 1

    def steps(self, kb):
        w1, w3, w2 = self.w
        w1v = w1.rearrange("(k p) n -> p k n", p=128)
        w3v = w3.rearrange("(k p) n -> p k n", p=128)
        w2v = w2.rearrange("(j p) n -> p j n", p=128)
        out = []
        for jp in range(KF // 2):
            gi = self.grp13(jp)
            out.append(lambda jp=jp, gi=gi: kb.dma("pool", self.s13[jp, :, 0, :, :], w1v[:, :, jp * 256:(jp + 1) * 256], self.dss[gi], writes=[self.bufs[gi]], part=True))
            out.append(lambda jp=jp, gi=gi: kb.dma("pool", self.s13[jp, :, 1, :, :], w3v[:, :, jp * 256:(jp + 1) * 256], self.dss[gi], writes=[self.bufs[gi]], part=True))
        for mb in range(8):
            out.append(lambda mb=mb: kb.dma("pool", self.s2[mb, :, :, :], w2v[:, :, mb * 128:(mb + 1) * 128], self.dss[2], writes=[self.bufs[2]], part=True))
        return out

    def convert(self, kb):
        for s in self.steps(kb):
            s()


def ffn_rings_bf16(kb, name, passes, q13="sp", q2="sp"):
    W13 = [kb.sb(f"{name}w13_{i}", [128, 2, KD, 256], BF16) for i in range(3)]
    W2 = [kb.sb(f"{name}w2_{i}", [128, KF, 128], BF16) for i in range(2)]
    it13, it2 = [], []
    for fs in passes:
        for jp in range(KF // 2):
            def f(kb, s, buf, ds, jp=jp, fs=fs):
                b_ = fs.bufs[fs.grp13(jp)]
                assert b_.w, "FFN weight block used before its conversion was emitted"
                kb.dma(q13, W13[s][:], fs.s13[jp], ds, reads=[b_], writes=[buf])
            it13.append(f)
        for mb in range(8):
            def f2(kb, s, buf, ds, mb=mb, fs=fs):
                assert fs.bufs[2].w, "FFN weight block used before its conversion was emitted"
                kb.dma(q2, W2[s][:], fs.s2[mb], ds, reads=[fs.bufs[2]], writes=[buf])
            it2.append(f2)
    r13 = Ring(kb, name + "r13", 3, it13)
    r2 = Ring(kb, name + "r2", 2, it2)
    r13.tiles = W13
    r2.tiles = W2
    return r13, r2


NH = 32
HP = 64
NG = 4
DI = 2048
C_Z, C_X, C_B, C_C, C_DT = 0, 2048, 4096, 4608, 5120


class WScr:
    def __init__(self, kb, nc):
        self.kb, self.nc = kb, nc
        self.blocks = {}
        self.pending = []

    def item(self, W8, key, parts, nelem, k, defer=False):
        kb = self.kb
        if key not in self.blocks:
            ap = self.nc.dram_tensor("wscr_" + key, [128, 4096], BF16, kind="Internal").ap()
            buf = Buf("wscr_" + key)
            ds = kb.dsem("ds_ws_" + key)

            def conv():
                first = True
                for (off, n, src) in parts:
                    dst = ap[:, 0:nelem].rearrange("p (k n) -> p k n", k=k)[:, :, off:off + n]
                    kb.dma("pool", dst, src, ds, writes=[buf], part=not first)
                    first = False
            self.blocks[key] = (ap, buf)
            if defer:
                self.pending.append(conv)
            else:
                conv()
        ap, sbuf = self.blocks[key]

        def f(kb, s, buf, ds):
            assert sbuf.w, "weight block used before its conversion was emitted: " + key
            kb.dma("pool", W8[s][:, 0:nelem], ap[:, 0:nelem], ds, reads=[sbuf], writes=[buf])
        return f

    def flush(self):
        while self.pending:
            self.pending.pop(0)()


def mamba_items_tile(W8, w_in, w_out, mode, ws):
    wv = w_in.rearrange("(k p) n -> p k n", p=128)
    items = []
    full = mode == "full"
    def xi(g):
        return ws.item(W8, f"x{g}", [(0, 512, wv[:, :, C_X + g * 512:C_X + (g + 1) * 512])], 8 * 512, 8)

    def bci(g):
        return ws.item(W8, f"bc{g}", [(0, 128, wv[:, :, C_B + g * 128:C_B + (g + 1) * 128]),
                                      (128, 128, wv[:, :, C_C + g * 128:C_C + (g + 1) * 128])], 8 * 256, 8)

    def zi(g):
        return ws.item(W8, f"z{g}", [(0, 512, wv[:, :, C_Z + g * 512:C_Z + (g + 1) * 512])], 8 * 512, 8)
    items += [xi(0), bci(0), ws.item(W8, "dt", [(0, 32, wv[:, :, C_DT:C_DT + 32])], 8 * 32, 8)]
    for g in range(NG):
        if g + 1 < NG:
            items += [xi(g + 1), bci(g + 1)]
        if full:
            items.append(zi(g))
    if full:
        wo = w_out.rearrange("(c p) n -> p c n", p=128)
        for mb in range(4):
            items.append(ws.item(W8, f"o{mb}", [(0, 256, wo[:, :, mb * 256:(mb + 1) * 256])], 16 * 256, 16))
    return items


class MambaP:
    def __init__(self, kb, dr, full):
        self.full = full
        sb = kb.sb
        self.cb_ = Buf("mconst")
        cbuf = self.cb_
        dsc = kb.dsem("ds_mc")
        self.g = sb("m_g", [128, KD], F32)
        self.cw = sb("m_cw", [128, 24, 4], F32)
        self.cb = sb("m_cb", [128, 24], F32)
        self.dtb = sb("m_dtb", [32, 1], F32)
        self.a = sb("m_a", [32, 1], F32)
        self.isec = sb("m_isec", [128, 1], F32)
        kb.dma("sp", self.g[:], dr["ssm_g"][:, :], dsc, writes=[cbuf])
        kb.dma("sp", self.cw[:], dr["cw"][:, :, :], dsc, writes=[cbuf], part=True)
        kb.dma("sp", self.cb[:], dr["cb"][:, :], dsc, writes=[cbuf], part=True)
        kb.dma("sp", self.dtb[:], dr["dtb"][:, :], dsc, writes=[cbuf], part=True)
        kb.dma("sp", self.a[:], dr["alog"][:, :], dsc, writes=[cbuf], part=True)
        kb.dma("sp", self.isec[:], dr["isec"][:, :], dsc, writes=[cbuf], part=True)
        if full:
            self.dch = sb("m_dch", [128, 16], F32)
            self.gn = sb("m_gn", [128, 16], F32)
            kb.dma("sp", self.dch[:], dr["dch"][:, :], dsc, writes=[cbuf], part=True)
            kb.dma("sp", self.gn[:], dr["gn"][:, :], dsc, writes=[cbuf], part=True)
        kb.op("act", lambda e: e.activation(out=self.a[:], in_=self.a[:], func=AF.Exp), reads=[cbuf], writes=[cbuf])
        kb.op("dve", lambda e: e.tensor_scalar(out=self.a[:], in0=self.a[:], scalar1=-1.0, scalar2=None, op0=ALU.mult),
              reads=[cbuf], writes=[cbuf])
        self.identf = sb("m_idf", [128, 128], F32)
        self.ident = sb("m_idb", [128, 128], BF16)
        self.tri = sb("m_tri", [128, 128], F32)
        self.sel = sb("m_sel", [128, 128], F32)
        self.kb_ = Buf("mk")
        k_ = self.kb_
        kb.op("pool", lambda e: e.memset(self.identf[:], 1.0), writes=[k_])
        kb.op("pool", lambda e: e.affine_select(out=self.identf[:], in_=self.identf[:], pattern=[[-1, 128]],
                                                 compare_op=ALU.is_equal, fill=0.0, base=0, channel_multiplier=1),
              reads=[k_], writes=[k_])
        kb.op("pool", lambda e: e.tensor_copy(out=self.ident[:], in_=self.identf[:]), reads=[k_], writes=[k_])
        kb.op("pool", lambda e: e.memset(self.tri[:], 1.0), reads=[k_], writes=[k_])
        kb.op("pool", lambda e: e.affine_select(out=self.tri[:], in_=self.tri[:], pattern=[[1, 128]],
                                                 compare_op=ALU.is_ge, fill=0.0, base=0, channel_multiplier=-1),
              reads=[k_], writes=[k_])
        kb.op("pool", lambda e: e.memset(self.sel[:], 1.0), reads=[k_], writes=[k_])
        kb.op("pool", lambda e: e.affine_select(out=self.sel[:], in_=self.sel[:], pattern=[[0, 128]],
                                                 compare_op=ALU.is_equal, fill=0.0, base=-127, channel_multiplier=1),
              reads=[k_], writes=[k_])
        self.DG = sb("m_DG", [128, 6 * 4, 128], BF16)
        self.DGb = Buf("DG")
        self.S = sb("m_S", [128, DI], F32)
        self.Sb = sb("m_Sb", [128, DI], BF16)
        self.Sbuf = [Buf(f"S{g}") for g in range(NG)]
        self.Sbb = [Buf(f"Sb{g}") for g in range(NG)]
        self.HAL = sb("m_hal", [128, 24, 4], BF16)
        self.halb = Buf("hal")
        self.acd = dr.get("acum_scr")
        self.acdb = Buf("acd")
        self.acds = kb.dsem("ds_acd")
        self.abcs = [kb.dsem(f"ds_abc{i}") for i in range(2)]


def mamba_tile(kb, P, R, hT, hb, T, mode, ring, W8, first=False, after_norm=None):
    full = mode == "full"
    NB = max(T // 128, 1)
    with contextlib.ExitStack() as st:
        sb = lambda n, s, d: st.enter_context(kb.nc.sbuf_tensor(uniq(n), list(s), d))
        ps = lambda n, s, d: st.enter_context(kb.nc.psum_tensor(uniq(n), list(s), d))
        PA = [ps(f"mpa{i}", [128, 512], F32) for i in range(2)]
        PAb = [Buf(f"mpa{i}") for i in range(2)]
        pai = [0]

        def nextpa():
            i = pai[0] % 2
            pai[0] += 1
            return PA[i], PAb[i]

        PT = ps("mpt", [128, 512], F32)
        PTb = Buf("mpt")
        rmsnorm_T(kb, R, hT, hb, P.g, P.cb_, PT, PTb, T)
        uT, ub = R.uT, R.ub
        if after_norm is not None:
            after_norm()

        XB = sb("m_XB", [128, 6, 4 + 512], BF16)
        XBb = Buf("XB")
        XC = sb("m_XC", [128, 6, 512], BF16)
        XCb = Buf("XC")

        def stageA(g):
            chunks = [g * 4 + i for i in range(4)] + [16 + g, 20 + g]
            if first:
                kb.op("dve", lambda e: e.memset(XB[:, :, 0:4], 0.0), writes=[XBb])
            else:
                kb.op("dve", lambda e: e.tensor_copy(out=XB[:, 0:4, 0:4], in_=P.HAL[:, g * 4:g * 4 + 4, :]), reads=[P.halb], writes=[XBb])
                kb.op("dve", lambda e: e.tensor_copy(out=XB[:, 4, 0:4], in_=P.HAL[:, 16 + g, :]), reads=[P.halb], writes=[XBb])
                kb.op("dve", lambda e: e.tensor_copy(out=XB[:, 5, 0:4], in_=P.HAL[:, 20 + g, :]), reads=[P.halb], writes=[XBb])
            s, wb = ring.get()
            Wx = W8[s][:, 0:8 * 512].rearrange("p (k n) -> p k n", k=8)
            for i in range(4):
                Pq, Pqb = nextpa()
                for k in range(KD):
                    kb.op("pe", lambda e, k=k, i=i: e.matmul(Pq[:, :T], Wx[:, k, i * 128:(i + 1) * 128], uT[:, k, :T],
                                                              start=(k == 0), stop=(k == KD - 1)),
                          reads=[wb, ub[k]], writes=[Pqb], inc=(k == KD - 1))
                kb.op("act", lambda e, i=i: e.activation(out=XB[:, i, 4:4 + T], in_=Pq[:, :T], func=AF.Copy),
                      reads=[Pqb], writes=[XBb])
            ring.prefetch()
            s, wb = ring.get()
            Wbc = W8[s][:, 0:8 * 256].rearrange("p (k n) -> p k n", k=8)
            for i in range(2):
                Pq, Pqb = nextpa()
                for k in range(KD):
                    kb.op("pe", lambda e, k=k, i=i: e.matmul(Pq[:, :T], Wbc[:, k, i * 128:(i + 1) * 128], uT[:, k, :T],
                                                              start=(k == 0), stop=(k == KD - 1)),
                          reads=[wb, ub[k]], writes=[Pqb], inc=(k == KD - 1))
                kb.op("act", lambda e, i=i: e.activation(out=XB[:, 4 + i, 4:4 + T], in_=Pq[:, :T], func=AF.Copy),
                      reads=[Pqb], writes=[XBb])
            ring.prefetch()
            for (d0, nd, c0) in ((0, 16, g * 4), (16, 4, 16 + g), (20, 4, 20 + g)):
                nch = nd // 4
                kb.op("pool", lambda e, d0=d0, nd=nd, c0=c0, nch=nch: e.tensor_tensor(
                    out=P.DG[:, d0:d0 + nd, :],
                    in0=P.identf[:].unsqueeze(1).to_broadcast([128, nd, 128]),
                    in1=P.cw[:, c0:c0 + nch, :].rearrange("p c k -> p (c k)").unsqueeze(2).to_broadcast([128, nd, 128]), op=ALU.mult),
                      reads=[P.kb_, P.cb_], writes=[P.DGb])

        assert mode != "halo"
        stageA(0)
        if mode != "halo":
            dtT = sb("m_dtT", [32, 512], F32)
            acT = sb("m_acT", [32, 512], F32)
            dtb_ = Buf("dtT")
            acb_ = Buf("acT")
            TK = sb("m_TK", [128, 4, 5, 32], F32)
            TKb = [Buf(f"TK{i}") for i in range(4)]
            s, wb = ring.get()
            Wd = W8[s][:, 0:8 * 32].rearrange("p (k n) -> p k n", k=8)
            Pd, Pdb = nextpa()
            for k in range(KD):
                kb.op("pe", lambda e, k=k: e.matmul(Pd[0:32, :T], Wd[:, k, 0:32], uT[:, k, :T], start=(k == 0), stop=(k == KD - 1)),
                      reads=[wb, ub[k]], writes=[Pdb], inc=(k == KD - 1))
            ring.prefetch()
            kb.op("act", lambda e: e.activation(out=dtT[:, :T], in_=Pd[0:32, :T], func=AF.Exp, bias=P.dtb[:], scale=1.0),
                  reads=[Pdb, P.cb_], writes=[dtb_])
            kb.op("act", lambda e: e.activation(out=dtT[:, :T], in_=dtT[:, :T], func=AF.Ln, bias=1.0, scale=1.0),
                  reads=[dtb_], writes=[dtb_])
            kb.op("dve", lambda e: e.tensor_scalar(out=acT[:, :T], in0=dtT[:, :T], scalar1=P.a[:, 0:1], scalar2=None, op0=ALU.mult),
                  reads=[dtb_, P.cb_], writes=[acb_])
            for tb in range(NB):
                sl = slice(tb * 128, (tb + 1) * 128)
                kb.op("dve", lambda e, sl=sl: e.tensor_tensor_scan(out=acT[:, sl], data0=R.ones[0:32, 0:128], data1=acT[:, sl],
                                                                    initial=0.0, op0=ALU.mult, op1=ALU.add),
                      reads=[acb_, R.onesb], writes=[acb_])
            if full:
                kb.dma("sp", P.acd[:, :T], acT[:, :T], P.acds, reads=[acb_], writes=[P.acdb])
            for tb in range(NB):
                sl = slice(tb * 128, (tb + 1) * 128)
                kb.op("pe", lambda e, sl=sl: e.transpose(PT[:, 0:32], dtT[:, sl], P.identf[0:32, 0:32]),
                      reads=[dtb_, P.kb_], writes=[PTb])
                kb.op("dve", lambda e, tb=tb: e.tensor_copy(out=TK[:, tb, 0, :], in_=PT[:, 0:32]), reads=[PTb], writes=[TKb[tb]])
                kb.op("pe", lambda e, sl=sl: e.transpose(PT[:, 32:64], acT[:, sl], P.identf[0:32, 0:32]),
                      reads=[acb_, P.kb_], writes=[PTb])
                kb.op("dve", lambda e, tb=tb: e.tensor_copy(out=TK[:, tb, 4, :], in_=PT[:, 32:64]), reads=[PTb], writes=[TKb[tb]])
                kb.op("dve", lambda e, tb=tb: e.tensor_scalar(out=TK[:, tb, 1, :], in0=TK[:, tb, 4, :], scalar1=-1.0, scalar2=None, op0=ALU.mult),
                      reads=[TKb[tb]], writes=[TKb[tb]])
                kb.op("act", lambda e, tb=tb: e.activation(out=TK[:, tb, 2, :], in_=TK[:, tb, 4, :], func=AF.Exp),
                      reads=[TKb[tb]], writes=[TKb[tb]])
                kb.op("pe", lambda e, tb=tb: e.matmul(PT[:, 64:96], P.sel[:], TK[:, tb, 4, :], start=True, stop=True),
                      reads=[TKb[tb], P.kb_], writes=[PTb])
                kb.op("dve", lambda e, tb=tb: e.tensor_tensor(out=TK[:, tb, 3, :], in0=PT[:, 64:96], in1=TK[:, tb, 4, :], op=ALU.subtract),
                      reads=[PTb, TKb[tb]], writes=[TKb[tb]])
                kb.op("act", lambda e, tb=tb: e.activation(out=TK[:, tb, 3, :], in_=TK[:, tb, 3, :], func=AF.Exp),
                      reads=[TKb[tb]], writes=[TKb[tb]])
                kb.op("dve", lambda e, tb=tb: e.tensor_tensor(out=TK[:, tb, 3, :], in0=TK[:, tb, 3, :], in1=TK[:, tb, 0, :], op=ALU.mult),
                      reads=[TKb[tb]], writes=[TKb[tb]])
                kb.op("act", lambda e, tb=tb: e.activation(out=TK[:, tb, 4, :], in_=PT[:, 64:96], func=AF.Exp),
                      reads=[PTb, TKb[tb]], writes=[TKb[tb]])
            PX = ps("mpx", [128, 1024], BF16)
            PXb = Buf("mpx")
            xw = sb("m_xw", [128, 4, 512], BF16)
            xwb = Buf("xw")
            Btok = sb("m_Bt", [128, 4, 128], BF16)
            Btb = Buf("Bt")
            t2 = sb("m_t2", [128, 512], F32)
            t2b = Buf("t2")
            if full:
                PYd = [ps(f"mpyd{i}", [128, 512], F32) for i in range(2)]
                PYdb = [Buf(f"mpyd{i}") for i in range(2)]
                PYo = [ps(f"mpyo{i}", [128, 512], F32) for i in range(2)]
                PYob = [Buf(f"mpyo{i}") for i in range(2)]
                SZ = sb("m_SZ", [128, 4, 512], BF16)
                SZb = Buf("SZ")
                xD = sb("m_xD", [128, 4, 512], BF16)
                xDb = Buf("xD")
                xdt = sb("m_xdt", [128, 4, 512], BF16)
                xdtb = Buf("xdt")
                ABC = [sb(f"m_abc{i}", [128, 8, 128], F32) for i in range(2)]
                ABCb = [Buf(f"abc{i}") for i in range(2)]
                dar = [sb(f"m_dar{i}", [128, 4, 128], F32) for i in range(4)]
                darb = [Buf(f"dar{i}") for i in range(4)]
                Ee = [sb(f"m_Ee{i}", [128, 4, 128], BF16) for i in range(4)]
                Eeb = [Buf(f"Ee{i}") for i in range(4)]
                CBm = [sb(f"m_CBm{i}", [128, 128], F32) for i in range(2)]
                CBmb = [Buf(f"CBm{i}") for i in range(2)]
                scT = [sb(f"m_sc{i}", [128, 4, 128], BF16) for i in range(4)]
                scb = [Buf(f"sc{i}") for i in range(4)]
                yg = sb("m_yg", [128, 4, 512], F32)
                ygb = [Buf(f"yg{i}") for i in range(4)]
                t1s = [sb(f"m_t1_{i}", [128, 512], F32) for i in range(2)]
                t1bs = [Buf(f"t1_{i}") for i in range(2)]
                pend = []
                pend2 = []
                ynb = sb("m_ynb", [128, 4, 512], BF16)
                ynbb = Buf("ynb")
                ynT = sb("m_ynT", [128, 16, 512], BF16)
                ynTb = Buf("ynT")
                ss = sb("m_ss", [128, 4], F32)
                ssb = Buf("ss")
                abci = [0]

        norm_pend = []
        for g in range(NG):
            chunks = [g * 4 + i for i in range(4)] + [16 + g, 20 + g]
            for i, c in enumerate(chunks):
                Pq, Pqb = nextpa()
                for k in range(4):
                    kb.op("pe", lambda e, k=k, i=i, c=c: e.matmul(Pq[:, :T], P.DG[:, i * 4 + k, :], XB[:, i, 1 + k:1 + k + T],
                                                                   start=(k == 0), stop=(k == 3)),
                          reads=[XBb, P.DGb], writes=[Pqb], inc=(k == 3))
                kb.op("act", lambda e, i=i, c=c: e.activation(out=XC[:, i, :T], in_=Pq[:, :T], func=AF.Silu, bias=P.cb[:, c:c + 1], scale=1.0),
                      reads=[Pqb, P.cb_], writes=[XCb])
            kb.op("dve", lambda e: e.tensor_copy(out=P.HAL[:, g * 4:g * 4 + 4, :], in_=XB[:, 0:4, T:T + 4]), reads=[XBb], writes=[P.halb])
            kb.op("dve", lambda e: e.tensor_copy(out=P.HAL[:, 16 + g, :], in_=XB[:, 4, T:T + 4]), reads=[XBb], writes=[P.halb])
            kb.op("dve", lambda e: e.tensor_copy(out=P.HAL[:, 20 + g, :], in_=XB[:, 5, T:T + 4]), reads=[XBb], writes=[P.halb])
            if g + 1 < NG:
                stageA(g + 1)
            if full:
                s, wb = ring.get()
                Wz = W8[s][:, 0:8 * 512].rearrange("p (k n) -> p k n", k=8)
                for tb in range(NB):
                    Pq, Pqb = nextpa()
                    for k in range(KD):
                        kb.op("pe", lambda e, k=k, tb=tb: e.matmul(Pq[:, :], uT[:, k, tb * 128:(tb + 1) * 128], Wz[:, k, :],
                                                                    start=(k == 0), stop=(k == KD - 1)),
                              reads=[wb, ub[k]], writes=[Pqb], inc=(k == KD - 1))
                    kb.op("act", lambda e, tb=tb: e.activation(out=SZ[:, tb, :], in_=Pq[:, :], func=AF.Silu),
                          reads=[Pqb], writes=[SZb])
                ring.prefetch()
                for i in range(4):
                    kb.op("act", lambda e, i=i: e.activation(out=xD[:, i, :T], in_=XC[:, i, :T], func=AF.Copy,
                                                              scale=P.dch[:, g * 4 + i:g * 4 + i + 1]),
                          reads=[XCb, P.cb_], writes=[xDb])
            while norm_pend:
                norm_pend.pop(0)()
            for tb in range(NB):
                sl = slice(tb * 128, (tb + 1) * 128)
                kb.op("pe", lambda e, sl=sl, tb=tb: e.transpose(PX[:, 512 + tb * 128:640 + tb * 128], XC[:, 4, sl], P.ident[:]),
                      reads=[XCb, P.kb_], writes=[PXb], inc=(tb == NB - 1))
            kb.op("act", lambda e: e.activation(out=Btok[:, 0:NB, :], in_=PX[:, 512:512 + NB * 128].rearrange("p (t n) -> p t n", t=NB), func=AF.Copy),
                  reads=[PXb], writes=[Btb])
            for tb in range(NB):
                sl = slice(tb * 128, (tb + 1) * 128)
                for i in range(4):
                    kb.op("pe", lambda e, i=i, sl=sl: e.transpose(PX[:, i * 128:(i + 1) * 128], XC[:, i, sl], P.ident[:]),
                          reads=[XCb, P.kb_], writes=[PXb], inc=(i == 3))
                hs = slice(g * 8, (g + 1) * 8)
                kb.op("dve", lambda e, tb=tb: e.tensor_tensor(out=xw[:, tb, :].rearrange("p (h d) -> p h d", h=8),
                                                               in0=PX[:, 0:512].rearrange("p (h d) -> p h d", h=8),
                                                               in1=TK[:, tb, 3, hs].unsqueeze(2).to_broadcast([128, 8, 64]), op=ALU.mult),
                      reads=[PXb, TKb[tb]], writes=[xwb])
                if full:
                    kb.op("dve", lambda e, tb=tb: e.tensor_tensor(out=xdt[:, tb, :].rearrange("p (h d) -> p h d", h=8),
                                                                   in0=PX[:, 0:512].rearrange("p (h d) -> p h d", h=8),
                                                                   in1=TK[:, tb, 0, hs].unsqueeze(2).to_broadcast([128, 8, 64]), op=ALU.mult),
                          reads=[PXb, TKb[tb]], writes=[xdtb])
            gs = slice(g * 512, (g + 1) * 512)
            hs = slice(g * 8, (g + 1) * 8)

            def state_update(tb):
                PS_, PSb = nextpa()
                kb.op("pe", lambda e: e.matmul(PS_[:, :], Btok[:, tb, :], xw[:, tb, :], start=True, stop=True),
                      reads=[Btb, xwb], writes=[PSb])
                kb.op("pool", lambda e: e.tensor_tensor(out=t2[:].rearrange("p (h d) -> p h d", h=8),
                                                        in0=P.S[:, gs].rearrange("p (h d) -> p h d", h=8),
                                                        in1=TK[:, tb, 4, hs].unsqueeze(2).to_broadcast([128, 8, 64]), op=ALU.mult),
                      reads=[P.Sbuf[g], TKb[tb]], writes=[t2b])
                if full:
                    kb.op("dve", lambda e: e.tensor_tensor(out=P.Sb[:, gs], in0=PS_[:, :], in1=t2[:], op=ALU.add),
                          reads=[PSb, t2b], writes=[P.Sbb[g]])
                kb.op("dve", lambda e: e.tensor_tensor(out=P.S[:, gs], in0=PS_[:, :], in1=t2[:], op=ALU.add),
                      reads=[PSb, t2b], writes=[P.Sbuf[g]])

            if not full:
                for tb in range(NB):
                    state_update(tb)
            else:
                def head_a(tb):
                    sl = slice(tb * 128, (tb + 1) * 128)
                    p = tb % 2
                    ai = abci[0] % 2
                    abci[0] += 1
                    src = P.acd[g * 8:(g + 1) * 8, sl]
                    kb.dma("sp", ABC[ai][:], src.partition_broadcast(128), P.abcs[ai], reads=[P.acdb], writes=[ABCb[ai]])
                    kb.op("pe", lambda e: e.matmul(PT[:, 0:128], XC[:, 4, sl], XC[:, 5, sl], start=True, stop=True),
                          reads=[XCb], writes=[PTb])
                    kb.op("dve", lambda e: e.tensor_tensor(out=CBm[p][:], in0=PT[:, 0:128], in1=P.tri[:], op=ALU.mult),
                          reads=[PTb, P.kb_], writes=[CBmb[p]])
                    for hb4 in range(2):
                        bi = p * 2 + hb4
                        h0 = g * 8 + hb4 * 4
                        kb.op("pool", lambda e, hb4=hb4, h0=h0, bi=bi: e.tensor_tensor(
                            out=dar[bi][:], in0=ABC[ai][:, hb4 * 4:hb4 * 4 + 4, :],
                            in1=TK[:, tb, 1, h0:h0 + 4].unsqueeze(2).to_broadcast([128, 4, 128]), op=ALU.add),
                              reads=[ABCb[ai], TKb[tb]], writes=[darb[bi]])
                        kb.op("pool", lambda e, bi=bi: e.tensor_tensor(out=dar[bi][:], in0=dar[bi][:],
                                                                        in1=P.tri[:].unsqueeze(1).to_broadcast([128, 4, 128]), op=ALU.mult),
                              reads=[darb[bi], P.kb_], writes=[darb[bi]])
                        kb.op("act", lambda e, bi=bi: e.activation(out=Ee[bi][:], in_=dar[bi][:], func=AF.Exp),
                              reads=[darb[bi]], writes=[Eeb[bi]])

                def head_b(tb):
                    p = tb % 2
                    for hb4 in range(2):
                        bi = p * 2 + hb4
                        kb.op("dve", lambda e, bi=bi: e.tensor_tensor(
                            out=scT[bi][:], in0=Ee[bi][:], in1=CBm[p][:].unsqueeze(1).to_broadcast([128, 4, 128]), op=ALU.mult),
                              reads=[Eeb[bi], CBmb[p]], writes=[scb[bi]])

                def body_pre(tb):
                    sl = slice(tb * 128, (tb + 1) * 128)
                    p = tb % 2
                    kb.op("pe", lambda e: e.matmul(PYo[p][:, :], XC[:, 5, sl], P.Sb[:, gs], start=True, stop=True),
                          reads=[XCb, P.Sbb[g]], writes=[PYob[p]])
                    t1, t1b = t1s[p], t1bs[p]
                    kb.op("dve", lambda e: e.tensor_tensor(out=t1[:].rearrange("p (h d) -> p h d", h=8),
                                                           in0=PYo[p][:, :].rearrange("p (h d) -> p h d", h=8),
                                                           in1=TK[:, tb, 2, hs].unsqueeze(2).to_broadcast([128, 8, 64]), op=ALU.mult),
                          reads=[PYob[p], TKb[tb]], writes=[t1b])
                    state_update(tb)
                    while pend:
                        pend.pop(0)()

                def body_mm(tb):
                    sl = slice(tb * 128, (tb + 1) * 128)
                    p = tb % 2
                    t1, t1b = t1s[p], t1bs[p]
                    for hb4 in range(2):
                        bi = p * 2 + hb4
                        for pp in range(2):
                            pr = hb4 * 2 + pp
                            kb.op("pe", lambda e, pr=pr: e.matmul(PYd[p][:, pr * 128:(pr + 1) * 128], xD[:, pr, sl], P.ident[:],
                                                                   start=True, stop=False),
                                  reads=[xDb, P.kb_], writes=[PYdb[p]], inc=False)
                            for hh in range(2):
                                hl = pr * 2 + hh
                                j4 = pp * 2 + hh
                                kb.op("pe", lambda e, hl=hl, j4=j4, bi=bi, hh=hh: e.matmul(
                                    PYd[p][:, hl * 64:(hl + 1) * 64], scT[bi][:, j4, :], xdt[:, tb, hl * 64:(hl + 1) * 64], start=False, stop=(hh == 1)),
                                      reads=[scb[bi], xdtb], writes=[PYdb[p]], inc=(hh == 1))
                    while pend2:
                        pend2.pop(0)()

                    def tail1():
                        kb.op("dve", lambda e: e.tensor_tensor(out=t1[:], in0=PYd[p][:, :], in1=t1[:], op=ALU.add),
                              reads=[PYdb[p], t1b], writes=[t1b])

                    def tail2():
                        kb.op("pool", lambda e: e.tensor_tensor(out=yg[:, tb, :], in0=t1[:], in1=SZ[:, tb, :], op=ALU.mult),
                              reads=[t1b, SZb], writes=[ygb[tb]])
                    pend.append(tail1)
                    pend2.append(tail2)

                head_a(0)
                head_b(0)
                for tb in range(NB):
                    if tb + 1 < NB:
                        head_a(tb + 1)
                    body_pre(tb)
                    if tb + 1 < NB:
                        head_b(tb + 1)
                    body_mm(tb)
                while pend:
                    pend.pop(0)()
                while pend2:
                    pend2.pop(0)()
            if full:
                for tb in range(NB):
                    kb.op("act", lambda e, tb=tb: e.activation(out=ynb[:, tb, :], in_=yg[:, tb, :], func=AF.Square, accum_out=ss[:, tb:tb + 1]),
                          reads=[ygb[tb]], writes=[ynbb, ssb])
                kb.op("act", lambda e: e.activation(out=ss[:, 0:NB], in_=ss[:, 0:NB], func=AF.Sqrt, bias=R.epsb[:], scale=1.0 / 512),
                      reads=[ssb, R.epsbuf], writes=[ssb])
                kb.op("dve", lambda e: e.reciprocal(ss[:, 0:NB], ss[:, 0:NB]), reads=[ssb], writes=[ssb])
                for tb in range(NB):
                    kb.op("dve", lambda e, tb=tb: e.tensor_scalar(out=ynb[:, tb, :], in0=yg[:, tb, :], scalar1=ss[:, tb:tb + 1], scalar2=None, op0=ALU.mult),
                          reads=[ygb[tb], ssb], writes=[ynbb])
                def norm_tr(g=g):
                    for i in range(4):
                        for tb in range(NB):
                            kb.op("pe", lambda e, i=i, tb=tb: e.transpose(PX[:, tb * 128:(tb + 1) * 128], ynb[:, tb, i * 128:(i + 1) * 128], P.ident[:]),
                                  reads=[ynbb, P.kb_], writes=[PXb], inc=(tb == NB - 1))
                        c = g * 4 + i
                        kb.op("dve", lambda e, c=c: e.tensor_scalar(out=ynT[:, c, :T], in0=PX[:, 0:T], scalar1=P.gn[:, c:c + 1], scalar2=None, op0=ALU.mult),
                              reads=[PXb, P.cb_], writes=[ynTb])
                norm_pend.append(norm_tr)
        while norm_pend:
            norm_pend.pop(0)()
        if full:
            for mb in range(4):
                s, wb = ring.get()
                Wo = W8[s][:, 0:16 * 256].rearrange("p (c n) -> p c n", c=16)
                for mm in range(2):
                    m = mb * 2 + mm
                    Pq, Pqb = nextpa()
                    for c in range(16):
                        kb.op("pe", lambda e, c=c, mm=mm: e.matmul(Pq[:, :T], Wo[:, c, mm * 128:(mm + 1) * 128], ynT[:, c, :T],
                                                                    start=(c == 0), stop=(c == 15)),
                              reads=[wb, ynTb], writes=[Pqb], inc=(c == 15))
                    kb.op("dve", lambda e, m=m: e.tensor_tensor(out=hT[:, m, :T], in0=Pq[:, :T], in1=hT[:, m, :T], op=ALU.add),
                          reads=[Pqb, hb], writes=[hb])
                ring.prefetch()
        kb.barrier()


T = 512
NTOK = 4096
NT = NTOK // T


def dram_in(nc, name, shape, dt=F32):
    return nc.dram_tensor(name, list(shape), dt, kind="ExternalInput").ap()


def dram_out(nc, name, shape, dt=F32):
    return nc.dram_tensor(name, list(shape), dt, kind="ExternalOutput").ap()


def load_vec(kb, name, ap, shape, buf, ds, dt=F32):
    t = kb.sb(name, shape, dt)
    kb.dma("sp", t[:], ap, ds, writes=[buf], part=True)
    return t


def mamba_inputs(nc, full):
    dr = {"ssm_g": dram_in(nc, "ssm_g", [128, KD]), "cw": dram_in(nc, "cw", [128, 24, 4]), "cb": dram_in(nc, "cb", [128, 24]),
          "dtb": dram_in(nc, "dtb", [32, 1]), "alog": dram_in(nc, "alog", [32, 1]), "isec": dram_in(nc, "isec", [128, 1]),
          "w_in": dram_in(nc, "w_in", [D, 5152])}
    if full:
        dr["dch"] = dram_in(nc, "dch", [128, 16])
        dr["gn"] = dram_in(nc, "gn", [128, 16])
        dr["w_out"] = dram_in(nc, "w_out", [2048, D])
        dr["acum_scr"] = nc.dram_tensor("acum_scr", [32, 512], F32, kind="Internal").ap()
    return dr


def kv_items(W8, w_kv, ws):
    wv = w_kv.rearrange("(k p) n -> p k n", p=128)
    parts = []
    for kvh in range(2):
        for dup in range(2):
            parts.append((kvh * 128 + dup * 64, 64, wv[:, :, kvh * 64:(kvh + 1) * 64]))
    parts.append((256, 128, wv[:, :, 128:256]))
    return [ws.item(W8, "kv", parts, 8 * 384, 8)]


def kv_tile(kb, R, KVc, hT, hb, ring, W8, KT_out, V_out, t, osem, dbuf=None, after_norm=None):
    with contextlib.ExitStack() as st:
        sb = lambda n, s, d: st.enter_context(kb.nc.sbuf_tensor(uniq(n), list(s), d))
        ps = lambda n, s, d: st.enter_context(kb.nc.psum_tensor(uniq(n), list(s), d))
        PA = [ps(f"kpa{i}", [128, 512], F32) for i in range(2)]
        PAb = [Buf(f"kpa{i}") for i in range(2)]
        PN = ps("kpn", [128, 512], F32); PNb = Buf("kpn")
        rmsnorm_T(kb, R, hT, hb, KVc["g"], KVc["buf"], PN, PNb, T)
        if after_norm is not None:
            after_norm()
        s, wb = ring.get()
        W = W8[s][:, 0:8 * 384].rearrange("p (k n) -> p k n", k=8)
        KT = sb("k_KT", [128, 2, T], BF16); KTb = Buf("KT")
        sq = sb("k_sq", [128, T], BF16); sqb = Buf("ksq")
        rs = sb("k_rs", [128, T], F32); rsb = Buf("krs")
        Vt = sb("k_V", [128, 4, 128], BF16); Vb = Buf("kV")
        for kvh in range(2):
            Pq, Pqb = PA[kvh], PAb[kvh]
            for k in range(KD):
                kb.op("pe", lambda e, k=k: e.matmul(Pq[:, :T], W[:, k, kvh * 128:(kvh + 1) * 128], R.uT[:, k, :T],
                                                     start=(k == 0), stop=(k == KD - 1)),
                      reads=[wb, R.ub[k]], writes=[Pqb], inc=(k == KD - 1))
            kb.op("act", lambda e: e.activation(out=sq[:, :T], in_=Pq[:, :T], func=AF.Square), reads=[Pqb], writes=[sqb])
            kb.op("pe", lambda e: e.matmul(PN[:, :T], KVc["BD"][:], sq[:, :T], start=True, stop=True),
                  reads=[sqb, KVc["buf"]], writes=[PNb])
            kb.op("act", lambda e: e.activation(out=rs[:, :T], in_=PN[:, :T], func=AF.Sqrt, bias=R.epsb[:], scale=1.0 / 64),
                  reads=[PNb, R.epsbuf], writes=[rsb])
            kb.op("dve", lambda e: e.reciprocal(rs[:, :T], rs[:, :T]), reads=[rsb], writes=[rsb])
            kb.op("dve", lambda e: e.scalar_tensor_tensor(out=KT[:, kvh, :T], in0=Pq[:, :T], scalar=KVc["kn2"][:, 0:1], in1=rs[:, :T],
                                                           op0=ALU.mult, op1=ALU.mult),
                  reads=[Pqb, rsb, KVc["buf"]], writes=[KTb])
        for tb in range(4):
            Pq, Pqb = PA[tb % 2], PAb[tb % 2]
            for k in range(KD):
                kb.op("pe", lambda e, k=k: e.matmul(Pq[:, 0:128], R.uT[:, k, tb * 128:(tb + 1) * 128], W[:, k, 256:384],
                                                     start=(k == 0), stop=(k == KD - 1)),
                      reads=[wb, R.ub[k]], writes=[Pqb], inc=(k == KD - 1))
            kb.op("act", lambda e: e.activation(out=Vt[:, tb, :], in_=Pq[:, 0:128], func=AF.Copy), reads=[Pqb], writes=[Vb])
        ring.prefetch()
        wr = [dbuf] if dbuf is not None else []
        if dbuf is None:
            for kvh in range(2):
                kb.dma("sp", KT_out[kvh, :, t * T:(t + 1) * T], KT[:, kvh, :], osem, reads=[KTb])
            kb.dma("sp", V_out[t * T:(t + 1) * T, :].rearrange("(b p) c -> p b c", p=128), Vt[:], osem, reads=[Vb])
        elif t < 0:
            for kvh in range(2):
                kb.dma("sp", KT_out[kvh, :, 0:128], KT[:, kvh, T - 128:T], osem, reads=[KTb], writes=wr, part=True)
            kb.dma("sp", V_out[0:128, :], Vt[:, 3, :], osem, reads=[Vb], writes=wr, part=True)
        else:
            o = 128 + t * T
            for kvh in range(2):
                kb.dma("sp", KT_out[kvh, :, o:o + T], KT[:, kvh, :], osem, reads=[KTb], writes=wr, part=True)
            kb.dma("sp", V_out[o:o + T, :].rearrange("(b p) c -> p b c", p=128), Vt[:], osem, reads=[Vb], writes=wr, part=True)
        kb.barrier()


def make_BD(kb, name, buf):
    t = kb.sb(name, [128, 128], BF16)
    kb.op("pool", lambda e: e.memset(t[:], 0.0), writes=[buf])
    kb.op("pool", lambda e: e.memset(t[0:64, 0:64], 1.0), reads=[buf], writes=[buf])
    kb.op("pool", lambda e: e.memset(t[64:128, 64:128], 1.0), reads=[buf], writes=[buf])
    return t


CSH = 4.0


def attn_items(W8, w_q, w_o, ws):
    wq = w_q.rearrange("(k p) n -> p k n", p=128)
    wo = w_o.rearrange("(k p) n -> p k n", p=128)
    a = [ws.item(W8, f"q{h}", [(0, 512, wq[:, :, h * 512:(h + 1) * 512])], 8 * 512, 8, defer=True) for h in range(2)]
    b = [ws.item(W8, f"wo{h}", [(0, 512, wo[:, :, h * 512:(h + 1) * 512])], 8 * 512, 8, defer=True) for h in range(2)]
    return a, b


def attn_tile(kb, R, AC, hT, hb, ring, W8, t):
    with contextlib.ExitStack() as st:
        sb = lambda n, s, d: st.enter_context(kb.nc.sbuf_tensor(uniq(n), list(s), d))
        ps = lambda n, s, d: st.enter_context(kb.nc.psum_tensor(uniq(n), list(s), d))
        PSs = [[ps(f"aps{q}{i}", [128, 512], F32) for i in range(2)] for q in range(2)]
        PSsb = [[Buf(f"aps{q}{i}") for i in range(2)] for q in range(2)]
        PO = [ps(f"apo{i}", [128, 512], F32) for i in range(2)]; POb = [Buf(f"apo{i}") for i in range(2)]
        PD = ps("apd", [128, 512], F32); PDb = Buf("apd")
        PX = ps("apx", [128, 1024], BF16); PXb = Buf("apx")
        cb = AC["buf"]
        rmsnorm_T(kb, R, hT, hb, AC["g"], cb, PD, PDb, T)
        QT = sb("a_QT", [128, 8, T], BF16); QTb = Buf("QT")
        sq = sb("a_sq", [128, T], BF16); sqb = Buf("asq")
        rs = sb("a_rs", [128, T], F32); rsb = Buf("ars")
        for half in range(2):
            s, wb = ring.get()
            W = W8[s][:, 0:8 * 512].rearrange("p (k n) -> p k n", k=8)
            for i in range(4):
                qc = half * 4 + i
                Pq, Pqb = PSs[0][i % 2], PSsb[0][i % 2]
                for k in range(KD):
                    kb.op("pe", lambda e, k=k, i=i: e.matmul(Pq[:, :T], W[:, k, i * 128:(i + 1) * 128], R.uT[:, k, :T],
                                                              start=(k == 0), stop=(k == KD - 1)),
                          reads=[wb, R.ub[k]], writes=[Pqb], inc=(k == KD - 1))
                kb.op("act", lambda e: e.activation(out=sq[:, :T], in_=Pq[:, :T], func=AF.Square), reads=[Pqb], writes=[sqb])
                kb.op("pe", lambda e: e.matmul(PD[:, :T], AC["BD"][:], sq[:, :T], start=True, stop=True),
                      reads=[sqb, cb], writes=[PDb])
                kb.op("act", lambda e: e.activation(out=rs[:, :T], in_=PD[:, :T], func=AF.Sqrt, bias=R.epsb[:], scale=1.0 / 64),
                      reads=[PDb, R.epsbuf], writes=[rsb])
                kb.op("dve", lambda e: e.reciprocal(rs[:, :T], rs[:, :T]), reads=[rsb], writes=[rsb])
                kb.op("dve", lambda e, qc=qc: e.scalar_tensor_tensor(out=QT[:, qc, :T], in0=Pq[:, :T], scalar=AC["qn2"][:, 0:1], in1=rs[:, :T],
                                                                      op0=ALU.mult, op1=ALU.mult),
                      reads=[Pqb, rsb, cb], writes=[QTb])
            ring.prefetch()
        OT = sb("a_OT", [128, 8, T], BF16); OTb = Buf("OT")
        PTs = [[sb(f"a_PT{q}{i}", [128, 512], BF16) for i in range(2)] for q in range(2)]
        PTsb = [[Buf(f"aPT{q}{i}") for i in range(2)] for q in range(2)]
        Ee = [[sb(f"a_Ee{q}{i}", [128, 512], BF16) for i in range(2)] for q in range(2)]
        Eeb = [[Buf(f"aEe{q}{i}") for i in range(2)] for q in range(2)]
        On = sb("a_On", [128, 1024], BF16); Onb = Buf("On")
        den = sb("a_den", [128, 16], F32); denb = Buf("den")
        KTs, Vs = AC["KT"], AC["V"]
        quads = [(qb, kvh, par) for qb in range(4) for kvh in range(2) for par in range(2)]

        def front(qi):
            qb, kvh, par = quads[qi]
            pq = qi % 2
            gb = t * 4 + qb
            EBx = AC["EB0"] if gb == 0 else AC["EB"]
            qs = slice(qb * 128, (qb + 1) * 128)
            rows = slice(par * 64, par * 64 + 64)
            for blk in range(2):
                kc = gb * 128 + blk * 128
                kb.op("pe", lambda e, blk=blk, kc=kc: e.matmul(PSs[pq][blk][:, :], KTs[rows, kvh, kc:kc + 128],
                                                                QT[rows, kvh * 4:(kvh + 1) * 4, qs], start=True, stop=True),
                      reads=[AC["kvbuf"], QTb], writes=[PSsb[pq][blk]])
                kb.op("act", lambda e, blk=blk: e.activation(out=Ee[pq][blk][:], in_=PSs[pq][blk][:, :], func=AF.Exp, scale=0.125),
                      reads=[PSsb[pq][blk]], writes=[Eeb[pq][blk]])
                idx = (blk * 2 + kvh) * 2 + par
                kb.op("dve", lambda e, blk=blk, idx=idx: e.tensor_tensor(
                    out=PTs[pq][blk][:], in0=Ee[pq][blk][:],
                    in1=EBx[:, idx * 4:(idx + 1) * 4, :].rearrange("p j q -> p (j q)"), op=ALU.mult),
                      reads=[Eeb[pq][blk], cb], writes=[PTsb[pq][blk]])

        def back(qi):
            qb, kvh, par = quads[qi]
            pq = qi % 2
            gb = t * 4 + qb
            for jj in range(4):
                h = kvh * 8 + 2 * jj + par
                bank, hc = h // 8, (h % 8) * 64
                for blk in range(2):
                    kb.op("pe", lambda e, blk=blk: e.matmul(PO[bank][:, hc:hc + 64], PTs[pq][blk][:, jj * 128:(jj + 1) * 128],
                                                             Vs[:, gb + blk, kvh * 64:(kvh + 1) * 64], start=(blk == 0), stop=(blk == 1)),
                          reads=[PTsb[pq][blk], AC["kvbuf"]], writes=[POb[bank]], inc=(blk == 1))
                for blk in range(2):
                    kb.op("pe", lambda e, blk=blk: e.matmul(PD[:, h:h + 1], PTs[pq][blk][:, jj * 128:(jj + 1) * 128],
                                                             AC["onec"][:, 0:1], start=(blk == 0), stop=(blk == 1)),
                          reads=[PTsb[pq][blk], cb], writes=[PDb], inc=(blk == 1))

        front(0)
        for qi in range(len(quads)):
            qb = quads[qi][0]
            qs = slice(qb * 128, (qb + 1) * 128)
            if qi + 1 < len(quads):
                front(qi + 1)
            back(qi)
            if qi % 4 != 3:
                continue
            kb.op("dve", lambda e: e.tensor_tensor(out=den[:], in0=PD[:, 0:16], in1=AC["esink"][:], op=ALU.add),
                  reads=[PDb, cb], writes=[denb])
            kb.op("dve", lambda e: e.reciprocal(den[:], den[:]), reads=[denb], writes=[denb])
            for bank in range(2):
                kb.op("dve", lambda e, bank=bank: e.tensor_tensor(
                    out=On[:, bank * 512:(bank + 1) * 512].rearrange("p (h d) -> p h d", h=8),
                    in0=PO[bank][:, :].rearrange("p (h d) -> p h d", h=8),
                    in1=den[:, bank * 8:(bank + 1) * 8].unsqueeze(2).to_broadcast([128, 8, 64]), op=ALU.mult),
                      reads=[POb[bank], denb], writes=[Onb])
            for c in range(8):
                kb.op("pe", lambda e, c=c: e.transpose(PX[:, c * 128:(c + 1) * 128], On[:, c * 128:(c + 1) * 128], AC["ident"][:]),
                      reads=[Onb, cb], writes=[PXb], inc=(c == 7))
            kb.op("act", lambda e: e.activation(out=OT[:, :, qs], in_=PX[:, :].rearrange("p (c q) -> p c q", c=8), func=AF.Copy),
                  reads=[PXb], writes=[OTb])
        for half in range(2):
            s, wb = ring.get()
            W = W8[s][:, 0:8 * 512].rearrange("p (k n) -> p k n", k=8)
            for i in range(4):
                m = half * 4 + i
                Pq, Pqb = PSs[0][i % 2], PSsb[0][i % 2]
                for c in range(8):
                    kb.op("pe", lambda e, c=c, i=i: e.matmul(Pq[:, :T], W[:, c, i * 128:(i + 1) * 128], OT[:, c, :T],
                                                              start=(c == 0), stop=(c == 7)),
                          reads=[wb, OTb], writes=[Pqb], inc=(c == 7))
                kb.op("dve", lambda e, m=m: e.tensor_tensor(out=hT[:, m, :T], in0=Pq[:, :T], in1=hT[:, m, :T], op=ALU.add),
                      reads=[Pqb, hb], writes=[hb])
            ring.prefetch()
        kb.barrier()


def build_fused(ntiles=NT):
    nc = bass.Bass("TRN2", target_bir_lowering=False)
    ntok = ntiles * T
    xTp = dram_in(nc, "xTp", [D, ntok])
    xT = dram_in(nc, "xT", [D, ntok])
    gF = [dram_in(nc, f"gF{i}", [128, KD]) for i in range(4)]
    wF = [[dram_in(nc, f"wF{i}_{j}", s) for j, s in enumerate([[D, DFF], [D, DFF], [DFF, D]])] for i in range(4)]
    dr = mamba_inputs(nc, True)
    kvg = dram_in(nc, "kvg", [128, KD]); kn2 = dram_in(nc, "kn2", [128, 1]); w_kv = dram_in(nc, "w_kv", [D, 256])
    gat = dram_in(nc, "gat", [128, KD])
    w_q = dram_in(nc, "w_q", [D, D]); w_o = dram_in(nc, "w_o", [D, D])
    qn2 = dram_in(nc, "qn2", [128, 1]); sinkrow = dram_in(nc, "sinkrow", [128, 16])
    biasT = dram_in(nc, "biasT", [128, 32, 128]); biasT0 = dram_in(nc, "biasT0", [128, 32, 128])
    outT = dram_out(nc, "outT", [D, ntok])
    h3d = nc.dram_tensor("h3_scr", [D, ntok], F32, kind="Internal").ap()
    KTd = nc.dram_tensor("KT_scr", [2, 128, 128 + ntok], BF16, kind="Internal").ap()
    Vd = nc.dram_tensor("V_scr", [128 + ntok, 128], BF16, kind="Internal").ap()
    h3b = Buf("h3d"); kvdb = Buf("kvd")
    nblk = ntok // 128 + 1
    with contextlib.ExitStack() as st:
        kb = KB(nc, st)
        R = FFNRes(kb, T)
        cbuf = Buf("c"); dsc = kb.dsem("ds_c")
        gv = [load_vec(kb, f"gv{i}", gF[i][:, :], [128, KD], cbuf, dsc) for i in range(4)]
        KVc = {"buf": cbuf}
        KVc["g"] = load_vec(kb, "kvg_sb", kvg[:, :], [128, KD], cbuf, dsc)
        KVc["kn2"] = load_vec(kb, "kn2_sb", kn2[:, :], [128, 1], cbuf, dsc)
        AC = {"buf": cbuf}
        AC["g"] = load_vec(kb, "gvat", gat[:, :], [128, KD], cbuf, dsc)
        AC["qn2"] = load_vec(kb, "qn2_sb", qn2[:, :], [128, 1], cbuf, dsc)
        AC["esink"] = load_vec(kb, "esink", sinkrow[:, :], [128, 16], cbuf, dsc)
        BD = make_BD(kb, "BD", cbuf)
        KVc["BD"] = BD; AC["BD"] = BD
        H = kb.sb("h", [128, KD, T], F32); Hb = Buf("h"); Hs = kb.dsem("ds_h")
        osem = kb.dsem("ds_o")
        FS = [FFNSet(kb, nc, f"f{i}", *wF[i]) for i in range(4)]
        ws = WScr(kb, nc)
        W8 = [kb.sb(f"w8_{i}", [128, 4096], BF16) for i in range(2)]
        FS[0].convert(kb)
        items = []
        for t in range(ntiles - 1):
            items += mamba_items_tile(W8, dr["w_in"], None, "state", ws)
        items += mamba_items_tile(W8, dr["w_in"], dr["w_out"], "full", ws) + kv_items(W8, w_kv, ws)
        for t in range(ntiles):
            items += mamba_items_tile(W8, dr["w_in"], dr["w_out"], "full", ws) + kv_items(W8, w_kv, ws)
        for t in range(ntiles):
            a, b = attn_items(W8, w_q, w_o, ws)
            items += a + b
        tw = [FS[0]] * ntiles + [FS[1]]
        for t in range(ntiles):
            tw += [FS[0], FS[1]]
        for t in range(ntiles):
            tw += [FS[2], FS[3]]
        r13, r2 = ffn_rings_bf16(kb, "a", tw)
        ring = Ring(kb, "w8", 2, items)
        xpv = xTp.rearrange("(k p) t -> p k t", p=128)
        xv = xT.rearrange("(k p) t -> p k t", p=128)
        h3v = h3d.rearrange("(k p) t -> p k t", p=128)
        yv = outT.rearrange("(k p) t -> p k t", p=128)
        with contextlib.ExitStack() as stM:
            kb.stack = stM
            P = MambaP(kb, dr, True)
            for g in range(NG):
                kb.op("pool", lambda e, g=g: e.memset(P.S[:, g * 512:(g + 1) * 512], 0.0), writes=[P.Sbuf[g]])
                kb.op("pool", lambda e, g=g: e.memset(P.Sb[:, g * 512:(g + 1) * 512], 0.0), writes=[P.Sbb[g]])
            def load_prev(t):
                kb.dma("sp", H[:], xpv[:, :, t * T:(t + 1) * T], Hs, writes=[Hb])

            def load_own(t):
                kb.dma("sp", H[:], xv[:, :, t * T:(t + 1) * T], Hs, writes=[Hb])
            bg = FS[1].steps(kb) + list(ws.pending) + FS[2].steps(kb) + FS[3].steps(kb)
            ws.pending = []
            nbg = 6 if ntiles >= 8 else max(6, -(-len(bg) // max(ntiles - 2, 1)))

            def trickle(n=None):
                for _ in range(nbg if n is None else n):
                    if bg:
                        bg.pop(0)()
            load_prev(0)
            for t in range(ntiles):
                if t >= 1 or ntiles == 1:
                    trickle()
                ffn(kb, R, H, Hb, gv[0], cbuf, r13, r2, T)
                if t < ntiles - 1:
                    mamba_tile(kb, P, R, H, Hb, T, "state", ring, W8, first=(t == 0), after_norm=lambda t=t: load_prev(t + 1))
                else:
                    for g in range(NG):
                        kb.op("act", lambda e, g=g: e.activation(out=P.Sb[:, g * 512:(g + 1) * 512], in_=P.S[:, g * 512:(g + 1) * 512], func=AF.Copy),
                              reads=[P.Sbuf[g]], writes=[P.Sbb[g]])
                    mamba_tile(kb, P, R, H, Hb, T, "full", ring, W8, first=(t == 0))
                    assert len(bg) <= len(FS[2].steps(kb)) + len(FS[3].steps(kb)) + 8, "FS[1] must be converted before its first use"
                    ffn(kb, R, H, Hb, gv[1], cbuf, r13, r2, T)
                    kv_tile(kb, R, KVc, H, Hb, ring, W8, KTd, Vd, -1, osem, kvdb, after_norm=lambda: load_own(0))
            for g in range(NG):
                gs = slice(g * 512, (g + 1) * 512)
                kb.op("dve", lambda e, gs=gs: e.tensor_scalar(out=P.S[:, gs], in0=P.S[:, gs], scalar1=P.isec[:, 0:1], scalar2=None, op0=ALU.mult),
                      reads=[P.Sbuf[g], P.cb_], writes=[P.Sbuf[g]])
                kb.op("dve", lambda e, gs=gs: e.tensor_scalar(out=P.Sb[:, gs], in0=P.S[:, gs], scalar1=1.0, scalar2=None, op0=ALU.mult),
                      reads=[P.Sbuf[g]], writes=[P.Sbb[g]])
            kb.op("dve", lambda e: e.tensor_scalar(out=P.HAL[:], in0=P.HAL[:], scalar1=P.isec[:, 0:1], scalar2=None, op0=ALU.mult),
                  reads=[P.halb, P.cb_], writes=[P.halb])
            for t in range(ntiles):
                trickle(len(bg) if t == ntiles - 1 else None)
                ffn(kb, R, H, Hb, gv[0], cbuf, r13, r2, T)
                mamba_tile(kb, P, R, H, Hb, T, "full", ring, W8)
                ffn(kb, R, H, Hb, gv[1], cbuf, r13, r2, T)
                kb.dma("sp", h3v[:, :, t * T:(t + 1) * T], H[:], Hs, reads=[Hb], writes=[h3b], part=(t > 0))
                kv_tile(kb, R, KVc, H, Hb, ring, W8, KTd, Vd, t, osem, kvdb,
                        after_norm=(lambda t=t: load_own(t + 1)) if t + 1 < ntiles else None)
            kb.barrier()
            kb.stack = st
        AC["kvbuf"] = Buf("kv")
        AC["KT"] = kb.sb("KTs", [128, 2, 128 + ntok], BF16)
        AC["V"] = kb.sb("Vs", [128, nblk, 128], BF16)
        dkv = kb.dsem("ds_kv")
        for kvh in range(2):
            kb.dma("sp", AC["KT"][:, kvh, :], KTd[kvh, :, :], dkv, reads=[kvdb], writes=[AC["kvbuf"]], part=(kvh > 0))
        kb.dma("sp", AC["V"][:], Vd.rearrange("(b p) c -> p b c", p=128), dkv, reads=[kvdb], writes=[AC["kvbuf"]], part=True)
        AC["EB"] = kb.sb("EB", [128, 32, 128], BF16)
        AC["EB0"] = kb.sb("EB0", [128, 32, 128], BF16)
        negc = kb.sb("negc", [128, 1], F32)
        kb.op("pool", lambda e: e.memset(negc[:], -CSH), reads=[cbuf], writes=[cbuf])
        with contextlib.ExitStack() as st2:
            stg = st2.enter_context(nc.sbuf_tensor("bias_stage", [128, 32, 128], F32))
            stb = Buf("stage"); dst_ = kb.dsem("ds_stage")
            for src, dstt in ((biasT, AC["EB"]), (biasT0, AC["EB0"])):
                kb.dma("sp", stg[:], src[:, :, :], dst_, reads=[], writes=[stb])
                kb.op("act", lambda e, dstt=dstt: e.activation(out=dstt[:], in_=stg[:], func=AF.Exp, bias=negc[:], scale=1.0),
                      reads=[stb, cbuf], writes=[cbuf])
            kb.barrier()
        kb.op("act", lambda e: e.activation(out=AC["esink"][:], in_=AC["esink"][:], func=AF.Exp, bias=negc[:], scale=1.0),
              reads=[cbuf], writes=[cbuf])
        AC["onec"] = kb.sb("onec", [128, 1], BF16)
        kb.op("pool", lambda e: e.memset(AC["onec"][:], 1.0), reads=[cbuf], writes=[cbuf])
        identf = kb.sb("identf", [128, 128], F32)
        AC["ident"] = kb.sb("identb", [128, 128], BF16)
        kb.op("pool", lambda e: e.memset(identf[:], 1.0), reads=[cbuf], writes=[cbuf])
        kb.op("pool", lambda e: e.affine_select(out=identf[:], in_=identf[:], pattern=[[-1, 128]],
                                                 compare_op=ALU.is_equal, fill=0.0, base=0, channel_multiplier=1),
              reads=[cbuf], writes=[cbuf])
        kb.op("pool", lambda e: e.tensor_copy(out=AC["ident"][:], in_=identf[:]), reads=[cbuf], writes=[cbuf])
        H2 = kb.sb("h2", [128, KD, T], F32); H2b = Buf("h2"); Hs2 = kb.dsem("ds_h2")
        HH = [(H, Hb, Hs), (H2, H2b, Hs2)]

        def load_l1(t):
            h_, hb_, hs_ = HH[t % 2]
            kb.dma("sp", h_[:], h3v[:, :, t * T:(t + 1) * T], hs_, reads=[h3b], writes=[hb_])
        load_l1(0)
        for t in range(ntiles):
            h_, hb_, hs_ = HH[t % 2]
            if t + 1 < ntiles:
                load_l1(t + 1)
            ffn(kb, R, h_, hb_, gv[2], cbuf, r13, r2, T)
            attn_tile(kb, R, AC, h_, hb_, ring, W8, t)
            ffn(kb, R, h_, hb_, gv[3], cbuf, r13, r2, T)
            kb.dma("sp", yv[:, :, t * T:(t + 1) * T], h_[:], hs_, reads=[hb_])
        kb.barrier()
        print("F: waits", kb.nwait, kb.ecnt)
    return nc

import numpy as np
def chunkvec(v, nch):
    return np.ascontiguousarray(v.reshape(nch, 128).T).astype(np.float32)
def prep_common(I):
    c = {}
    c["ssm_g"] = chunkvec(I["ssm_norm"][0], 8)
    c["cw"] = np.ascontiguousarray(I["ssm_conv_w"][0].reshape(4, 24, 128).transpose(2, 1, 0)).astype(np.float32)
    c["cb"] = chunkvec(I["ssm_conv_b"][0], 24)
    c["dtb"] = I["ssm_dt_bias"][0].reshape(32, 1).astype(np.float32)
    c["alog"] = I["ssm_a_log"][0].reshape(32, 1).astype(np.float32)
    c["w_in"] = np.ascontiguousarray(I["ssm_w_in"][0])
    c["dch"] = chunkvec(np.repeat(I["ssm_d"][0], 64), 16)
    c["gn"] = chunkvec(I["ssm_gate_norm"][0], 16)
    c["w_out"] = np.ascontiguousarray(I["ssm_w_out"][0])
    return c
import math
def t5_bucket_np(dist):
    n = np.maximum(dist, 0); me = 16
    nf = np.maximum(n, 1).astype(np.float32)
    large = me + (np.log(nf / me) / math.log(128 / me) * (32 - me)).astype(np.int32)
    large = np.minimum(large, 31)
    return np.where(n < me, n, large)
def prep_bias(rel_bias, first_valid):
    qi = np.arange(128)[:, None] + 128; kj = np.arange(256)[None, :]
    dist = qi - kj
    valid = (dist >= 0) & (dist < 128)
    bias = rel_bias[t5_bucket_np(dist)]
    out = np.empty((128, 32, 128), np.float32)
    for blk in range(2):
        for kvh in range(2):
            for par in range(2):
                for jj in range(4):
                    h = kvh * 8 + 2 * jj + par
                    idx = ((blk * 2 + kvh) * 2 + par) * 4 + jj
                    b = bias[:, blk * 128:(blk + 1) * 128, h]
                    v = valid[:, blk * 128:(blk + 1) * 128]
                    if blk == 0 and not first_valid:
                        v = np.zeros_like(v)
                    out[:, idx, :] = np.where(v, b, np.float32(-30000.0)).T
    return out


_NC_CACHE = {}


def kernel(**I):
    I = {k: np.asarray(v) for k, v in I.items()}
    c = prep_common(I)
    x = I["x"].astype(np.float32)
    n = 8
    cores = list(range(n))
    f32c = lambda a: np.ascontiguousarray(a, dtype=np.float32)
    if "F" not in _NC_CACHE:
        _NC_CACHE["F"] = build_fused()
    rb = I["rel_bias"].astype(np.float32)
    bias1 = prep_bias(rb, True)
    bias0 = prep_bias(rb, False)
    shared = {"kvg": chunkvec(I["kv_norm"], 8), "kn2": np.tile(I["k_norm"], 2).reshape(128, 1).astype(np.float32),
              "w_kv": f32c(I["w_kv"]), "gat": chunkvec(I["attn_norm"][0], 8), "w_q": f32c(I["w_q"][0]), "w_o": f32c(I["w_o"][0]),
              "qn2": np.tile(I["q_norm"][0], 2).reshape(128, 1).astype(np.float32),
              "sinkrow": np.ascontiguousarray(np.broadcast_to(I["sinks"][0][None, :], (128, 16))).astype(np.float32),
              "biasT": bias1}
    for i, (l, w) in enumerate([(0, 0), (0, 1), (1, 0), (1, 1)]):
        shared[f"gF{i}"] = chunkvec(I["ffn_norm"][l, w], 8)
        shared[f"wF{i}_0"] = f32c(I["ffn_w1"][l, w])
        shared[f"wF{i}_1"] = f32c(I["ffn_w3"][l, w])
        shared[f"wF{i}_2"] = f32c(I["ffn_w2"][l, w])
    for k in ["ssm_g", "cw", "cb", "dtb", "alog", "w_in", "dch", "gn", "w_out"]:
        shared[k] = c[k]
    maps = []
    for core in cores:
        b, hf = core // 2, core % 2
        m = dict(shared)
        m["xTp"] = f32c(x[b, 0:NTOK].T) if hf else np.zeros((D, NTOK), np.float32)
        m["xT"] = f32c(x[b, hf * NTOK:(hf + 1) * NTOK].T)
        m["biasT0"] = bias1 if hf else bias0
        m["isec"] = np.full((128, 1), float(hf), np.float32)
        maps.append(m)
    res = run_bass_kernel_spmd(_NC_CACHE["F"], maps, core_ids=cores).results
    out = np.empty((4, 2 * NTOK, D), np.float32)
    for core in cores:
        b, hf = core // 2, core % 2
        out[b, hf * NTOK:(hf + 1) * NTOK] = np.asarray(res[core]["outT"]).T
    return out
```

```python
import contextlib
import math
import ml_dtypes
from concourse.bass_utils import run_bass_kernel_spmd
import numpy as np
import concourse.bass as bass
import concourse.mybir as mybir

F32 = mybir.dt.float32
BF16 = mybir.dt.bfloat16
AF = mybir.ActivationFunctionType
ALU = mybir.AluOpType


_UNIQ = [0]


def uniq(n):
    _UNIQ[0] += 1
    return f"{n}_{_UNIQ[0]}"


class Buf:
    __slots__ = ("name", "w", "r")

    def __init__(self, name):
        self.name = name
        self.w = {}
        self.r = {}


class DSem:
    def __init__(self, sem, key):
        self.sem = sem
        self.key = key
        self.cnt = 0


class KB:
    SAME_ENG_SYNC = True
    NOSYNC_SAME = ('pe',)

    def __init__(self, nc, stack):
        self.nc = nc
        self.stack = stack
        self.root = stack
        self.eng = {"pe": nc.tensor, "act": nc.scalar, "dve": nc.vector,
                    "pool": nc.gpsimd, "sp": nc.sync}
        self.esem = {}
        self.ecnt = {}
        self.seen = {}
        for e in self.eng:
            self.esem[e] = stack.enter_context(nc.semaphore("es_" + e))
            self.ecnt[e] = 0
            self.seen[e] = {}
        self.dsems = []
        self.nwait = 0

    def sb(self, name, shape, dt):
        return self.stack.enter_context(self.nc.sbuf_tensor(name, list(shape), dt))

    def ps(self, name, shape, dt):
        return self.stack.enter_context(self.nc.psum_tensor(name, list(shape), dt))

    def dsem(self, name, in_barrier=True):
        s = self.root.enter_context(self.nc.semaphore(name))
        d = DSem(s, name)
        d.in_barrier = in_barrier
        self.dsems.append(d)
        return d

    def _waits(self, e, toks):
        need = {}
        for (key, sem, val, src) in toks:
            if src == e and (e in self.NOSYNC_SAME or not self.SAME_ENG_SYNC):
                continue
            if key not in need or need[key][1] < val:
                need[key] = (sem, val, src)
        for key, (sem, val, src) in need.items():
            if self.seen[e].get(key, 0) >= val:
                continue
            if src is not None and val > self.ecnt[src]:
                raise RuntimeError(f"unresolved lazy token {key}={val} (cnt {self.ecnt[src]}) waited by {e}")
            self.eng[e].wait_ge(sem, val)
            self.nwait += 1
            self.seen[e][key] = val

    @staticmethod
    def _gather(reads, writes):
        toks = []
        for b in reads:
            toks += [(k,) + v for k, v in b.w.items()]
        for b in writes:
            toks += [(k,) + v for k, v in b.w.items()]
            toks += [(k,) + v for k, v in b.r.items()]
        return toks

    @staticmethod
    def _update(tok, key, reads, writes, part=False):
        for b in writes:
            if part:
                b.w[key] = tok
            else:
                b.w = {key: tok}
                b.r = {}
        for b in reads:
            if b in writes:
                continue
            old = b.r.get(key)
            if old is None or old[1] < tok[1]:
                b.r[key] = tok

    def op(self, e, fn, reads=(), writes=(), inc=True):
        self._waits(e, self._gather(reads, writes))
        ins = fn(self.eng[e])
        key = "es_" + e
        if inc:
            self.ecnt[e] += 1
            ins.then_inc(self.esem[e], 1)
            tok = (self.esem[e], self.ecnt[e], e)
        else:
            tok = (self.esem[e], self.ecnt[e] + 1, e)
        self._update(tok, key, reads, writes)
        return ins

    def dma(self, q, out, in_, ds, reads=(), writes=(), part=False, **kw):
        self._waits(q, self._gather(reads, writes))
        ins = self.eng[q].dma_start(out=out, in_=in_, **kw)
        ds.cnt += 16
        ins.then_inc(ds.sem, 16)
        tok = (ds.sem, ds.cnt, None)
        self._update(tok, ds.key, reads, writes, part=part)
        return ins

    def barrier(self):
        for e in self.eng:
            toks = []
            for x in self.eng:
                if x != e and self.ecnt[x] > 0:
                    toks.append(("es_" + x, self.esem[x], self.ecnt[x], x))
            for d in self.dsems:
                if d.cnt > 0 and d.in_barrier:
                    toks.append((d.key, d.sem, d.cnt, None))
            self._waits(e, toks)


class Ring:
    def __init__(self, kb, name, depth, items):
        self.kb = kb
        self.depth = depth
        self.items = items
        self.bufs = [Buf(f"{name}{i}") for i in range(depth)]
        self.sems = [kb.dsem(f"ds_{name}{i}", in_barrier=False) for i in range(depth)]
        self.issued = 0
        self.taken = 0

    def _issue_upto(self, n):
        while self.issued < min(n, len(self.items)):
            i = self.issued
            s = i % self.depth
            self.items[i](self.kb, s, self.bufs[s], self.sems[s])
            self.issued += 1

    def get(self):
        i = self.taken
        self._issue_upto(i + 1)
        self.taken += 1
        s = i % self.depth
        return s, self.bufs[s]

    def prefetch(self):
        self._issue_upto(self.taken + self.depth - 1)


D = 1024
KD = 8
DFF = 2816
KF = 22
EPS = 1e-6


def ffn_rings(kb, name, tiles_w):
    W13 = [kb.sb(f"{name}w13_{i}", [128, 2, KD, 256], BF16) for i in range(3)]
    W2 = [kb.sb(f"{name}w2_{i}", [128, KF, 256], BF16) for i in range(2)]
    it13, it2 = [], []
    for (w1, w3, w2) in tiles_w:
        w1v = w1.rearrange("(k p) n -> p k n", p=128)
        w3v = w3.rearrange("(k p) n -> p k n", p=128)
        w2v = w2.rearrange("(j p) n -> p j n", p=128)
        for jp in range(KF // 2):
            def f(kb, s, buf, ds, jp=jp, w1v=w1v, w3v=w3v):
                kb.dma("pool", W13[s][:, 0], w1v[:, :, jp * 256:(jp + 1) * 256], ds, writes=[buf])
                kb.dma("pool", W13[s][:, 1], w3v[:, :, jp * 256:(jp + 1) * 256], ds, writes=[buf], part=True)
            it13.append(f)
        for mb in range(4):
            def f2(kb, s, buf, ds, mb=mb, w2v=w2v):
                kb.dma("pool", W2[s][:], w2v[:, :, mb * 256:(mb + 1) * 256], ds, writes=[buf])
            it2.append(f2)
    r13 = Ring(kb, name + "r13", 3, it13)
    r2 = Ring(kb, name + "r2", 2, it2)
    r13.tiles = W13
    r2.tiles = W2
    return r13, r2


class FFNRes:
    def __init__(self, kb, T, name="f"):
        self.T = T
        self.sq = kb.sb(name + "sq", [128, KD, T], BF16)
        self.sqb = [Buf("sq0"), Buf("sq1")]
        self.uT = kb.sb(name + "uT", [128, KD, T], BF16)
        self.ub = [Buf(f"uT{k}") for k in range(KD)]
        self.rstd = kb.sb(name + "rstd", [128, T], F32)
        self.rb = Buf("rstd")
        self.ones = kb.sb(name + "ones", [128, 128], BF16)
        self.onesb = Buf("ones")
        kb.op("pool", lambda e: e.memset(self.ones[:], 1.0), writes=[self.onesb])
        self.epsb = kb.sb(name + "eps", [128, 1], F32)
        self.epsbuf = Buf("eps")
        kb.op("pool", lambda e: e.memset(self.epsb[:], EPS), writes=[self.epsbuf])


def rmsnorm_T(kb, R, hT, hb, gvec, gb, PN, PNb, T):
    for hf in range(2):
        kb.op("act", lambda e, hf=hf: e.activation(out=R.sq[:, hf * 4:hf * 4 + 4, :T], in_=hT[:, hf * 4:hf * 4 + 4, :T], func=AF.Square),
              reads=[hb], writes=[R.sqb[hf]])
    for k in range(KD):
        kb.op("pe", lambda e, k=k: e.matmul(PN[:, :T], R.ones[:], R.sq[:, k, :T], start=(k == 0), stop=(k == KD - 1)),
              reads=[R.sqb[k // 4], R.onesb], writes=[PNb], inc=(k == KD - 1))
    kb.op("act", lambda e: e.activation(out=R.rstd[:, :T], in_=PN[:, :T], func=AF.Sqrt, bias=R.epsb[:], scale=1.0 / D),
          reads=[PNb, R.epsbuf], writes=[R.rb])
    kb.op("dve", lambda e: e.reciprocal(PN[:, :T], R.rstd[:, :T]), reads=[R.rb, PNb], writes=[PNb])
    for k in range(KD):
        kb.op("dve", lambda e, k=k: e.scalar_tensor_tensor(out=R.uT[:, k, :T], in0=hT[:, k, :T], scalar=gvec[:, k:k + 1],
                                                            in1=PN[:, :T], op0=ALU.mult, op1=ALU.mult),
              reads=[hb, gb, PNb], writes=[R.ub[k]])


def ffn(kb, R, hT, hb, gvec, gb, r13, r2, T):
    with contextlib.ExitStack() as st:
        _ffn(kb, st, R, hT, hb, gvec, gb, r13, r2, T)
        kb.barrier()


def _ffn(kb, st, R, hT, hb, gvec, gb, r13, r2, T):
    sb = lambda n, s, d: st.enter_context(kb.nc.sbuf_tensor(uniq(n), list(s), d))
    ps = lambda n, s, d: st.enter_context(kb.nc.psum_tensor(uniq(n), list(s), d))
    PS = {"n": ps("pn", [128, 512], F32), "nb": Buf("pn"),
          "h1": [ps(f"ph1{i}", [128, 512], F32) for i in range(2)], "h1b": [Buf(f"ph1{i}") for i in range(2)],
          "h3": [ps(f"ph3{i}", [128, 512], F32) for i in range(2)], "h3b": [Buf(f"ph3{i}") for i in range(2)],
          "o": [ps(f"po{i}", [128, 512], F32) for i in range(2)], "ob": [Buf(f"po{i}") for i in range(2)]}
    gT = sb("f_gT", [128, KF, T], BF16)
    gbufs = [Buf(f"g{j}") for j in range(KF)]
    s1 = [sb(f"f_s1_{i}", [128, T], F32) for i in range(2)]
    s1b = [Buf(f"s1_{i}") for i in range(2)]
    rmsnorm_T(kb, R, hT, hb, gvec, gb, PS["n"], PS["nb"], T)
    for jp in range(KF // 2):
        s, wb = r13.get()
        W = r13.tiles[s]
        for jj in range(2):
            j = jp * 2 + jj
            pi = j % 2
            P1, P1b = PS["h1"][pi], PS["h1b"][pi]
            P3, P3b = PS["h3"][pi], PS["h3b"][pi]
            for k in range(KD):
                kb.op("pe", lambda e, k=k: e.matmul(P1[:, :T], W[:, 0, k, jj * 128:(jj + 1) * 128], R.uT[:, k, :T],
                                                     start=(k == 0), stop=(k == KD - 1)),
                      reads=[wb, R.ub[k]], writes=[P1b], inc=(k == KD - 1))
            for k in range(KD):
                kb.op("pe", lambda e, k=k: e.matmul(P3[:, :T], W[:, 1, k, jj * 128:(jj + 1) * 128], R.uT[:, k, :T],
                                                     start=(k == 0), stop=(k == KD - 1)),
                      reads=[wb, R.ub[k]], writes=[P3b], inc=(k == KD - 1))
            kb.op("act", lambda e: e.activation(out=s1[pi][:, :T], in_=P1[:, :T], func=AF.Silu),
                  reads=[P1b], writes=[s1b[pi]])
            kb.op("dve", lambda e: e.tensor_tensor(out=gT[:, j, :T], in0=P3[:, :T], in1=s1[pi][:, :T], op=ALU.mult),
                  reads=[P3b, s1b[pi]], writes=[gbufs[j]])
        r13.prefetch()
    for m in range(KD):
        s, wb = r2.get()
        W = r2.tiles[s]
        pi = m % 2
        PO, POb = PS["o"][pi], PS["ob"][pi]
        for j in range(KF):
            kb.op("pe", lambda e, j=j: e.matmul(PO[:, :T], W[:, j, :], gT[:, j, :T],
                                                 start=(j == 0), stop=(j == KF - 1)),
                  reads=[wb, gbufs[j]], writes=[POb], inc=(j == KF - 1))
        kb.op("dve", lambda e, m=m: e.scalar_tensor_tensor(out=hT[:, m, :T], in0=PO[:, :T], scalar=0.5, in1=hT[:, m, :T],
                                                            op0=ALU.mult, op1=ALU.add),
              reads=[POb, hb], writes=[hb])
        r2.prefetch()


class FFNSet:
    def __init__(self, kb, nc, name, w1, w3, w2):
        self.s13 = nc.dram_tensor(name + "_s13", [KF // 2, 128, 2, KD, 256], BF16, kind="Internal").ap()
        self.s2 = nc.dram_tensor(name + "_s2", [8, 128, KF, 128], BF16, kind="Internal").ap()
        self.bufs = [Buf(f"{name}_scr{i}") for i in range(3)]
        self.dss = [kb.dsem(f"ds_cv_{name}_{i}", in_barrier=False) for i in range(3)]
        self.w = (w1, w3, w2)

    def grp13(self, jp):
        return 0 if jp < 3 else 1

    def steps(self, kb):
        w1, w3, w2 = self.w
        w1v = w1.rearrange("(k p) n -> p k n", p=128)
        w3v = w3.rearrange("(k p) n -> p k n", p=128)
        w2v = w2.rearrange("(j p) n -> p j n", p=128)
        out = []
        for jp in range(KF // 2):
            gi = self.grp13(jp)
            out.append(lambda jp=jp, gi=gi: kb.dma("pool", self.s13[jp, :, 0, :, :], w1v[:, :, jp * 256:(jp + 1) * 256], self.dss[gi], writes=[self.bufs[gi]], part=True))
            out.append(lambda jp=jp, gi=gi: kb.dma("pool", self.s13[jp, :, 1, :, :], w3v[:, :, jp * 256:(jp + 1) * 256], self.dss[gi], writes=[self.bufs[gi]], part=True))
        for mb in range(8):
            out.append(lambda mb=mb: kb.dma("pool", self.s2[mb, :, :, :], w2v[:, :, mb * 128:(mb + 1) * 128], self.dss[2], writes=[self.bufs[2]], part=True))
        return out

    def convert(self, kb):
        for s in self.steps(kb):
            s()


def ffn_rings_bf16(kb, name, passes, q13="sp", q2="sp"):
    W13 = [kb.sb(f"{name}w13_{i}", [128, 2, KD, 256], BF16) for i in range(3)]
    W2 = [kb.sb(f"{name}w2_{i}", [128, KF, 128], BF16) for i in range(2)]
    it13, it2 = [], []
    for fs in passes:
        for jp in range(KF // 2):
            def f(kb, s, buf, ds, jp=jp, fs=fs):
                b_ = fs.bufs[fs.grp13(jp)]
                assert b_.w, "FFN weight block used before its conversion was emitted"
                kb.dma(q13, W13[s][:], fs.s13[jp], ds, reads=[b_], writes=[buf])
            it13.append(f)
        for mb in range(8):
            def f2(kb, s, buf, ds, mb=mb, fs=fs):
                assert fs.bufs[2].w, "FFN weight block used before its conversion was emitted"
                kb.dma(q2, W2[s][:], fs.s2[mb], ds, reads=[fs.bufs[2]], writes=[buf])
            it2.append(f2)
    r13 = Ring(kb, name + "r13", 3, it13)
    r2 = Ring(kb, name + "r2", 2, it2)
    r13.tiles = W13
    r2.tiles = W2
    return r13, r2


NH = 32
HP = 64
NG = 4
DI = 2048
C_Z, C_X, C_B, C_C, C_DT = 0, 2048, 4096, 4608, 5120


class WScr:
    def __init__(self, kb, nc):
        self.kb, self.nc = kb, nc
        self.blocks = {}
        self.pending = []

    def item(self, W8, key, parts, nelem, k, defer=False):
        kb = self.kb
        if key not in self.blocks:
            ap = self.nc.dram_tensor("wscr_" + key, [128, 4096], BF16, kind="Internal").ap()
            buf = Buf("wscr_" + key)
            ds = kb.dsem("ds_ws_" + key, in_barrier=False)

            def conv():
                first = True
                for (off, n, src) in parts:
                    dst = ap[:, 0:nelem].rearrange("p (k n) -> p k n", k=k)[:, :, off:off + n]
                    kb.dma("pool", dst, src, ds, writes=[buf], part=not first)
                    first = False
            self.blocks[key] = (ap, buf)
            if defer:
                self.pending.append(conv)
            else:
                conv()
        ap, sbuf = self.blocks[key]

        def f(kb, s, buf, ds):
            assert sbuf.w, "weight block used before its conversion was emitted: " + key
            kb.dma("pool", W8[s][:, 0:nelem], ap[:, 0:nelem], ds, reads=[sbuf], writes=[buf])
        return f

    def flush(self):
        while self.pending:
            self.pending.pop(0)()


def mamba_items_tile(W8, w_in, w_out, mode, ws):
    wv = w_in.rearrange("(k p) n -> p k n", p=128)
    items = []
    full = mode == "full"
    def xi(g):
        return ws.item(W8, f"x{g}", [(0, 512, wv[:, :, C_X + g * 512:C_X + (g + 1) * 512])], 8 * 512, 8)

    def bci(g):
        return ws.item(W8, f"bc{g}", [(0, 128, wv[:, :, C_B + g * 128:C_B + (g + 1) * 128]),
                                      (128, 128, wv[:, :, C_C + g * 128:C_C + (g + 1) * 128])], 8 * 256, 8)

    def zi(g):
        return ws.item(W8, f"z{g}", [(0, 512, wv[:, :, C_Z + g * 512:C_Z + (g + 1) * 512])], 8 * 512, 8)
    items += [xi(0), bci(0), ws.item(W8, "dt", [(0, 32, wv[:, :, C_DT:C_DT + 32])], 8 * 32, 8)]
    for g in range(NG):
        if g + 1 < NG:
            items += [xi(g + 1), bci(g + 1)]
        if full:
            items.append(zi(g))
    if full:
        wo = w_out.rearrange("(c p) n -> p c n", p=128)
        for mb in range(4):
            items.append(ws.item(W8, f"o{mb}", [(0, 256, wo[:, :, mb * 256:(mb + 1) * 256])], 16 * 256, 16))
    return items


class MambaP:
    def __init__(self, kb, dr, full):
        self.full = full
        sb = kb.sb
        self.cb_ = Buf("mconst")
        cbuf = self.cb_
        dsc = kb.dsem("ds_mc")
        self.g = sb("m_g", [128, KD], F32)
        self.cw = sb("m_cw", [128, 24, 4], F32)
        self.cb = sb("m_cb", [128, 24], F32)
        self.dtb = sb("m_dtb", [32, 1], F32)
        self.a = sb("m_a", [32, 1], F32)
        self.isec = sb("m_isec", [128, 1], F32)
        kb.dma("sp", self.g[:], dr["ssm_g"][:, :], dsc, writes=[cbuf])
        kb.dma("sp", self.cw[:], dr["cw"][:, :, :], dsc, writes=[cbuf], part=True)
        kb.dma("sp", self.cb[:], dr["cb"][:, :], dsc, writes=[cbuf], part=True)
        kb.dma("sp", self.dtb[:], dr["dtb"][:, :], dsc, writes=[cbuf], part=True)
        kb.dma("sp", self.a[:], dr["alog"][:, :], dsc, writes=[cbuf], part=True)
        kb.dma("sp", self.isec[:], dr["isec"][:, :], dsc, writes=[cbuf], part=True)
        if full:
            self.dch = sb("m_dch", [128, 16], F32)
            self.gn = sb("m_gn", [128, 16], F32)
            kb.dma("sp", self.dch[:], dr["dch"][:, :], dsc, writes=[cbuf], part=True)
            kb.dma("sp", self.gn[:], dr["gn"][:, :], dsc, writes=[cbuf], part=True)
        kb.op("act", lambda e: e.activation(out=self.a[:], in_=self.a[:], func=AF.Exp), reads=[cbuf], writes=[cbuf])
        kb.op("dve", lambda e: e.tensor_scalar(out=self.a[:], in0=self.a[:], scalar1=-1.0, scalar2=None, op0=ALU.mult),
              reads=[cbuf], writes=[cbuf])
        self.identf = sb("m_idf", [128, 128], F32)
        self.ident = sb("m_idb", [128, 128], BF16)
        self.tri = sb("m_tri", [128, 128], F32)
        self.sel = sb("m_sel", [128, 128], F32)
        self.kb_ = Buf("mk")
        k_ = self.kb_
        kb.op("pool", lambda e: e.memset(self.identf[:], 1.0), writes=[k_])
        kb.op("pool", lambda e: e.affine_select(out=self.identf[:], in_=self.identf[:], pattern=[[-1, 128]],
                                                 compare_op=ALU.is_equal, fill=0.0, base=0, channel_multiplier=1),
              reads=[k_], writes=[k_])
        kb.op("pool", lambda e: e.tensor_copy(out=self.ident[:], in_=self.identf[:]), reads=[k_], writes=[k_])
        kb.op("pool", lambda e: e.memset(self.tri[:], 1.0), reads=[k_], writes=[k_])
        kb.op("pool", lambda e: e.affine_select(out=self.tri[:], in_=self.tri[:], pattern=[[1, 128]],
                                                 compare_op=ALU.is_ge, fill=0.0, base=0, channel_multiplier=-1),
              reads=[k_], writes=[k_])
        kb.op("pool", lambda e: e.memset(self.sel[:], 1.0), reads=[k_], writes=[k_])
        kb.op("pool", lambda e: e.affine_select(out=self.sel[:], in_=self.sel[:], pattern=[[0, 128]],
                                                 compare_op=ALU.is_equal, fill=0.0, base=-127, channel_multiplier=1),
              reads=[k_], writes=[k_])
        self.DG = sb("m_DG", [128, 6 * 4, 128], BF16)
        self.DGb = Buf("DG")
        self.S = sb("m_S", [128, DI], F32)
        self.Sb = sb("m_Sb", [128, DI], BF16)
        self.Sbuf = [Buf(f"S{g}") for g in range(NG)]
        self.Sbb = [Buf(f"Sb{g}") for g in range(NG)]
        self.HAL = sb("m_hal", [128, 24, 4], BF16)
        self.halb = Buf("hal")
        self.acd = dr.get("acum_scr")
        self.acdb = Buf("acd")
        self.acds = kb.dsem("ds_acd")
        self.abcs = [kb.dsem(f"ds_abc{i}") for i in range(2)]


def mamba_tile(kb, P, R, hT, hb, T, mode, ring, W8, first=False, after_norm=None):
    full = mode == "full"
    NB = max(T // 128, 1)
    with contextlib.ExitStack() as st:
        sb = lambda n, s, d: st.enter_context(kb.nc.sbuf_tensor(uniq(n), list(s), d))
        ps = lambda n, s, d: st.enter_context(kb.nc.psum_tensor(uniq(n), list(s), d))
        PA = [ps(f"mpa{i}", [128, 512], F32) for i in range(2)]
        PAb = [Buf(f"mpa{i}") for i in range(2)]
        pai = [0]

        def nextpa():
            i = pai[0] % 2
            pai[0] += 1
            return PA[i], PAb[i]

        PT = ps("mpt", [128, 512], F32)
        PTb = Buf("mpt")
        rmsnorm_T(kb, R, hT, hb, P.g, P.cb_, PT, PTb, T)
        uT, ub = R.uT, R.ub
        if after_norm is not None:
            after_norm()

        XB = sb("m_XB", [128, 6, 4 + 512], BF16)
        XBb = Buf("XB")
        XC = sb("m_XC", [128, 6, 512], BF16)
        XCb = Buf("XC")

        def stageA(g):
            chunks = [g * 4 + i for i in range(4)] + [16 + g, 20 + g]
            if first:
                kb.op("dve", lambda e: e.memset(XB[:, :, 0:4], 0.0), writes=[XBb])
            else:
                kb.op("dve", lambda e: e.tensor_copy(out=XB[:, 0:4, 0:4], in_=P.HAL[:, g * 4:g * 4 + 4, :]), reads=[P.halb], writes=[XBb])
                kb.op("dve", lambda e: e.tensor_copy(out=XB[:, 4, 0:4], in_=P.HAL[:, 16 + g, :]), reads=[P.halb], writes=[XBb])
                kb.op("dve", lambda e: e.tensor_copy(out=XB[:, 5, 0:4], in_=P.HAL[:, 20 + g, :]), reads=[P.halb], writes=[XBb])
            s, wb = ring.get()
            Wx = W8[s][:, 0:8 * 512].rearrange("p (k n) -> p k n", k=8)
            for i in range(4):
                Pq, Pqb = nextpa()
                for k in range(KD):
                    kb.op("pe", lambda e, k=k, i=i: e.matmul(Pq[:, :T], Wx[:, k, i * 128:(i + 1) * 128], uT[:, k, :T],
                                                              start=(k == 0), stop=(k == KD - 1)),
                          reads=[wb, ub[k]], writes=[Pqb], inc=(k == KD - 1))
                kb.op("act", lambda e, i=i: e.activation(out=XB[:, i, 4:4 + T], in_=Pq[:, :T], func=AF.Copy),
                      reads=[Pqb], writes=[XBb])
            ring.prefetch()
            s, wb = ring.get()
            Wbc = W8[s][:, 0:8 * 256].rearrange("p (k n) -> p k n", k=8)
            for i in range(2):
                Pq, Pqb = nextpa()
                for k in range(KD):
                    kb.op("pe", lambda e, k=k, i=i: e.matmul(Pq[:, :T], Wbc[:, k, i * 128:(i + 1) * 128], uT[:, k, :T],
                                                              start=(k == 0), stop=(k == KD - 1)),
                          reads=[wb, ub[k]], writes=[Pqb], inc=(k == KD - 1))
                kb.op("act", lambda e, i=i: e.activation(out=XB[:, 4 + i, 4:4 + T], in_=Pq[:, :T], func=AF.Copy),
                      reads=[Pqb], writes=[XBb])
            ring.prefetch()
            for (d0, nd, c0) in ((0, 16, g * 4), (16, 4, 16 + g), (20, 4, 20 + g)):
                nch = nd // 4
                kb.op("pool", lambda e, d0=d0, nd=nd, c0=c0, nch=nch: e.tensor_tensor(
                    out=P.DG[:, d0:d0 + nd, :],
                    in0=P.identf[:].unsqueeze(1).to_broadcast([128, nd, 128]),
                    in1=P.cw[:, c0:c0 + nch, :].rearrange("p c k -> p (c k)").unsqueeze(2).to_broadcast([128, nd, 128]), op=ALU.mult),
                      reads=[P.kb_, P.cb_], writes=[P.DGb])

        assert mode != "halo"
        stageA(0)
        if mode != "halo":
            dtT = sb("m_dtT", [32, 512], F32)
            acT = sb("m_acT", [32, 512], F32)
            dtb_ = Buf("dtT")
            acb_ = Buf("acT")
            TK = sb("m_TK", [128, 4, 5, 32], F32)
            TKb = [Buf(f"TK{i}") for i in range(4)]
            s, wb = ring.get()
            Wd = W8[s][:, 0:8 * 32].rearrange("p (k n) -> p k n", k=8)
            Pd, Pdb = nextpa()
            for k in range(KD):
                kb.op("pe", lambda e, k=k: e.matmul(Pd[0:32, :T], Wd[:, k, 0:32], uT[:, k, :T], start=(k == 0), stop=(k == KD - 1)),
                      reads=[wb, ub[k]], writes=[Pdb], inc=(k == KD - 1))
            ring.prefetch()
            kb.op("act", lambda e: e.activation(out=dtT[:, :T], in_=Pd[0:32, :T], func=AF.Exp, bias=P.dtb[:], scale=1.0),
                  reads=[Pdb, P.cb_], writes=[dtb_])
            kb.op("act", lambda e: e.activation(out=dtT[:, :T], in_=dtT[:, :T], func=AF.Ln, bias=1.0, scale=1.0),
                  reads=[dtb_], writes=[dtb_])
            kb.op("dve", lambda e: e.tensor_scalar(out=acT[:, :T], in0=dtT[:, :T], scalar1=P.a[:, 0:1], scalar2=None, op0=ALU.mult),
                  reads=[dtb_, P.cb_], writes=[acb_])
            for tb in range(NB):
                sl = slice(tb * 128, (tb + 1) * 128)
                kb.op("dve", lambda e, sl=sl: e.tensor_tensor_scan(out=acT[:, sl], data0=R.ones[0:32, 0:128], data1=acT[:, sl],
                                                                    initial=0.0, op0=ALU.mult, op1=ALU.add),
                      reads=[acb_, R.onesb], writes=[acb_])
            if full:
                kb.dma("sp", P.acd[:, :T], acT[:, :T], P.acds, reads=[acb_], writes=[P.acdb])
            for tb in range(NB):
                sl = slice(tb * 128, (tb + 1) * 128)
                kb.op("pe", lambda e, sl=sl: e.transpose(PT[:, 0:32], dtT[:, sl], P.identf[0:32, 0:32]),
                      reads=[dtb_, P.kb_], writes=[PTb])
                kb.op("dve", lambda e, tb=tb: e.tensor_copy(out=TK[:, tb, 0, :], in_=PT[:, 0:32]), reads=[PTb], writes=[TKb[tb]])
                kb.op("pe", lambda e, sl=sl: e.transpose(PT[:, 32:64], acT[:, sl], P.identf[0:32, 0:32]),
                      reads=[acb_, P.kb_], writes=[PTb])
                kb.op("dve", lambda e, tb=tb: e.tensor_copy(out=TK[:, tb, 4, :], in_=PT[:, 32:64]), reads=[PTb], writes=[TKb[tb]])
                kb.op("dve", lambda e, tb=tb: e.tensor_scalar(out=TK[:, tb, 1, :], in0=TK[:, tb, 4, :], scalar1=-1.0, scalar2=None, op0=ALU.mult),
                      reads=[TKb[tb]], writes=[TKb[tb]])
                kb.op("act", lambda e, tb=tb: e.activation(out=TK[:, tb, 2, :], in_=TK[:, tb, 4, :], func=AF.Exp),
                      reads=[TKb[tb]], writes=[TKb[tb]])
                kb.op("pe", lambda e, tb=tb: e.matmul(PT[:, 64:96], P.sel[:], TK[:, tb, 4, :], start=True, stop=True),
                      reads=[TKb[tb], P.kb_], writes=[PTb])
                kb.op("dve", lambda e, tb=tb: e.tensor_tensor(out=TK[:, tb, 3, :], in0=PT[:, 64:96], in1=TK[:, tb, 4, :], op=ALU.subtract),
                      reads=[PTb, TKb[tb]], writes=[TKb[tb]])
                kb.op("act", lambda e, tb=tb: e.activation(out=TK[:, tb, 3, :], in_=TK[:, tb, 3, :], func=AF.Exp),
                      reads=[TKb[tb]], writes=[TKb[tb]])
                kb.op("dve", lambda e, tb=tb: e.tensor_tensor(out=TK[:, tb, 3, :], in0=TK[:, tb, 3, :], in1=TK[:, tb, 0, :], op=ALU.mult),
                      reads=[TKb[tb]], writes=[TKb[tb]])
                kb.op("act", lambda e, tb=tb: e.activation(out=TK[:, tb, 4, :], in_=PT[:, 64:96], func=AF.Exp),
                      reads=[PTb, TKb[tb]], writes=[TKb[tb]])
            PX = ps("mpx", [128, 1024], BF16)
            PXb = Buf("mpx")
            xw = sb("m_xw", [128, 4, 512], BF16)
            xwb = Buf("xw")
            Btok = sb("m_Bt", [128, 4, 128], BF16)
            Btb = Buf("Bt")
            t2 = sb("m_t2", [128, 512], F32)
            t2b = Buf("t2")
            if full:
                PYd = [ps(f"mpyd{i}", [128, 512], F32) for i in range(2)]
                PYdb = [Buf(f"mpyd{i}") for i in range(2)]
                PYo = [ps(f"mpyo{i}", [128, 512], F32) for i in range(2)]
                PYob = [Buf(f"mpyo{i}") for i in range(2)]
                SZ = sb("m_SZ", [128, 4, 512], BF16)
                SZb = Buf("SZ")
                xD = sb("m_xD", [128, 4, 512], BF16)
                xDb = Buf("xD")
                xdt = sb("m_xdt", [128, 4, 512], BF16)
                xdtb = Buf("xdt")
                ABC = [sb(f"m_abc{i}", [128, 8, 128], F32) for i in range(2)]
                ABCb = [Buf(f"abc{i}") for i in range(2)]
                dar = [sb(f"m_dar{i}", [128, 4, 128], F32) for i in range(4)]
                darb = [Buf(f"dar{i}") for i in range(4)]
                Ee = [sb(f"m_Ee{i}", [128, 4, 128], BF16) for i in range(4)]
                Eeb = [Buf(f"Ee{i}") for i in range(4)]
                CBm = [sb(f"m_CBm{i}", [128, 128], F32) for i in range(2)]
                CBmb = [Buf(f"CBm{i}") for i in range(2)]
                scT = [sb(f"m_sc{i}", [128, 4, 128], BF16) for i in range(4)]
                scb = [Buf(f"sc{i}") for i in range(4)]
                yg = sb("m_yg", [128, 4, 512], F32)
                ygb = [Buf(f"yg{i}") for i in range(4)]
                t1s = [sb(f"m_t1_{i}", [128, 512], F32) for i in range(2)]
                t1bs = [Buf(f"t1_{i}") for i in range(2)]
                pend = []
                pend2 = []
                ynb = sb("m_ynb", [128, 4, 512], BF16)
                ynbb = Buf("ynb")
                ynT = sb("m_ynT", [128, 16, 512], BF16)
                ynTb = Buf("ynT")
                ss = sb("m_ss", [128, 4], F32)
                ssb = Buf("ss")
                abci = [0]

        norm_pend = []
        for g in range(NG):
            chunks = [g * 4 + i for i in range(4)] + [16 + g, 20 + g]
            for i, c in enumerate(chunks):
                Pq, Pqb = nextpa()
                for k in range(4):
                    kb.op("pe", lambda e, k=k, i=i, c=c: e.matmul(Pq[:, :T], P.DG[:, i * 4 + k, :], XB[:, i, 1 + k:1 + k + T],
                                                                   start=(k == 0), stop=(k == 3)),
                          reads=[XBb, P.DGb], writes=[Pqb], inc=(k == 3))
                kb.op("act", lambda e, i=i, c=c: e.activation(out=XC[:, i, :T], in_=Pq[:, :T], func=AF.Silu, bias=P.cb[:, c:c + 1], scale=1.0),
                      reads=[Pqb, P.cb_], writes=[XCb])
            kb.op("dve", lambda e: e.tensor_copy(out=P.HAL[:, g * 4:g * 4 + 4, :], in_=XB[:, 0:4, T:T + 4]), reads=[XBb], writes=[P.halb])
            kb.op("dve", lambda e: e.tensor_copy(out=P.HAL[:, 16 + g, :], in_=XB[:, 4, T:T + 4]), reads=[XBb], writes=[P.halb])
            kb.op("dve", lambda e: e.tensor_copy(out=P.HAL[:, 20 + g, :], in_=XB[:, 5, T:T + 4]), reads=[XBb], writes=[P.halb])
            if g + 1 < NG:
                stageA(g + 1)
            if full:
                s, wb = ring.get()
                Wz = W8[s][:, 0:8 * 512].rearrange("p (k n) -> p k n", k=8)
                for tb in range(NB):
                    Pq, Pqb = nextpa()
                    for k in range(KD):
                        kb.op("pe", lambda e, k=k, tb=tb: e.matmul(Pq[:, :], uT[:, k, tb * 128:(tb + 1) * 128], Wz[:, k, :],
                                                                    start=(k == 0), stop=(k == KD - 1)),
                              reads=[wb, ub[k]], writes=[Pqb], inc=(k == KD - 1))
                    kb.op("act", lambda e, tb=tb: e.activation(out=SZ[:, tb, :], in_=Pq[:, :], func=AF.Silu),
                          reads=[Pqb], writes=[SZb])
                ring.prefetch()
                for i in range(4):
                    kb.op("act", lambda e, i=i: e.activation(out=xD[:, i, :T], in_=XC[:, i, :T], func=AF.Copy,
                                                              scale=P.dch[:, g * 4 + i:g * 4 + i + 1]),
                          reads=[XCb, P.cb_], writes=[xDb])
            while norm_pend:
                norm_pend.pop(0)()
            for tb in range(NB):
                sl = slice(tb * 128, (tb + 1) * 128)
                kb.op("pe", lambda e, sl=sl, tb=tb: e.transpose(PX[:, 512 + tb * 128:640 + tb * 128], XC[:, 4, sl], P.ident[:]),
                      reads=[XCb, P.kb_], writes=[PXb], inc=(tb == NB - 1))
            kb.op("act", lambda e: e.activation(out=Btok[:, 0:NB, :], in_=PX[:, 512:512 + NB * 128].rearrange("p (t n) -> p t n", t=NB), func=AF.Copy),
                  reads=[PXb], writes=[Btb])
            for tb in range(NB):
                sl = slice(tb * 128, (tb + 1) * 128)
                for i in range(4):
                    kb.op("pe", lambda e, i=i, sl=sl: e.transpose(PX[:, i * 128:(i + 1) * 128], XC[:, i, sl], P.ident[:]),
                          reads=[XCb, P.kb_], writes=[PXb], inc=(i == 3))
                hs = slice(g * 8, (g + 1) * 8)
                kb.op("dve", lambda e, tb=tb: e.tensor_tensor(out=xw[:, tb, :].rearrange("p (h d) -> p h d", h=8),
                                                               in0=PX[:, 0:512].rearrange("p (h d) -> p h d", h=8),
                                                               in1=TK[:, tb, 3, hs].unsqueeze(2).to_broadcast([128, 8, 64]), op=ALU.mult),
                      reads=[PXb, TKb[tb]], writes=[xwb])
                if full:
                    kb.op("dve", lambda e, tb=tb: e.tensor_tensor(out=xdt[:, tb, :].rearrange("p (h d) -> p h d", h=8),
                                                                   in0=PX[:, 0:512].rearrange("p (h d) -> p h d", h=8),
                                                                   in1=TK[:, tb, 0, hs].unsqueeze(2).to_broadcast([128, 8, 64]), op=ALU.mult),
                          reads=[PXb, TKb[tb]], writes=[xdtb])
            gs = slice(g * 512, (g + 1) * 512)
            hs = slice(g * 8, (g + 1) * 8)

            def state_update(tb):
                PS_, PSb = nextpa()
                kb.op("pe", lambda e: e.matmul(PS_[:, :], Btok[:, tb, :], xw[:, tb, :], start=True, stop=True),
                      reads=[Btb, xwb], writes=[PSb])
                kb.op("pool", lambda e: e.tensor_tensor(out=t2[:].rearrange("p (h d) -> p h d", h=8),
                                                        in0=P.S[:, gs].rearrange("p (h d) -> p h d", h=8),
                                                        in1=TK[:, tb, 4, hs].unsqueeze(2).to_broadcast([128, 8, 64]), op=ALU.mult),
                      reads=[P.Sbuf[g], TKb[tb]], writes=[t2b])
                if full:
                    kb.op("dve", lambda e: e.tensor_tensor(out=P.Sb[:, gs], in0=PS_[:, :], in1=t2[:], op=ALU.add),
                          reads=[PSb, t2b], writes=[P.Sbb[g]])
                kb.op("dve", lambda e: e.tensor_tensor(out=P.S[:, gs], in0=PS_[:, :], in1=t2[:], op=ALU.add),
                      reads=[PSb, t2b], writes=[P.Sbuf[g]])

            if not full:
                for tb in range(NB):
                    state_update(tb)
            else:
                def head_a(tb):
                    sl = slice(tb * 128, (tb + 1) * 128)
                    p = tb % 2
                    ai = abci[0] % 2
                    abci[0] += 1
                    src = P.acd[g * 8:(g + 1) * 8, sl]
                    kb.dma("sp", ABC[ai][:], src.partition_broadcast(128), P.abcs[ai], reads=[P.acdb], writes=[ABCb[ai]])
                    kb.op("pe", lambda e: e.matmul(PT[:, 0:128], XC[:, 4, sl], XC[:, 5, sl], start=True, stop=True),
                          reads=[XCb], writes=[PTb])
                    kb.op("dve", lambda e: e.tensor_tensor(out=CBm[p][:], in0=PT[:, 0:128], in1=P.tri[:], op=ALU.mult),
                          reads=[PTb, P.kb_], writes=[CBmb[p]])
                    for hb4 in range(2):
                        bi = p * 2 + hb4
                        h0 = g * 8 + hb4 * 4
                        kb.op("pool", lambda e, hb4=hb4, h0=h0, bi=bi: e.tensor_tensor(
                            out=dar[bi][:], in0=ABC[ai][:, hb4 * 4:hb4 * 4 + 4, :],
                            in1=TK[:, tb, 1, h0:h0 + 4].unsqueeze(2).to_broadcast([128, 4, 128]), op=ALU.add),
                              reads=[ABCb[ai], TKb[tb]], writes=[darb[bi]])
                        kb.op("pool", lambda e, bi=bi: e.tensor_tensor(out=dar[bi][:], in0=dar[bi][:],
                                                                        in1=P.tri[:].unsqueeze(1).to_broadcast([128, 4, 128]), op=ALU.mult),
                              reads=[darb[bi], P.kb_], writes=[darb[bi]])
                        kb.op("act", lambda e, bi=bi: e.activation(out=Ee[bi][:], in_=dar[bi][:], func=AF.Exp),
                              reads=[darb[bi]], writes=[Eeb[bi]])

                def head_b(tb):
                    p = tb % 2
                    for hb4 in range(2):
                        bi = p * 2 + hb4
                        kb.op("dve", lambda e, bi=bi: e.tensor_tensor(
                            out=scT[bi][:], in0=Ee[bi][:], in1=CBm[p][:].unsqueeze(1).to_broadcast([128, 4, 128]), op=ALU.mult),
                              reads=[Eeb[bi], CBmb[p]], writes=[scb[bi]])

                def body_pre(tb):
                    sl = slice(tb * 128, (tb + 1) * 128)
                    p = tb % 2
                    kb.op("pe", lambda e: e.matmul(PYo[p][:, :], XC[:, 5, sl], P.Sb[:, gs], start=True, stop=True),
                          reads=[XCb, P.Sbb[g]], writes=[PYob[p]])
                    t1, t1b = t1s[p], t1bs[p]
                    kb.op("dve", lambda e: e.tensor_tensor(out=t1[:].rearrange("p (h d) -> p h d", h=8),
                                                           in0=PYo[p][:, :].rearrange("p (h d) -> p h d", h=8),
                                                           in1=TK[:, tb, 2, hs].unsqueeze(2).to_broadcast([128, 8, 64]), op=ALU.mult),
                          reads=[PYob[p], TKb[tb]], writes=[t1b])
                    state_update(tb)
                    while pend:
                        pend.pop(0)()

                def body_mm(tb):
                    sl = slice(tb * 128, (tb + 1) * 128)
                    p = tb % 2
                    t1, t1b = t1s[p], t1bs[p]
                    for hb4 in range(2):
                        bi = p * 2 + hb4
                        for pp in range(2):
                            pr = hb4 * 2 + pp
                            kb.op("pe", lambda e, pr=pr: e.matmul(PYd[p][:, pr * 128:(pr + 1) * 128], xD[:, pr, sl], P.ident[:],
                                                                   start=True, stop=False),
                                  reads=[xDb, P.kb_], writes=[PYdb[p]], inc=False)
                            for hh in range(2):
                                hl = pr * 2 + hh
                                j4 = pp * 2 + hh
                                kb.op("pe", lambda e, hl=hl, j4=j4, bi=bi, hh=hh: e.matmul(
                                    PYd[p][:, hl * 64:(hl + 1) * 64], scT[bi][:, j4, :], xdt[:, tb, hl * 64:(hl + 1) * 64], start=False, stop=(hh == 1)),
                                      reads=[scb[bi], xdtb], writes=[PYdb[p]], inc=(hh == 1))
                    while pend2:
                        pend2.pop(0)()

                    def tail1():
                        kb.op("dve", lambda e: e.tensor_tensor(out=t1[:], in0=PYd[p][:, :], in1=t1[:], op=ALU.add),
                              reads=[PYdb[p], t1b], writes=[t1b])

                    def tail2():
                        kb.op("pool", lambda e: e.tensor_tensor(out=yg[:, tb, :], in0=t1[:], in1=SZ[:, tb, :], op=ALU.mult),
                              reads=[t1b, SZb], writes=[ygb[tb]])
                    pend.append(tail1)
                    pend2.append(tail2)

                head_a(0)
                head_b(0)
                for tb in range(NB):
                    if tb + 1 < NB:
                        head_a(tb + 1)
                    body_pre(tb)
                    if tb + 1 < NB:
                        head_b(tb + 1)
                    body_mm(tb)
                while pend:
                    pend.pop(0)()
                while pend2:
                    pend2.pop(0)()
            if full:
                for tb in range(NB):
                    kb.op("act", lambda e, tb=tb: e.activation(out=ynb[:, tb, :], in_=yg[:, tb, :], func=AF.Square, accum_out=ss[:, tb:tb + 1]),
                          reads=[ygb[tb]], writes=[ynbb, ssb])
                kb.op("act", lambda e: e.activation(out=ss[:, 0:NB], in_=ss[:, 0:NB], func=AF.Sqrt, bias=R.epsb[:], scale=1.0 / 512),
                      reads=[ssb, R.epsbuf], writes=[ssb])
                kb.op("dve", lambda e: e.reciprocal(ss[:, 0:NB], ss[:, 0:NB]), reads=[ssb], writes=[ssb])
                for tb in range(NB):
                    kb.op("dve", lambda e, tb=tb: e.tensor_scalar(out=ynb[:, tb, :], in0=yg[:, tb, :], scalar1=ss[:, tb:tb + 1], scalar2=None, op0=ALU.mult),
                          reads=[ygb[tb], ssb], writes=[ynbb])
                def norm_tr(g=g):
                    for i in range(4):
                        for tb in range(NB):
                            kb.op("pe", lambda e, i=i, tb=tb: e.transpose(PX[:, tb * 128:(tb + 1) * 128], ynb[:, tb, i * 128:(i + 1) * 128], P.ident[:]),
                                  reads=[ynbb, P.kb_], writes=[PXb], inc=(tb == NB - 1))
                        c = g * 4 + i
                        kb.op("dve", lambda e, c=c: e.tensor_scalar(out=ynT[:, c, :T], in0=PX[:, 0:T], scalar1=P.gn[:, c:c + 1], scalar2=None, op0=ALU.mult),
                              reads=[PXb, P.cb_], writes=[ynTb])
                norm_pend.append(norm_tr)
        while norm_pend:
            norm_pend.pop(0)()
        if full:
            for mb in range(4):
                s, wb = ring.get()
                Wo = W8[s][:, 0:16 * 256].rearrange("p (c n) -> p c n", c=16)
                for mm in range(2):
                    m = mb * 2 + mm
                    Pq, Pqb = nextpa()
                    for c in range(16):
                        kb.op("pe", lambda e, c=c, mm=mm: e.matmul(Pq[:, :T], Wo[:, c, mm * 128:(mm + 1) * 128], ynT[:, c, :T],
                                                                    start=(c == 0), stop=(c == 15)),
                              reads=[wb, ynTb], writes=[Pqb], inc=(c == 15))
                    kb.op("dve", lambda e, m=m: e.tensor_tensor(out=hT[:, m, :T], in0=Pq[:, :T], in1=hT[:, m, :T], op=ALU.add),
                          reads=[Pqb, hb], writes=[hb])
                ring.prefetch()
        kb.barrier()


T = 512
NTOK = 4096
NT = NTOK // T


def dram_in(nc, name, shape, dt=F32):
    return nc.dram_tensor(name, list(shape), dt, kind="ExternalInput").ap()


def dram_out(nc, name, shape, dt=F32):
    return nc.dram_tensor(name, list(shape), dt, kind="ExternalOutput").ap()


def load_vec(kb, name, ap, shape, buf, ds, dt=F32):
    t = kb.sb(name, shape, dt)
    kb.dma("sp", t[:], ap, ds, writes=[buf], part=True)
    return t


def mamba_inputs(nc, full):
    dr = {"ssm_g": dram_in(nc, "ssm_g", [128, KD]), "cw": dram_in(nc, "cw", [128, 24, 4]), "cb": dram_in(nc, "cb", [128, 24]),
          "dtb": dram_in(nc, "dtb", [32, 1]), "alog": dram_in(nc, "alog", [32, 1]), "isec": dram_in(nc, "isec", [128, 1]),
          "w_in": dram_in(nc, "w_in", [D, 5152])}
    if full:
        dr["dch"] = dram_in(nc, "dch", [128, 16])
        dr["gn"] = dram_in(nc, "gn", [128, 16])
        dr["w_out"] = dram_in(nc, "w_out", [2048, D])
        dr["acum_scr"] = nc.dram_tensor("acum_scr", [32, 512], F32, kind="Internal").ap()
    return dr


def kv_items(W8, w_kv, ws):
    wv = w_kv.rearrange("(k p) n -> p k n", p=128)
    parts = []
    for kvh in range(2):
        for dup in range(2):
            parts.append((kvh * 128 + dup * 64, 64, wv[:, :, kvh * 64:(kvh + 1) * 64]))
    parts.append((256, 128, wv[:, :, 128:256]))
    return [ws.item(W8, "kv", parts, 8 * 384, 8)]


def kv_tile(kb, R, KVc, hT, hb, ring, W8, KT_out, V_out, t, osem, dbuf=None, after_norm=None):
    with contextlib.ExitStack() as st:
        sb = lambda n, s, d: st.enter_context(kb.nc.sbuf_tensor(uniq(n), list(s), d))
        ps = lambda n, s, d: st.enter_context(kb.nc.psum_tensor(uniq(n), list(s), d))
        PA = [ps(f"kpa{i}", [128, 512], F32) for i in range(2)]
        PAb = [Buf(f"kpa{i}") for i in range(2)]
        PN = ps("kpn", [128, 512], F32); PNb = Buf("kpn")
        rmsnorm_T(kb, R, hT, hb, KVc["g"], KVc["buf"], PN, PNb, T)
        if after_norm is not None:
            after_norm()
        s, wb = ring.get()
        W = W8[s][:, 0:8 * 384].rearrange("p (k n) -> p k n", k=8)
        KT = sb("k_KT", [128, 2, T], BF16); KTb = Buf("KT")
        sq = sb("k_sq", [128, T], BF16); sqb = Buf("ksq")
        rs = sb("k_rs", [128, T], F32); rsb = Buf("krs")
        Vt = sb("k_V", [128, 4, 128], BF16); Vb = Buf("kV")
        for kvh in range(2):
            Pq, Pqb = PA[kvh], PAb[kvh]
            for k in range(KD):
                kb.op("pe", lambda e, k=k: e.matmul(Pq[:, :T], W[:, k, kvh * 128:(kvh + 1) * 128], R.uT[:, k, :T],
                                                     start=(k == 0), stop=(k == KD - 1)),
                      reads=[wb, R.ub[k]], writes=[Pqb], inc=(k == KD - 1))
            kb.op("act", lambda e: e.activation(out=sq[:, :T], in_=Pq[:, :T], func=AF.Square), reads=[Pqb], writes=[sqb])
            kb.op("pe", lambda e: e.matmul(PN[:, :T], KVc["BD"][:], sq[:, :T], start=True, stop=True),
                  reads=[sqb, KVc["buf"]], writes=[PNb])
            kb.op("act", lambda e: e.activation(out=rs[:, :T], in_=PN[:, :T], func=AF.Sqrt, bias=R.epsb[:], scale=1.0 / 64),
                  reads=[PNb, R.epsbuf], writes=[rsb])
            kb.op("dve", lambda e: e.reciprocal(rs[:, :T], rs[:, :T]), reads=[rsb], writes=[rsb])
            kb.op("dve", lambda e: e.scalar_tensor_tensor(out=KT[:, kvh, :T], in0=Pq[:, :T], scalar=KVc["kn2"][:, 0:1], in1=rs[:, :T],
                                                           op0=ALU.mult, op1=ALU.mult),
                  reads=[Pqb, rsb, KVc["buf"]], writes=[KTb])
        for tb in range(4):
            Pq, Pqb = PA[tb % 2], PAb[tb % 2]
            for k in range(KD):
                kb.op("pe", lambda e, k=k: e.matmul(Pq[:, 0:128], R.uT[:, k, tb * 128:(tb + 1) * 128], W[:, k, 256:384],
                                                     start=(k == 0), stop=(k == KD - 1)),
                      reads=[wb, R.ub[k]], writes=[Pqb], inc=(k == KD - 1))
            kb.op("act", lambda e: e.activation(out=Vt[:, tb, :], in_=Pq[:, 0:128], func=AF.Copy), reads=[Pqb], writes=[Vb])
        ring.prefetch()
        wr = [dbuf] if dbuf is not None else []
        if dbuf is None:
            for kvh in range(2):
                kb.dma("sp", KT_out[kvh, :, t * T:(t + 1) * T], KT[:, kvh, :], osem, reads=[KTb])
            kb.dma("sp", V_out[t * T:(t + 1) * T, :].rearrange("(b p) c -> p b c", p=128), Vt[:], osem, reads=[Vb])
        elif t < 0:
            for kvh in range(2):
                kb.dma("sp", KT_out[kvh, :, 0:128], KT[:, kvh, T - 128:T], osem, reads=[KTb], writes=wr, part=True)
            kb.dma("sp", V_out[0:128, :], Vt[:, 3, :], osem, reads=[Vb], writes=wr, part=True)
        else:
            o = 128 + t * T
            for kvh in range(2):
                kb.dma("sp", KT_out[kvh, :, o:o + T], KT[:, kvh, :], osem, reads=[KTb], writes=wr, part=True)
            kb.dma("sp", V_out[o:o + T, :].rearrange("(b p) c -> p b c", p=128), Vt[:], osem, reads=[Vb], writes=wr, part=True)
        kb.barrier()


def make_BD(kb, name, buf):
    t = kb.sb(name, [128, 128], BF16)
    kb.op("pool", lambda e: e.memset(t[:], 0.0), writes=[buf])
    kb.op("pool", lambda e: e.memset(t[0:64, 0:64], 1.0), reads=[buf], writes=[buf])
    kb.op("pool", lambda e: e.memset(t[64:128, 64:128], 1.0), reads=[buf], writes=[buf])
    return t


CSH = 4.0


def attn_items(W8, w_q, w_o, ws):
    wq = w_q.rearrange("(k p) n -> p k n", p=128)
    wo = w_o.rearrange("(k p) n -> p k n", p=128)
    a = [ws.item(W8, f"q{h}", [(0, 512, wq[:, :, h * 512:(h + 1) * 512])], 8 * 512, 8, defer=True) for h in range(2)]
    b = [ws.item(W8, f"wo{h}", [(0, 512, wo[:, :, h * 512:(h + 1) * 512])], 8 * 512, 8, defer=True) for h in range(2)]
    return a, b


def attn_tile(kb, R, AC, hT, hb, ring, W8, t):
    with contextlib.ExitStack() as st:
        sb = lambda n, s, d: st.enter_context(kb.nc.sbuf_tensor(uniq(n), list(s), d))
        ps = lambda n, s, d: st.enter_context(kb.nc.psum_tensor(uniq(n), list(s), d))
        PSs = [[ps(f"aps{q}{i}", [128, 512], F32) for i in range(2)] for q in range(2)]
        PSsb = [[Buf(f"aps{q}{i}") for i in range(2)] for q in range(2)]
        PO = [ps(f"apo{i}", [128, 512], F32) for i in range(2)]; POb = [Buf(f"apo{i}") for i in range(2)]
        PD = ps("apd", [128, 512], F32); PDb = Buf("apd")
        PX = ps("apx", [128, 1024], BF16); PXb = Buf("apx")
        cb = AC["buf"]
        rmsnorm_T(kb, R, hT, hb, AC["g"], cb, PD, PDb, T)
        QT = sb("a_QT", [128, 8, T], BF16); QTb = Buf("QT")
        sq = sb("a_sq", [128, T], BF16); sqb = Buf("asq")
        rs = sb("a_rs", [128, T], F32); rsb = Buf("ars")
        for half in range(2):
            s, wb = ring.get()
            W = W8[s][:, 0:8 * 512].rearrange("p (k n) -> p k n", k=8)
            for i in range(4):
                qc = half * 4 + i
                Pq, Pqb = PSs[0][i % 2], PSsb[0][i % 2]
                for k in range(KD):
                    kb.op("pe", lambda e, k=k, i=i: e.matmul(Pq[:, :T], W[:, k, i * 128:(i + 1) * 128], R.uT[:, k, :T],
                                                              start=(k == 0), stop=(k == KD - 1)),
                          reads=[wb, R.ub[k]], writes=[Pqb], inc=(k == KD - 1))
                kb.op("act", lambda e: e.activation(out=sq[:, :T], in_=Pq[:, :T], func=AF.Square), reads=[Pqb], writes=[sqb])
                kb.op("pe", lambda e: e.matmul(PD[:, :T], AC["BD"][:], sq[:, :T], start=True, stop=True),
                      reads=[sqb, cb], writes=[PDb])
                kb.op("act", lambda e: e.activation(out=rs[:, :T], in_=PD[:, :T], func=AF.Sqrt, bias=R.epsb[:], scale=1.0 / 64),
                      reads=[PDb, R.epsbuf], writes=[rsb])
                kb.op("dve", lambda e: e.reciprocal(rs[:, :T], rs[:, :T]), reads=[rsb], writes=[rsb])
                kb.op("dve", lambda e, qc=qc: e.scalar_tensor_tensor(out=QT[:, qc, :T], in0=Pq[:, :T], scalar=AC["qn2"][:, 0:1], in1=rs[:, :T],
                                                                      op0=ALU.mult, op1=ALU.mult),
                      reads=[Pqb, rsb, cb], writes=[QTb])
            ring.prefetch()
        OT = sb("a_OT", [128, 8, T], BF16); OTb = Buf("OT")
        PTs = [[sb(f"a_PT{q}{i}", [128, 512], BF16) for i in range(2)] for q in range(2)]
        PTsb = [[Buf(f"aPT{q}{i}") for i in range(2)] for q in range(2)]
        Ee = [[sb(f"a_Ee{q}{i}", [128, 512], BF16) for i in range(2)] for q in range(2)]
        Eeb = [[Buf(f"aEe{q}{i}") for i in range(2)] for q in range(2)]
        On = sb("a_On", [128, 1024], BF16); Onb = Buf("On")
        den = sb("a_den", [128, 16], F32); denb = Buf("den")
        KTs, Vs = AC["KT"], AC["V"]
        quads = [(qb, kvh, par) for qb in range(4) for kvh in range(2) for par in range(2)]

        def front(qi):
            qb, kvh, par = quads[qi]
            pq = qi % 2
            gb = t * 4 + qb
            EBx = AC["EB0"] if gb == 0 else AC["EB"]
            qs = slice(qb * 128, (qb + 1) * 128)
            rows = slice(par * 64, par * 64 + 64)
            for blk in range(2):
                kc = gb * 128 + blk * 128
                kb.op("pe", lambda e, blk=blk, kc=kc: e.matmul(PSs[pq][blk][:, :], KTs[rows, kvh, kc:kc + 128],
                                                                QT[rows, kvh * 4:(kvh + 1) * 4, qs], start=True, stop=True),
                      reads=[AC["kvbuf"], QTb], writes=[PSsb[pq][blk]])
                kb.op("act", lambda e, blk=blk: e.activation(out=Ee[pq][blk][:], in_=PSs[pq][blk][:, :], func=AF.Exp, scale=0.125),
                      reads=[PSsb[pq][blk]], writes=[Eeb[pq][blk]])
                idx = (blk * 2 + kvh) * 2 + par
                kb.op("dve", lambda e, blk=blk, idx=idx: e.tensor_tensor(
                    out=PTs[pq][blk][:], in0=Ee[pq][blk][:],
                    in1=EBx[:, idx * 4:(idx + 1) * 4, :].rearrange("p j q -> p (j q)"), op=ALU.mult),
                      reads=[Eeb[pq][blk], cb], writes=[PTsb[pq][blk]])

        def back(qi):
            qb, kvh, par = quads[qi]
            pq = qi % 2
            gb = t * 4 + qb
            for jj in range(4):
                h = kvh * 8 + 2 * jj + par
                bank, hc = h // 8, (h % 8) * 64
                for blk in range(2):
                    kb.op("pe", lambda e, blk=blk: e.matmul(PO[bank][:, hc:hc + 64], PTs[pq][blk][:, jj * 128:(jj + 1) * 128],
                                                             Vs[:, gb + blk, kvh * 64:(kvh + 1) * 64], start=(blk == 0), stop=(blk == 1)),
                          reads=[PTsb[pq][blk], AC["kvbuf"]], writes=[POb[bank]], inc=(blk == 1))
                for blk in range(2):
                    kb.op("pe", lambda e, blk=blk: e.matmul(PD[:, h:h + 1], PTs[pq][blk][:, jj * 128:(jj + 1) * 128],
                                                             AC["onec"][:, 0:1], start=(blk == 0), stop=(blk == 1)),
                          reads=[PTsb[pq][blk], cb], writes=[PDb], inc=(blk == 1))

        front(0)
        for qi in range(len(quads)):
            qb = quads[qi][0]
            qs = slice(qb * 128, (qb + 1) * 128)
            if qi + 1 < len(quads):
                front(qi + 1)
            back(qi)
            if qi % 4 != 3:
                continue
            kb.op("dve", lambda e: e.tensor_tensor(out=den[:], in0=PD[:, 0:16], in1=AC["esink"][:], op=ALU.add),
                  reads=[PDb, cb], writes=[denb])
            kb.op("dve", lambda e: e.reciprocal(den[:], den[:]), reads=[denb], writes=[denb])
            for bank in range(2):
                kb.op("dve", lambda e, bank=bank: e.tensor_tensor(
                    out=On[:, bank * 512:(bank + 1) * 512].rearrange("p (h d) -> p h d", h=8),
                    in0=PO[bank][:, :].rearrange("p (h d) -> p h d", h=8),
                    in1=den[:, bank * 8:(bank + 1) * 8].unsqueeze(2).to_broadcast([128, 8, 64]), op=ALU.mult),
                      reads=[POb[bank], denb], writes=[Onb])
            for c in range(8):
                kb.op("pe", lambda e, c=c: e.transpose(PX[:, c * 128:(c + 1) * 128], On[:, c * 128:(c + 1) * 128], AC["ident"][:]),
                      reads=[Onb, cb], writes=[PXb], inc=(c == 7))
            kb.op("act", lambda e: e.activation(out=OT[:, :, qs], in_=PX[:, :].rearrange("p (c q) -> p c q", c=8), func=AF.Copy),
                  reads=[PXb], writes=[OTb])
        for half in range(2):
            s, wb = ring.get()
            W = W8[s][:, 0:8 * 512].rearrange("p (k n) -> p k n", k=8)
            for i in range(4):
                m = half * 4 + i
                Pq, Pqb = PSs[0][i % 2], PSsb[0][i % 2]
                for c in range(8):
                    kb.op("pe", lambda e, c=c, i=i: e.matmul(Pq[:, :T], W[:, c, i * 128:(i + 1) * 128], OT[:, c, :T],
                                                              start=(c == 0), stop=(c == 7)),
                          reads=[wb, OTb], writes=[Pqb], inc=(c == 7))
                kb.op("dve", lambda e, m=m: e.tensor_tensor(out=hT[:, m, :T], in0=Pq[:, :T], in1=hT[:, m, :T], op=ALU.add),
                      reads=[Pqb, hb], writes=[hb])
            ring.prefetch()
        kb.barrier()


def build_fused(ntiles=NT):
    nc = bass.Bass("TRN2", target_bir_lowering=False)
    ntok = ntiles * T
    xTp = dram_in(nc, "xTp", [D, ntok])
    xT = dram_in(nc, "xT", [D, ntok])
    gF = [dram_in(nc, f"gF{i}", [128, KD]) for i in range(4)]
    wF = [[dram_in(nc, f"wF{i}_{j}", s) for j, s in enumerate([[D, DFF], [D, DFF], [DFF, D]])] for i in range(4)]
    dr = mamba_inputs(nc, True)
    kvg = dram_in(nc, "kvg", [128, KD]); kn2 = dram_in(nc, "kn2", [128, 1]); w_kv = dram_in(nc, "w_kv", [D, 256])
    gat = dram_in(nc, "gat", [128, KD])
    w_q = dram_in(nc, "w_q", [D, D]); w_o = dram_in(nc, "w_o", [D, D])
    qn2 = dram_in(nc, "qn2", [128, 1]); sinkrow = dram_in(nc, "sinkrow", [128, 16])
    biasT = dram_in(nc, "biasT", [128, 32, 128]); biasT0 = dram_in(nc, "biasT0", [128, 32, 128])
    outT = dram_out(nc, "outT", [D, ntok])
    h3d = nc.dram_tensor("h3_scr", [D, ntok], F32, kind="Internal").ap()
    KTd = nc.dram_tensor("KT_scr", [2, 128, 128 + ntok], BF16, kind="Internal").ap()
    Vd = nc.dram_tensor("V_scr", [128 + ntok, 128], BF16, kind="Internal").ap()
    h3b = Buf("h3d"); kvdb = Buf("kvd")
    nblk = ntok // 128 + 1
    with contextlib.ExitStack() as st:
        kb = KB(nc, st)
        R = FFNRes(kb, T)
        cbuf = Buf("c"); dsc = kb.dsem("ds_c")
        gv = [load_vec(kb, f"gv{i}", gF[i][:, :], [128, KD], cbuf, dsc) for i in range(4)]
        KVc = {"buf": cbuf}
        KVc["g"] = load_vec(kb, "kvg_sb", kvg[:, :], [128, KD], cbuf, dsc)
        KVc["kn2"] = load_vec(kb, "kn2_sb", kn2[:, :], [128, 1], cbuf, dsc)
        AC = {"buf": cbuf}
        AC["g"] = load_vec(kb, "gvat", gat[:, :], [128, KD], cbuf, dsc)
        AC["qn2"] = load_vec(kb, "qn2_sb", qn2[:, :], [128, 1], cbuf, dsc)
        AC["esink"] = load_vec(kb, "esink", sinkrow[:, :], [128, 16], cbuf, dsc)
        BD = make_BD(kb, "BD", cbuf)
        KVc["BD"] = BD; AC["BD"] = BD
        H = kb.sb("h", [128, KD, T], F32); Hb = Buf("h"); Hs = kb.dsem("ds_h")
        osem = kb.dsem("ds_o")
        FS = [FFNSet(kb, nc, f"f{i}", *wF[i]) for i in range(4)]
        ws = WScr(kb, nc)
        W8 = [kb.sb(f"w8_{i}", [128, 4096], BF16) for i in range(2)]
        FS[0].convert(kb)
        items = []
        for t in range(ntiles - 1):
            items += mamba_items_tile(W8, dr["w_in"], None, "state", ws)
        items += mamba_items_tile(W8, dr["w_in"], dr["w_out"], "full", ws) + kv_items(W8, w_kv, ws)
        for t in range(ntiles):
            items += mamba_items_tile(W8, dr["w_in"], dr["w_out"], "full", ws) + kv_items(W8, w_kv, ws)
        for t in range(ntiles):
            a, b = attn_items(W8, w_q, w_o, ws)
            items += a + b
        tw = [FS[0]] * ntiles + [FS[1]]
        for t in range(ntiles):
            tw += [FS[0], FS[1]]
        for t in range(ntiles):
            tw += [FS[2], FS[3]]
        r13, r2 = ffn_rings_bf16(kb, "a", tw)
        ring = Ring(kb, "w8", 2, items)
        xpv = xTp.rearrange("(k p) t -> p k t", p=128)
        xv = xT.rearrange("(k p) t -> p k t", p=128)
        h3v = h3d.rearrange("(k p) t -> p k t", p=128)
        yv = outT.rearrange("(k p) t -> p k t", p=128)
        with contextlib.ExitStack() as stM:
            kb.stack = stM
            P = MambaP(kb, dr, True)
            for g in range(NG):
                kb.op("pool", lambda e, g=g: e.memset(P.S[:, g * 512:(g + 1) * 512], 0.0), writes=[P.Sbuf[g]])
                kb.op("pool", lambda e, g=g: e.memset(P.Sb[:, g * 512:(g + 1) * 512], 0.0), writes=[P.Sbb[g]])
            def load_prev(t):
                kb.dma("sp", H[:], xpv[:, :, t * T:(t + 1) * T], Hs, writes=[Hb])

            def load_own(t):
                kb.dma("sp", H[:], xv[:, :, t * T:(t + 1) * T], Hs, writes=[Hb])
            bg = FS[1].steps(kb) + list(ws.pending) + FS[2].steps(kb) + FS[3].steps(kb)
            ws.pending = []
            nbg = 6 if ntiles >= 8 else max(6, -(-len(bg) // max(ntiles - 2, 1)))

            def trickle(n=None):
                for _ in range(nbg if n is None else n):
                    if bg:
                        bg.pop(0)()
            load_prev(0)
            for t in range(ntiles):
                if t >= 1 or ntiles == 1:
                    trickle()
                ffn(kb, R, H, Hb, gv[0], cbuf, r13, r2, T)
                if t < ntiles - 1:
                    mamba_tile(kb, P, R, H, Hb, T, "state", ring, W8, first=(t == 0), after_norm=lambda t=t: load_prev(t + 1))
                else:
                    for g in range(NG):
                        kb.op("act", lambda e, g=g: e.activation(out=P.Sb[:, g * 512:(g + 1) * 512], in_=P.S[:, g * 512:(g + 1) * 512], func=AF.Copy),
                              reads=[P.Sbuf[g]], writes=[P.Sbb[g]])
                    mamba_tile(kb, P, R, H, Hb, T, "full", ring, W8, first=(t == 0))
                    assert len(bg) <= len(FS[2].steps(kb)) + len(FS[3].steps(kb)) + 8, "FS[1] must be converted before its first use"
                    ffn(kb, R, H, Hb, gv[1], cbuf, r13, r2, T)
                    kv_tile(kb, R, KVc, H, Hb, ring, W8, KTd, Vd, -1, osem, kvdb, after_norm=lambda: load_own(0))
            for g in range(NG):
                gs = slice(g * 512, (g + 1) * 512)
                kb.op("dve", lambda e, gs=gs: e.tensor_scalar(out=P.S[:, gs], in0=P.S[:, gs], scalar1=P.isec[:, 0:1], scalar2=None, op0=ALU.mult),
                      reads=[P.Sbuf[g], P.cb_], writes=[P.Sbuf[g]])
                kb.op("dve", lambda e, gs=gs: e.tensor_scalar(out=P.Sb[:, gs], in0=P.S[:, gs], scalar1=1.0, scalar2=None, op0=ALU.mult),
                      reads=[P.Sbuf[g]], writes=[P.Sbb[g]])
            kb.op("dve", lambda e: e.tensor_scalar(out=P.HAL[:], in0=P.HAL[:], scalar1=P.isec[:, 0:1], scalar2=None, op0=ALU.mult),
                  reads=[P.halb, P.cb_], writes=[P.halb])
            for t in range(ntiles):
                trickle(len(bg) if t == ntiles - 1 else None)
                ffn(kb, R, H, Hb, gv[0], cbuf, r13, r2, T)
                mamba_tile(kb, P, R, H, Hb, T, "full", ring, W8)
                ffn(kb, R, H, Hb, gv[1], cbuf, r13, r2, T)
                kb.dma("sp", h3v[:, :, t * T:(t + 1) * T], H[:], Hs, reads=[Hb], writes=[h3b], part=(t > 0))
                kv_tile(kb, R, KVc, H, Hb, ring, W8, KTd, Vd, t, osem, kvdb,
                        after_norm=(lambda t=t: load_own(t + 1)) if t + 1 < ntiles else None)
            kb.barrier()
            kb.stack = st
        AC["kvbuf"] = Buf("kv")
        AC["KT"] = kb.sb("KTs", [128, 2, 128 + ntok], BF16)
        AC["V"] = kb.sb("Vs", [128, nblk, 128], BF16)
        dkv = kb.dsem("ds_kv")
        for kvh in range(2):
            kb.dma("sp", AC["KT"][:, kvh, :], KTd[kvh, :, :], dkv, reads=[kvdb], writes=[AC["kvbuf"]], part=(kvh > 0))
        kb.dma("sp", AC["V"][:], Vd.rearrange("(b p) c -> p b c", p=128), dkv, reads=[kvdb], writes=[AC["kvbuf"]], part=True)
        AC["EB"] = kb.sb("EB", [128, 32, 128], BF16)
        AC["EB0"] = kb.sb("EB0", [128, 32, 128], BF16)
        negc = kb.sb("negc", [128, 1], F32)
        kb.op("pool", lambda e: e.memset(negc[:], -CSH), reads=[cbuf], writes=[cbuf])
        with contextlib.ExitStack() as st2:
            stg = st2.enter_context(nc.sbuf_tensor("bias_stage", [128, 32, 128], F32))
            stb = Buf("stage"); dst_ = kb.dsem("ds_stage")
            for src, dstt in ((biasT, AC["EB"]), (biasT0, AC["EB0"])):
                kb.dma("sp", stg[:], src[:, :, :], dst_, reads=[], writes=[stb])
                kb.op("act", lambda e, dstt=dstt: e.activation(out=dstt[:], in_=stg[:], func=AF.Exp, bias=negc[:], scale=1.0),
                      reads=[stb, cbuf], writes=[cbuf])
            kb.barrier()
        kb.op("act", lambda e: e.activation(out=AC["esink"][:], in_=AC["esink"][:], func=AF.Exp, bias=negc[:], scale=1.0),
              reads=[cbuf], writes=[cbuf])
        AC["onec"] = kb.sb("onec", [128, 1], BF16)
        kb.op("pool", lambda e: e.memset(AC["onec"][:], 1.0), reads=[cbuf], writes=[cbuf])
        identf = kb.sb("identf", [128, 128], F32)
        AC["ident"] = kb.sb("identb", [128, 128], BF16)
        kb.op("pool", lambda e: e.memset(identf[:], 1.0), reads=[cbuf], writes=[cbuf])
        kb.op("pool", lambda e: e.affine_select(out=identf[:], in_=identf[:], pattern=[[-1, 128]],
                                                 compare_op=ALU.is_equal, fill=0.0, base=0, channel_multiplier=1),
              reads=[cbuf], writes=[cbuf])
        kb.op("pool", lambda e: e.tensor_copy(out=AC["ident"][:], in_=identf[:]), reads=[cbuf], writes=[cbuf])
        H2 = kb.sb("h2", [128, KD, T], F32); H2b = Buf("h2"); Hs2 = kb.dsem("ds_h2")
        HH = [(H, Hb, Hs), (H2, H2b, Hs2)]

        def load_l1(t):
            h_, hb_, hs_ = HH[t % 2]
            kb.dma("sp", h_[:], h3v[:, :, t * T:(t + 1) * T], hs_, reads=[h3b], writes=[hb_])
        load_l1(0)
        for t in range(ntiles):
            h_, hb_, hs_ = HH[t % 2]
            if t + 1 < ntiles:
                load_l1(t + 1)
            ffn(kb, R, h_, hb_, gv[2], cbuf, r13, r2, T)
            attn_tile(kb, R, AC, h_, hb_, ring, W8, t)
            ffn(kb, R, h_, hb_, gv[3], cbuf, r13, r2, T)
            kb.dma("sp", yv[:, :, t * T:(t + 1) * T], h_[:], hs_, reads=[hb_])
        kb.barrier()
        print("F: waits", kb.nwait, kb.ecnt)
    return nc

import numpy as np
def chunkvec(v, nch):
    return np.ascontiguousarray(v.reshape(nch, 128).T).astype(np.float32)
def prep_common(I):
    c = {}
    c["ssm_g"] = chunkvec(I["ssm_norm"][0], 8)
    c["cw"] = np.ascontiguousarray(I["ssm_conv_w"][0].reshape(4, 24, 128).transpose(2, 1, 0)).astype(np.float32)
    c["cb"] = chunkvec(I["ssm_conv_b"][0], 24)
    c["dtb"] = I["ssm_dt_bias"][0].reshape(32, 1).astype(np.float32)
    c["alog"] = I["ssm_a_log"][0].reshape(32, 1).astype(np.float32)
    c["w_in"] = np.ascontiguousarray(I["ssm_w_in"][0])
    c["dch"] = chunkvec(np.repeat(I["ssm_d"][0], 64), 16)
    c["gn"] = chunkvec(I["ssm_gate_norm"][0], 16)
    c["w_out"] = np.ascontiguousarray(I["ssm_w_out"][0])
    return c
import math
def t5_bucket_np(dist):
    n = np.maximum(dist, 0); me = 16
    nf = np.maximum(n, 1).astype(np.float32)
    large = me + (np.log(nf / me) / math.log(128 / me) * (32 - me)).astype(np.int32)
    large = np.minimum(large, 31)
    return np.where(n < me, n, large)
def prep_bias(rel_bias, first_valid):
    qi = np.arange(128)[:, None] + 128; kj = np.arange(256)[None, :]
    dist = qi - kj
    valid = (dist >= 0) & (dist < 128)
    bias = rel_bias[t5_bucket_np(dist)]
    out = np.empty((128, 32, 128), np.float32)
    for blk in range(2):
        for kvh in range(2):
            for par in range(2):
                for jj in range(4):
                    h = kvh * 8 + 2 * jj + par
                    idx = ((blk * 2 + kvh) * 2 + par) * 4 + jj
                    b = bias[:, blk * 128:(blk + 1) * 128, h]
                    v = valid[:, blk * 128:(blk + 1) * 128]
                    if blk == 0 and not first_valid:
                        v = np.zeros_like(v)
                    out[:, idx, :] = np.where(v, b, np.float32(-30000.0)).T
    return out


_NC_CACHE = {}


def kernel(**I):
    I = {k: np.asarray(v) for k, v in I.items()}
    c = prep_common(I)
    x = I["x"].astype(np.float32)
    n = 8
    cores = list(range(n))
    f32c = lambda a: np.ascontiguousarray(a, dtype=np.float32)
    if "F" not in _NC_CACHE:
        _NC_CACHE["F"] = build_fused()
    rb = I["rel_bias"].astype(np.float32)
    bias1 = prep_bias(rb, True)
    bias0 = prep_bias(rb, False)
    shared = {"kvg": chunkvec(I["kv_norm"], 8), "kn2": np.tile(I["k_norm"], 2).reshape(128, 1).astype(np.float32),
              "w_kv": f32c(I["w_kv"]), "gat": chunkvec(I["attn_norm"][0], 8), "w_q": f32c(I["w_q"][0]), "w_o": f32c(I["w_o"][0]),
              "qn2": np.tile(I["q_norm"][0], 2).reshape(128, 1).astype(np.float32),
              "sinkrow": np.ascontiguousarray(np.broadcast_to(I["sinks"][0][None, :], (128, 16))).astype(np.float32),
              "biasT": bias1}
    for i, (l, w) in enumerate([(0, 0), (0, 1), (1, 0), (1, 1)]):
        shared[f"gF{i}"] = chunkvec(I["ffn_norm"][l, w], 8)
        shared[f"wF{i}_0"] = f32c(I["ffn_w1"][l, w])
        shared[f"wF{i}_1"] = f32c(I["ffn_w3"][l, w])
        shared[f"wF{i}_2"] = f32c(I["ffn_w2"][l, w])
    for k in ["ssm_g", "cw", "cb", "dtb", "alog", "w_in", "dch", "gn", "w_out"]:
        shared[k] = c[k]
    maps = []
    for core in cores:
        b, hf = core // 2, core % 2
        m = dict(shared)
        m["xTp"] = f32c(x[b, 0:NTOK].T) if hf else np.zeros((D, NTOK), np.float32)
        m["xT"] = f32c(x[b, hf * NTOK:(hf + 1) * NTOK].T)
        m["biasT0"] = bias1 if hf else bias0
        m["isec"] = np.full((128, 1), float(hf), np.float32)
        maps.append(m)
    res = run_bass_kernel_spmd(_NC_CACHE["F"], maps, core_ids=cores).results
    out = np.empty((4, 2 * NTOK, D), np.float32)
    for core in cores:
        b, hf = core // 2, core % 2
        out[b, hf * NTOK:(hf + 1) * NTOK] = np.asarray(res[core]["outT"]).T
    return out
```
